# Optimizing a Trainium2 kernel written in Bass

```python
import jax
import jax.numpy as jnp
from jax import lax
import numpy as np

D_MODEL = 1024
BATCH = 8
SEQ = 4096
DEPTH = 2

GRID_W = 64
CTX_LEN = 256
D_MIX = D_MODEL
CONV_WIDTH = D_MIX // 4
CONV_KERNEL = 31
ATTN_WIDTH = D_MIX // 2
ATTN_HEAD_DIM = 64
ATTN_HEADS = ATTN_WIDTH // ATTN_HEAD_DIM
ATTN_KV_HEADS = 2
ATTN_GROUP = ATTN_HEADS // ATTN_KV_HEADS
ATTN_WINDOW = 128
ATTN_BLOCK = 128
ROPE_THETA = 10000.0
GLA_WIDTH = D_MIX - CONV_WIDTH - ATTN_WIDTH
GLA_HEADS = 4
GLA_DV = GLA_WIDTH // GLA_HEADS
GLA_DK = GLA_DV // 2
GLA_LOW_RANK = 16
GLA_TAU = 16.0
GLA_CHUNK = 64
NORM_EPS = 1e-6
NEG_INF = -1e30
F32 = jnp.float32

IN_COLUMNS = (
    ('a_val', CONV_WIDTH), ('a_glu', CONV_WIDTH), ('a_gate', CONV_WIDTH),
    ('b_q', ATTN_WIDTH), ('b_k', ATTN_KV_HEADS * ATTN_HEAD_DIM), ('b_v', ATTN_KV_HEADS * ATTN_HEAD_DIM), ('b_gate', ATTN_WIDTH),
    ('c_q', GLA_HEADS * GLA_DK), ('c_k', GLA_HEADS * GLA_DK), ('c_v', GLA_WIDTH),
    ('c_lr_f', GLA_LOW_RANK), ('c_lr_b', GLA_LOW_RANK), ('c_gate', GLA_WIDTH),
)
N_IN = (3 * CONV_WIDTH + 2 * ATTN_WIDTH + 2 * ATTN_KV_HEADS * ATTN_HEAD_DIM
        + 2 * GLA_HEADS * GLA_DK + 2 * GLA_WIDTH + 2 * GLA_LOW_RANK)
CTX_KV_COLUMNS = ('b_k', 'b_v', 'c_k', 'c_v', 'c_lr_f', 'c_lr_b')

kernel_name = 'hybrid_conv_swa_gla_prefix_dit'


def _rms_norm(x):
    xf = x.astype(F32)
    return (xf * lax.rsqrt(jnp.mean(xf * xf, axis=-1, keepdims=True) + NORM_EPS)).astype(x.dtype)


def _column_slices():
    out, off = {}, 0
    for name, size in IN_COLUMNS:
        out[name] = (off, size)
        off += size
    return out


def _project(h, w_in, names=None):
    sl = _column_slices()
    if names is None:
        z = h @ w_in
        return {n: z[..., o:o + s] for n, (o, s) in sl.items()}
    return {n: h @ w_in[:, sl[n][0]:sl[n][0] + sl[n][1]] for n in names}


def _heads(t, n_heads):
    return t.reshape(t.shape[:2] + (n_heads, t.shape[-1] // n_heads))


def _rope_tables(n_tokens):
    rows = n_tokens // GRID_W
    r = jnp.repeat(jnp.arange(rows, dtype=F32), GRID_W)
    col = jnp.tile(jnp.arange(GRID_W, dtype=F32), rows)
    n_freq = ATTN_HEAD_DIM // 4
    inv = ROPE_THETA ** (-jnp.arange(n_freq, dtype=F32) / n_freq)
    ang = jnp.concatenate([r[:, None] * inv, col[:, None] * inv], axis=-1)
    ang = jnp.concatenate([ang, ang], axis=-1)
    return jnp.cos(ang), jnp.sin(ang)


def _apply_rope(x, cos, sin):
    xf = x.astype(F32)
    half = ATTN_HEAD_DIM // 2
    rot = jnp.concatenate([-xf[..., half:], xf[..., :half]], axis=-1)
    return (xf * cos[None, :, None, :] + rot * sin[None, :, None, :]).astype(x.dtype)


def _conv_branch(p, conv_w, conv_b, ln_w, ln_b):
    u = p['a_val'] * jax.nn.sigmoid(p['a_glu'])
    u = lax.conv_general_dilated(
        u, conv_w[:, None, :].astype(u.dtype), window_strides=(1,),
        padding=((CONV_KERNEL // 2, CONV_KERNEL // 2),),
        dimension_numbers=('NWC', 'WIO', 'NWC'), feature_group_count=CONV_WIDTH) + conv_b
    uf = u.astype(F32)
    mu = jnp.mean(uf, axis=-1, keepdims=True)
    var = jnp.mean(jnp.square(uf - mu), axis=-1, keepdims=True)
    u = ((uf - mu) * lax.rsqrt(var + NORM_EPS) * ln_w + ln_b).astype(p['a_val'].dtype)
    return jax.nn.silu(u) * jax.nn.silu(p['a_gate'])


def _windowed_attention(q, k, v, kc, vc, sink):
    B, S, H, Dh = q.shape
    nb = S // ATTN_BLOCK
    nc = kc.shape[1]
    qb = (q * Dh ** -0.5).reshape(B, nb, ATTN_BLOCK, ATTN_KV_HEADS, ATTN_GROUP, Dh).transpose(1, 0, 2, 3, 4, 5)

    def band(t):
        tb = t.reshape(B, nb, ATTN_BLOCK, ATTN_KV_HEADS, Dh)
        tp = jnp.pad(tb, ((0, 0), (1, 1), (0, 0), (0, 0), (0, 0)))
        return jnp.concatenate([tp[:, :-2], tp[:, 1:-1], tp[:, 2:]], axis=2).transpose(1, 0, 2, 3, 4)

    kw, vw = band(k), band(v)
    qi = jnp.arange(ATTN_BLOCK)
    kj = jnp.arange(3 * ATTN_BLOCK)
    rel = ATTN_BLOCK + qi[:, None] - kj[None, :]
    kpos = (jnp.arange(nb)[:, None] - 1) * ATTN_BLOCK + kj[None, :]
    mask = (jnp.abs(rel)[None] <= ATTN_WINDOW) & ((kpos >= 0) & (kpos < S))[:, None, :]
    sink_l = sink.astype(F32).reshape(ATTN_KV_HEADS, ATTN_GROUP)

    def one_block(args):
        qn, kn, vn, mn = args
        s_loc = jnp.einsum('bqhgd,bkhd->bhgqk', qn, kn).astype(F32)
        s_loc = jnp.where(mn, s_loc, NEG_INF)
        s_ctx = jnp.einsum('bqhgd,bchd->bhgqc', qn, kc).astype(F32)
        s_sink = jnp.broadcast_to(sink_l[None, :, :, None, None], s_ctx.shape[:-1] + (1,))
        p = jax.nn.softmax(jnp.concatenate([s_sink, s_ctx, s_loc], axis=-1), axis=-1)
        p_ctx = p[..., 1:1 + nc].astype(vn.dtype)
        p_loc = p[..., 1 + nc:].astype(vn.dtype)
        return (jnp.einsum('bhgqc,bchd->bqhgd', p_ctx, vc)
                + jnp.einsum('bhgqk,bkhd->bqhgd', p_loc, vn))

    o = lax.map(one_block, (qb, kw, vw, mask))
    return o.transpose(1, 0, 2, 3, 4, 5).reshape(B, S, H * Dh)


def _ctx_attention(qc, kc, vc, sink):
    B, Nc, H, Dh = qc.shape
    qg = (qc * Dh ** -0.5).reshape(B, Nc, ATTN_KV_HEADS, ATTN_GROUP, Dh)
    s = jnp.einsum('bqhgd,bkhd->bhgqk', qg, kc).astype(F32)
    s_sink = jnp.broadcast_to(sink.astype(F32).reshape(ATTN_KV_HEADS, ATTN_GROUP)[None, :, :, None, None],
                              s.shape[:-1] + (1,))
    p = jax.nn.softmax(jnp.concatenate([s_sink, s], axis=-1), axis=-1)[..., 1:].astype(vc.dtype)
    return jnp.einsum('bhgqk,bkhd->bqhgd', p, vc).reshape(B, Nc, H * Dh)


def _gla_decay(lr, w_up, b_up):
    z = lr.astype(F32) @ w_up.astype(F32) + b_up.astype(F32)
    return (jax.nn.log_sigmoid(z) / GLA_TAU).reshape(lr.shape[:2] + (GLA_HEADS, GLA_DK))


def _gla_chunked(q, k, v, log_a, s0):
    B, T, H, _ = q.shape
    dv = v.shape[-1]
    n = T // GLA_CHUNK

    def chunks(t):
        return t.astype(F32).reshape(B, n, GLA_CHUNK, H, t.shape[-1]).transpose(1, 0, 2, 3, 4)

    qc, kc, vc, la = chunks(q), chunks(k), chunks(v), chunks(log_a)
    b = jnp.cumsum(la, axis=2)
    b_last = b[:, :, -1:]
    q_in = qc * jnp.exp(b)
    k_in = kc * jnp.exp(-b)
    causal = jnp.tril(jnp.ones((GLA_CHUNK, GLA_CHUNK), dtype=bool))
    att = jnp.where(causal, jnp.einsum('nbthd,nbshd->nbhts', q_in, k_in), 0.0)
    o_intra = jnp.einsum('nbhts,nbshv->nbthv', att, vc)
    chunk_state = jnp.einsum('nbshd,nbshv->nbhdv', kc * jnp.exp(b_last - b), vc)
    chunk_decay = jnp.exp(b_last[:, :, 0])

    def step(s, xs):
        q_n, dec_n, st_n = xs
        o = jnp.einsum('bthd,bhdv->bthv', q_n, s)
        return dec_n[..., None] * s + st_n, o

    s_final, o_inter = lax.scan(step, s0.astype(F32), (q_in, chunk_decay, chunk_state))
    o = (o_intra + o_inter).transpose(1, 0, 2, 3, 4).reshape(B, T, H, dv)
    return o, s_final


def _gla_final_state(k, v, log_a):
    L = jnp.cumsum(log_a.astype(F32), axis=1)
    w = jnp.exp(L[:, -1:] - L)
    return jnp.einsum('bthd,bthv->bhdv', k.astype(F32) * w, v.astype(F32))


def _gla_output(o, gate, norm_w):
    B, T = o.shape[:2]
    of = o * lax.rsqrt(jnp.mean(o * o, axis=-1, keepdims=True) + NORM_EPS)
    of = of * norm_w.astype(F32).reshape(GLA_HEADS, GLA_DV)
    return of.reshape(B, T, GLA_WIDTH).astype(gate.dtype) * jax.nn.silu(gate)


def _flip(t):
    return jnp.flip(t, axis=1)


def _hybrid_layer(xl, xc, c, c_ctx, w_mod, b_mod, w_in, conv_w, conv_b, conv_ln_w, conv_ln_b,
                  attn_sink, gla_w_up, gla_b_up, gla_norm_w, w_out, rope_cos, rope_sin, update_ctx):
    B, S, _ = xl.shape
    D = D_MODEL
    shift, scale, gate = jnp.split(jax.nn.silu(c) @ w_mod + b_mod, 3, axis=-1)
    n_mod_c = 3 * D if update_ctx else 2 * D
    mod_c = jax.nn.silu(c_ctx) @ w_mod[:, :n_mod_c] + b_mod[:n_mod_c]
    hl = _rms_norm(xl) * (1.0 + scale[:, None]) + shift[:, None]
    hc = _rms_norm(xc) * (1.0 + mod_c[D:2 * D]) + mod_c[:D]
    pl = _project(hl, w_in)
    pc = _project(hc, w_in) if update_ctx else _project(hc, w_in, CTX_KV_COLUMNS)

    a_l = _conv_branch(pl, conv_w, conv_b, conv_ln_w, conv_ln_b)

    q_l = _apply_rope(_heads(pl['b_q'], ATTN_HEADS), rope_cos, rope_sin)
    k_l = _apply_rope(_heads(pl['b_k'], ATTN_KV_HEADS), rope_cos, rope_sin)
    v_l = _heads(pl['b_v'], ATTN_KV_HEADS)
    k_cx = _heads(pc['b_k'], ATTN_KV_HEADS)
    v_cx = _heads(pc['b_v'], ATTN_KV_HEADS)
    b_l = _windowed_attention(q_l, k_l, v_l, k_cx, v_cx, attn_sink) * jax.nn.silu(pl['b_gate'])

    gq_l = _heads(pl['c_q'], GLA_HEADS) * GLA_DK ** -0.5
    gk_l = _heads(pl['c_k'], GLA_HEADS)
    gv_l = _heads(pl['c_v'], GLA_HEADS)
    la_lf = _gla_decay(pl['c_lr_f'], gla_w_up[0], gla_b_up[0])
    la_lb = _gla_decay(pl['c_lr_b'], gla_w_up[1], gla_b_up[1])
    gk_c = _heads(pc['c_k'], GLA_HEADS)
    gv_c = _heads(pc['c_v'], GLA_HEADS)
    la_cf = _gla_decay(pc['c_lr_f'], gla_w_up[0], gla_b_up[0])
    la_cb = _gla_decay(pc['c_lr_b'], gla_w_up[1], gla_b_up[1])
    if update_ctx:
        gq_c = _heads(pc['c_q'], GLA_HEADS) * GLA_DK ** -0.5
        zeros = jnp.zeros((B, GLA_HEADS, GLA_DK, GLA_DV), F32)
        o_cf, s_f = _gla_chunked(gq_c, gk_c, gv_c, la_cf, zeros)
        o_cb, s_b = _gla_chunked(_flip(gq_c), _flip(gk_c), _flip(gv_c), _flip(la_cb), zeros)
    else:
        s_f = _gla_final_state(gk_c, gv_c, la_cf)
        s_b = _gla_final_state(_flip(gk_c), _flip(gv_c), _flip(la_cb))
    o_lf, _ = _gla_chunked(gq_l, gk_l, gv_l, la_lf, s_f)
    o_lb, _ = _gla_chunked(_flip(gq_l), _flip(gk_l), _flip(gv_l), _flip(la_lb), s_b)
    c_l = _gla_output(o_lf + _flip(o_lb), pl['c_gate'], gla_norm_w)

    y_l = jnp.concatenate([a_l, b_l, c_l], axis=-1) @ w_out
    xl = xl + gate[:, None] * y_l

    if update_ctx:
        a_c = _conv_branch(pc, conv_w, conv_b, conv_ln_w, conv_ln_b)
        q_cx = _heads(pc['b_q'], ATTN_HEADS)
        b_c = _ctx_attention(q_cx, k_cx, v_cx, attn_sink) * jax.nn.silu(pc['b_gate'])
        c_c = _gla_output(o_cf + _flip(o_cb), pc['c_gate'], gla_norm_w)
        y_c = jnp.concatenate([a_c, b_c, c_c], axis=-1) @ w_out
        xc = xc + mod_c[2 * D:] * y_c
    return xl, xc


def setup_inputs(seed: int = 0) -> dict:
    key = jax.random.key(seed)
    ks = jax.random.split(key, 17)
    nrm = jax.random.normal
    return {
        'x': nrm(ks[0], (BATCH, SEQ, D_MODEL), F32),
        'c': nrm(ks[1], (BATCH, D_MODEL), F32),
        'ctx': nrm(ks[2], (BATCH, CTX_LEN, D_MODEL), F32),
        'c_ctx': nrm(ks[3], (D_MODEL,), F32),
        'w_mod': nrm(ks[4], (DEPTH, D_MODEL, 3 * D_MODEL), F32) * (0.5 * D_MODEL ** -0.5),
        'b_mod': 0.01 * nrm(ks[5], (DEPTH, 3 * D_MODEL), F32),
        'w_in': nrm(ks[6], (DEPTH, D_MODEL, N_IN), F32) * D_MODEL ** -0.5,
        'conv_w': nrm(ks[7], (DEPTH, CONV_KERNEL, CONV_WIDTH), F32) * CONV_KERNEL ** -0.5,
        'conv_b': 0.01 * nrm(ks[8], (DEPTH, CONV_WIDTH), F32),
        'conv_ln_w': 1.0 + 0.01 * nrm(ks[9], (DEPTH, CONV_WIDTH), F32),
        'conv_ln_b': 0.01 * nrm(ks[10], (DEPTH, CONV_WIDTH), F32),
        'attn_sink': 0.5 * nrm(ks[11], (DEPTH, ATTN_HEADS), F32),
        'gla_w_up': nrm(ks[12], (DEPTH, 2, GLA_LOW_RANK, GLA_HEADS * GLA_DK), F32) * GLA_LOW_RANK ** -0.5,
        'gla_b_up': 0.1 * nrm(ks[13], (DEPTH, 2, GLA_HEADS * GLA_DK), F32),
        'gla_norm_w': 1.0 + 0.01 * nrm(ks[14], (DEPTH, GLA_WIDTH), F32),
        'w_out': nrm(ks[15], (DEPTH, D_MIX, D_MODEL), F32) * D_MIX ** -0.5,
        'final_norm_w': 1.0 + 0.01 * nrm(ks[16], (D_MODEL,), F32),
    }


def reference(x, c, ctx, c_ctx, w_mod, b_mod, w_in, conv_w, conv_b, conv_ln_w, conv_ln_b,
              attn_sink, gla_w_up, gla_b_up, gla_norm_w, w_out, final_norm_w):
    rope_cos, rope_sin = _rope_tables(x.shape[1])
    xl, xc = x, ctx
    for i in range(DEPTH):
        xl, xc = _hybrid_layer(
            xl, xc, c, c_ctx, w_mod[i], b_mod[i], w_in[i], conv_w[i], conv_b[i], conv_ln_w[i], conv_ln_b[i],
            attn_sink[i], gla_w_up[i], gla_b_up[i], gla_norm_w[i], w_out[i], rope_cos, rope_sin,
            update_ctx=(i < DEPTH - 1))
    return _rms_norm(xl) * final_norm_w
```

```python
import numpy as np
from contextlib import ExitStack
import concourse.bass as bass
import concourse.mybir as mybir
from concourse.bass_utils import run_bass_kernel_spmd

F32 = mybir.dt.float32
BF16 = mybir.dt.bfloat16
ALU = mybir.AluOpType
AF = mybir.ActivationFunctionType

N_DMA_SEMS = 2
EPOCH = 30000

D = 1024
S = 4096
CT = 256
N = 256
NBK = N // 128
L = 2
NIN = 2848
EPS = 1e-6
CORES_PER_LAUNCH = 8


class Prog:
    ENGS = ("pe", "act", "dve", "pool", "sp")

    def __init__(self, nc):
        self.nc = nc
        self.ops = []
        self.ndma = 0

    def op(self, eng, fn, reads=(), writes=()):
        self.ops.append(dict(eng=eng, fn=fn, reads=tuple(reads), writes=tuple(writes), dma=False))

    def dma(self, eng, out, in_, reads=(), writes=()):
        self.ops.append(dict(eng=eng, fn=lambda e: e.dma_start(out=out, in_=in_),
                             reads=tuple(reads), writes=tuple(writes), dma=True, didx=self.ndma))
        self.ndma += 1

    def analyze(self):
        last_w = {}
        readers = {}
        for o in self.ops:
            ex = tuple(r for r in o["reads"] if isinstance(r, str) and r.startswith("pb") and r not in o["writes"])
            if ex:
                o["writes"] = o["writes"] + ex
        for i, o in enumerate(self.ops):
            deps = set()
            for r in o["reads"]:
                if r in last_w:
                    deps.add(last_w[r])
            for w in o["writes"]:
                if w in last_w:
                    deps.add(last_w[w])
                for rd in readers.get(w, ()):
                    deps.add(rd)
            deps.discard(i)
            fd = set()
            for d in deps:
                p = self.ops[d]
                if (not p["dma"]) and (not o["dma"]) and p["eng"] == o["eng"]:
                    if o["eng"] == "pe":
                        continue
                fd.add(d)
            o["deps"] = fd
            for r in o["reads"]:
                readers.setdefault(r, []).append(i)
            for w in o["writes"]:
                last_w[w] = i
                readers[w] = []
        for o in self.ops:
            o["need_inc"] = False
        for o in self.ops:
            for d in o["deps"]:
                self.ops[d]["need_inc"] = True
        cnt = {e: 0 for e in self.ENGS}
        for o in self.ops:
            if o["dma"]:
                k = o["didx"]
                o["sem"] = ("dma", k % N_DMA_SEMS, 0)
                o["val"] = 16 * (k // N_DMA_SEMS + 1)
            elif o["need_inc"]:
                cnt[o["eng"]] += 1
                ep = (cnt[o["eng"]] - 1) // EPOCH
                o["sem"] = ("eng", o["eng"], ep)
                o["val"] = cnt[o["eng"]] - ep * EPOCH
        self.n_epochs = {e: (cnt[e] + EPOCH - 1) // EPOCH for e in self.ENGS}

    def emit(self, block, sems_eng, sems_dma):
        dma_ops = [o for o in self.ops if o["dma"]]
        by_eng = {e: [] for e in self.ENGS}
        for i, o in enumerate(self.ops):
            by_eng[o["eng"]].append(i)

        def semh(s):
            return sems_eng[(s[1], s[2])] if s[0] == "eng" else sems_dma[s[1]]

        def run(e, handle):
            waited = {}
            for i in by_eng[e]:
                o = self.ops[i]
                need = {}
                for d in o["deps"]:
                    p = self.ops[d]
                    s = p["sem"]
                    need[s] = max(need.get(s, 0), p["val"])
                if o["dma"]:
                    k = o["didx"]
                    if k >= N_DMA_SEMS:
                        s = ("dma", k % N_DMA_SEMS, 0)
                        need[s] = max(need.get(s, 0), 16 * (k // N_DMA_SEMS))
                for s, v in need.items():
                    if waited.get(s, 0) >= v:
                        continue
                    handle.wait_ge(semh(s), v)
                    waited[s] = v
                ins = o["fn"](handle)
                if o["dma"]:
                    ins.then_inc(semh(o["sem"]), 16)
                elif o["need_inc"]:
                    ins.then_inc(semh(o["sem"]), 1)
            if e == "sp":
                final = {}
                for o in dma_ops:
                    final[o["sem"]] = max(final.get(o["sem"], 0), o["val"])
                for s, v in final.items():
                    handle.wait_ge(semh(s), v)

        @block.tensor
        def _(h):
            run("pe", h)

        @block.scalar
        def _(h):
            run("act", h)

        @block.vector
        def _(h):
            run("dve", h)

        @block.gpsimd
        def _(h):
            run("pool", h)

        @block.sync
        def _(h):
            run("sp", h)


def _chunks():
    A = [[(0, 128)], [(128, 128)], [(256, 128)], [(384, 128)], [(1280, 128)],
         [(1280 + 32, 32), (1280, 32), (1280 + 96, 32), (1280 + 64, 32)],
         [(2176, 128)], [(2048, 128)], [(2560, 32)]]
    At = [(1408, 128), (2304, 256)]
    B = [[(512, 128)], [(640, 128)]]
    for g in range(4):
        B.append([(768 + g * 64, 64), (768 + (4 + g) * 64, 64)])
    for g in range(4):
        a = 768 + g * 64
        b = 768 + (4 + g) * 64
        B.append([(a + 32, 32), (a, 32), (b + 32, 32), (b, 32)])
    for g in range(4):
        B.append([(1536 + g * 64, 64), (1536 + (4 + g) * 64, 64)])
    B.append([(2592, 128)])
    B.append([(2720, 128)])
    return A, At, B


def bc_mid(ap, n):
    a = [list(x) for x in ap.ap]
    return bass.AP(ap.tensor, ap.offset, [a[0], [0, n]] + a[1:])


def bc_last(ap, n):
    a = [list(x) for x in ap.ap]
    a[-1] = [0, n]
    return bass.AP(ap.tensor, ap.offset, a)


def build(debug=False, stg_lim=99, layers=(0, 1)):
    nc = bass.Bass("TRN2", target_bir_lowering=False)
    es = ExitStack()
    P = Prog(nc)

    def din(name, shape, dt=F32):
        return nc.dram_tensor(name, list(shape), dt, kind="ExternalInput").ap()

    def dscr(name, shape, dt):
        return nc.dram_tensor(name, list(shape), dt, kind="Internal").ap()

    x_d = din("x", [S, D])
    ctx_d = din("ctx", [CT, D])
    cc_d = din("cc", [128, 8, 2])
    wmod_d = din("w_mod", [L, D, 3 * D])
    bmT_d = din("bmT", [L, 128, 24])
    win_d = din("w_in", [L, D, NIN])
    wout_d = din("w_out", [L, D, D])
    cw_d = din("cw", [L, 128, 2, 31])
    cp_d = din("cp", [L, 128, 2, 3])
    sink_d = din("sink", [L, 8])
    wup_d = din("wup", [L, 32, 2, 128])
    bup_d = din("bup", [L, 128, 2])
    gnw_d = din("gnw", [L, 128, 2])
    fnw_d = din("fnw", [D])
    ident_d = din("ident", [128, 128])
    maskP_d = din("maskP", [128, 128])
    maskN_d = din("maskN", [128, 128])
    rst_d = din("rst", [128, N])
    cos_d = din("cosT", [128, S])
    sin_d = din("sinT", [128, S])
    out_d = nc.dram_tensor("out", [S, D], F32, kind="ExternalOutput").ap()

    scr = {}
    for nm, T in (("l", S), ("c", CT)):
        scr[nm] = dict(
            T=T,
            hT=dscr(f"hT_{nm}", [8, 128, T], BF16),
            uT=dscr(f"uT_{nm}", [2, 128, T + 32], BF16),
            kT=dscr(f"kT_{nm}", [128, T], BF16),
            vaug=dscr(f"vaug_{nm}", [T // 128, 128, 256], BF16),
            vg=dscr(f"vg_{nm}", [T // 128, 128, 256], BF16),
            qk=dscr(f"qk_{nm}", [4, 128, T], BF16),
        )
    x1_d = dscr("x1", [S, D], F32)
    xc1_d = dscr("xc1", [CT, D], F32)

    dbg = {}
    if debug:
        dbg["hT"] = nc.dram_tensor("dbg_hT", [8, 128, S], BF16, kind="ExternalOutput").ap()
        dbg["cat"] = nc.dram_tensor("dbg_cat", [8, 128, S], BF16, kind="ExternalOutput").ap()
        dbg["x1"] = nc.dram_tensor("dbg_x1", [S, D], F32, kind="ExternalOutput").ap()
        dbg["xc1"] = nc.dram_tensor("dbg_xc1", [CT, D], F32, kind="ExternalOutput").ap()
        dbg["mod"] = nc.dram_tensor("dbg_mod", [128, 48], F32, kind="ExternalOutput").ap()

    with es:
        def sb(name, shape, dt=F32):
            return es.enter_context(nc.sbuf_tensor("s_" + name, list(shape), dt))

        pb = [es.enter_context(nc.psum_tensor(f"pb{i}", [128, 512], F32)) for i in range(8)]

        identf = sb("identf", [128, 128]); identb = sb("identb", [128, 128], BF16)
        maskP = sb("maskP", [128, 128], BF16); maskN = sb("maskN", [128, 128], BF16)
        rst = sb("rst", [128, N])
        ones256 = sb("ones256", [128, 128]); ones64 = sb("ones64", [128, 128]); onesf = sb("onesf", [128, 128])
        stage = [sb("stage0", [128, 8, 128]), sb("stage1", [128, 8, 128])]
        scc = sb("scc", [128, 8, 2]); bmT = sb("bmT", [128, 24]); modT = sb("modT", [128, 24, 2]); one1p = sb("one1p", [128, 8, 2])
        gate_bc = [sb("gate_bc0", [128, D]), sb("gate_bc1", [128, D])]
        cw = sb("cw", [128, 2, 31]); cp = sb("cp", [128, 2, 3]); esink = sb("esink", [128, 8])
        wupf = sb("wupf", [32, 2, 128]); wupb = sb("wupb", [32, 2, 128], BF16)
        nbup = sb("nbup", [128, 2]); gnw = sb("gnw", [128, 2])
        dg = sb("dg", [128, 128])
        wAB = sb("wAB", [128, 8, 2048], BF16)
        wout = sb("wout", [128, 8, D], BF16)
        diag = sb("diag", [128, 2, 31, 128], BF16)
        Sin = {"l": [sb("SinLf", [128, 32, 64], BF16), sb("SinLb", [128, 32, 64], BF16)],
               "c": [sb("SinCf", [128, 2, 64], BF16), sb("SinCb", [128, 2, 64], BF16)]}
        stt = {"l": [sb("stLf", [128, 32, 64]), sb("stLb", [128, 32, 64])],
               "c": [sb("stCf", [128, 2, 64]), sb("stCb", [128, 2, 64])]}
        dec = {"l": [sb("decLf", [128, 32]), sb("decLb", [128, 32])],
               "c": [sb("decCf", [128, 2]), sb("decCb", [128, 2])]}
        Spp = [sb("Spp0", [128, 64]), sb("Spp1", [128, 64])]
        Sfin = [sb("Sfinf", [128, 64]), sb("Sfinb", [128, 64])]
        kzc = [sb("kzc0", [128, CT], BF16), sb("kzc1", [128, CT], BF16)]
        vaugc = sb("vaugc", [128, 2, 256], BF16)
        hT = sb("hT", [128, 8, N], BF16)
        xt = sb("xt", [128, D]); x2 = sb("x2", [128, D])
        smallf = sb("smallf", [128, 8])
        cosb = sb("cosb", [128, N]); sinb = sb("sinb", [128, N]); t1 = sb("t1", [128, N]); t2 = sb("t2", [128, N])
        bigf = [sb(f"bigf{i}", [128, 512]) for i in range(6)]
        sig = sb("sig", [128, N]); uTt = sb("uTt", [128, 2, N], BF16); kTt = sb("kTt", [128, N], BF16)
        qin_t = [sb("qin_f", [128, N], BF16), sb("qin_b", [128, N], BF16)]
        kin_t = [sb("kin_f", [128, N], BF16), sb("kin_b", [128, N], BF16)]
        kdtok = [sb("kdtok_f", [128, NBK, 128], BF16), sb("kdtok_b", [128, NBK, 128], BF16)]
        vaugt = sb("vaugt", [128, NBK, 256], BF16); vgt = sb("vgt", [128, NBK, 256], BF16)
        lrT = sb("lrT", [32, N], BF16)
        zpad = sb("zpad", [128, 2, 16], BF16)
        sga = sb("sga", [128, 2, N]); qT = sb("qT", [128, 4, N], BF16); sgb = sb("sgb", [128, 4, N], BF16); sgc = sb("sgc", [128, 2, N])
        uw = sb("uw", [128, 2, N + 32], BF16)
        cf = sb("cf", [128, 2, N]); sq = sb("sq", [128, 2, N]); m2 = sb("m2", [128, N]); crs = sb("crs", [128, N])
        tt = [sb("tt0", [128, N]), sb("tt1", [128, N])]
        catT = sb("catT", [128, 8, N], BF16)
        kzw = [sb("kzw0", [128, 4, 128], BF16), sb("kzw1", [128, 4, 128], BF16)]
        vaugw = sb("vaugw", [128, 4, 256], BF16)
        eT = [sb(f"eT{i}", [128, 512], BF16) for i in range(3)]
        qkl = [sb(f"qkl{i}", [128, N], BF16) for i in range(4)]
        vgl = sb("vgl", [128, NBK, 256], BF16)
        bd = [sb("bd_f", [128, 4, 128], BF16), sb("bd_b", [128, 4, 128], BF16)]
        attm = [sb("attm_f", [128, 4, 128], BF16), sb("attm_b", [128, 4, 128], BF16)]
        sbd = [sb("sbd_f", [128, 256], BF16), sb("sbd_b", [128, 256], BF16)]
        rgc = sb("rgc", [128, 128])

        q_rr = {"n": 0}

        def dq():
            q_rr["n"] += 1
            return "sp"

        def act(out, in_, func, reads, writes, bias=None, scale=None, accum=None):
            kw = {}
            if bias is not None:
                kw["bias"] = bias
            if scale is not None:
                kw["scale"] = scale
            if accum is not None:
                kw["accum_out"] = accum
            P.op("act", lambda e: e.activation(out=out, in_=in_, func=func, **kw), reads, writes)

        def tt_op(eng, out, in0, in1, op, reads, writes):
            P.op(eng, lambda e: e.tensor_tensor(out=out, in0=in0, in1=in1, op=op), reads, writes)

        def ts_op(eng, out, in0, s1, s2, op0, op1, reads, writes):
            if op1 is None and eng == "pool" and op0 == ALU.mult:
                P.op(eng, lambda e: e.tensor_scalar(out=out, in0=in0, scalar1=s1, scalar2=1.0, op0=ALU.mult, op1=ALU.mult), reads, writes)
            elif op1 is None:
                P.op(eng, lambda e: e.tensor_scalar(out=out, in0=in0, scalar1=s1, scalar2=None, op0=op0), reads, writes)
            else:
                P.op(eng, lambda e: e.tensor_scalar(out=out, in0=in0, scalar1=s1, scalar2=s2, op0=op0, op1=op1), reads, writes)

        def stt_op(out, in0, scalar, in1, op0, op1, reads, writes):
            P.op("dve", lambda e: e.scalar_tensor_tensor(out=out, in0=in0, scalar=scalar, in1=in1, op0=op0, op1=op1), reads, writes)

        def cp_op(eng, out, in_, reads, writes):
            if eng == "act":
                act(out, in_, AF.Copy, reads, writes)
            else:
                P.op(eng, lambda e: e.tensor_copy(out=out, in_=in_), reads, writes)

        def mm(out, lhsT, rhs, start, stop, reads, writes):
            P.op("pe", lambda e: e.matmul(out, lhsT=lhsT, rhs=rhs, start=start, stop=stop), reads, writes)

        def memset(eng, ap, v, writes):
            P.op(eng, lambda e: e.memset(ap, v), (), writes)

        P.dma("sp", identf[:], ident_d, writes=["identf"])
        cp_op("dve", identb[:], identf[:], ["identf"], ["identb"])
        P.dma("sp", stage[0][:, 0, :], maskP_d, writes=["stage0"])
        cp_op("dve", maskP[:], stage[0][:, 0, :], ["stage0"], ["maskP"])
        P.dma("sp", stage[1][:, 0, :], maskN_d, writes=["stage1"])
        cp_op("dve", maskN[:], stage[1][:, 0, :], ["stage1"], ["maskN"])
        P.dma("sp", rst[:], rst_d, writes=["rst"])
        memset("pool", ones256[:], 1.0 / 256.0, ["ones256"])
        memset("pool", ones64[:], 1.0 / 64.0, ["ones64"])
        memset("pool", onesf[:], 1.0, ["onesf"])
        memset("pool", zpad[:], 0.0, ["zpad"])
        for d_ in range(2):
            memset("pool", bd[d_][:], 0.0, [f"bd{d_}"])
            memset("pool", sbd[d_][:], 0.0, [f"sbd{d_}"])
            memset("pool", kzw[d_][:], 0.0, [f"kzw{d_}"])
            memset("pool", kzc[d_][:], 0.0, [f"kzc{d_}"])
        memset("pool", vaugt[:, :, 0:64], 1.0, ["vaugt"])
        memset("pool", vaugt[:, :, 192:256], 1.0, ["vaugt"])
        for nm in ("l", "c"):
            T = scr[nm]["T"]
            P.dma("sp", scr[nm]["uT"][:, :, 0:16].rearrange("c p t -> p c t"), zpad[:], reads=["zpad"], writes=[("uTs", nm, "padl")])
            P.dma("sp", scr[nm]["uT"][:, :, 16 + T:32 + T].rearrange("c p t -> p c t"), zpad[:], reads=["zpad"], writes=[("uTs", nm, "padr")])

        Achunks, Atok, Bchunks = _chunks()
        stg_rr = {"n": 0}
        cast_rr = {"n": 0}

        def load_chunk_cols(l, pieces, dst_ap, dst_key, width):
            i = stg_rr["n"] % 2
            stg_rr["n"] += 1
            off = 0
            for (cs, w) in pieces:
                P.dma(dq(), stage[i][:, :, off:off + w], win_d[l, :, cs:cs + w].rearrange("(k p) n -> p k n", p=128),
                      writes=[f"stage{i}"])
                off += w
            eng = ("pool", "dve", "act")[cast_rr["n"] % 3]
            cast_rr["n"] += 1
            cp_op(eng, dst_ap, stage[i][:, :, 0:width], [f"stage{i}"], [dst_key])

        def seq_info(nm, l):
            T = scr[nm]["T"]
            if nm == "l":
                xsrc = x_d if (l == 0 or 0 not in layers) else x1_d
                xdst = x1_d if l == 0 else out_d
            else:
                xsrc = ctx_d if (l == 0 or 0 not in layers) else xc1_d
                xdst = xc1_d
            return T, xsrc, xdst

        for l in layers:
            upd = (l == 0)
            if l == layers[0]:
                P.dma("sp", scc[:], cc_d, writes=["scc"])
                act(scc[:], scc[:], AF.Silu, ["scc"], ["scc"])
            P.dma("sp", bmT[:], bmT_d[l], writes=["bmT"])
            P.dma("sp", cw[:], cw_d[l], writes=["cw"])
            P.dma("sp", cp[:], cp_d[l], writes=["cp"])
            P.dma("sp", esink[:], bass.AP(sink_d.tensor, sink_d[l].offset, [[0, 128], [1, 8]]), writes=["esink"])
            act(esink[:], esink[:], AF.Exp, ["esink"], ["esink"])
            P.dma("sp", wupf[:], wup_d[l], writes=["wupf"])
            cp_op("dve", wupb[:], wupf[:], ["wupf"], ["wupb"])
            P.dma("sp", nbup[:], bup_d[l], writes=["nbup"])
            ts_op("dve", nbup[:], nbup[:], -1.0, None, ALU.mult, None, ["nbup"], ["nbup"])
            P.dma("sp", gnw[:], gnw_d[l], writes=["gnw"])
            modps = pb[6]
            for j in range(24):
                i = stg_rr["n"] % 2
                stg_rr["n"] += 1
                P.dma(dq(), stage[i][:], wmod_d[l, :, j * 128:(j + 1) * 128].rearrange("(k p) n -> p k n", p=128),
                      writes=[f"stage{i}"])
                for k in range(8):
                    mm(modps[:, 2 * j:2 * j + 2], stage[i][:, k, :], scc[:, k, :], k == 0, k == 7,
                       [f"stage{i}", "scc"], ["pb6"])
            tt_op("dve", modT[:], modps[:, 0:48].rearrange("p (j s) -> p j s", s=2), bc_last(bmT[:].rearrange("p (j o) -> p j o", o=1), 2),
                  ALU.add, ["pb6", "bmT"], ["modT"])
            ts_op("dve", one1p[:], modT[:, 8:16, :], 1.0, None, ALU.add, None, ["modT"], ["one1p"])
            if debug and l == 0:
                P.dma("sp", dbg["mod"], modT[:].rearrange("p j s -> p (j s)"), reads=["modT"])
            for s_ in range(2):
                if s_ == 1 and not upd:
                    continue
                for k in range(8):
                    ts_op("dve", dg[:], identf[:], modT[:, 16 + k, s_:s_ + 1], None, ALU.mult, None, ["identf", "modT"], ["dg"])
                    bank = pb[4 + (k // 4)]
                    mm(bank[:, (k % 4) * 128:(k % 4 + 1) * 128], onesf[:], dg[:], True, True, ["onesf", "dg"], [f"pb{4 + k // 4}"])
                cp_op("act", gate_bc[s_][:, 0:512], pb[4][:], ["pb4"], [f"gate_bc{s_}"])
                cp_op("act", gate_bc[s_][:, 512:1024], pb[5][:], ["pb5"], [f"gate_bc{s_}"])
            if l == L - 1:
                P.dma("sp", gate_bc[1][:], bass.AP(fnw_d.tensor, 0, [[0, 128], [1, D]]), writes=["gate_bc1"])
            if stg_lim < 1:
                break
            for ci, pieces in enumerate(Achunks):
                wdt = sum(w for _, w in pieces)
                slot = 3 + ci
                load_chunk_cols(l, pieces, wAB[:, :, slot * 128:slot * 128 + wdt], ("wAB", slot), wdt)
            off = 0
            for (cs, w) in Atok:
                for sub in range(w // 128):
                    slot = (off // 128)
                    load_chunk_cols(l, [(cs + sub * 128, 128)], wAB[:, :, slot * 128:(slot + 1) * 128], ("wAB", slot), 128)
                    off += 128
            for kc in range(8):
                i = stg_rr["n"] % 2
                stg_rr["n"] += 1
                st2 = stage[i][:].rearrange("p k n -> p (k n)")
                if kc < 2:
                    P.dma(dq(), st2, wout_d[l, kc * 128:(kc + 1) * 128, :], writes=[f"stage{i}"])
                elif kc < 6:
                    g = kc - 2
                    P.dma(dq(), st2[0:64, :], wout_d[l, 256 + g * 64:256 + (g + 1) * 64, :], writes=[f"stage{i}"])
                    P.dma(dq(), st2[64:128, :], wout_d[l, 256 + (4 + g) * 64:256 + (5 + g) * 64, :], writes=[f"stage{i}"])
                else:
                    P.dma(dq(), st2, wout_d[l, 768 + (kc - 6) * 128:768 + (kc - 5) * 128, :], writes=[f"stage{i}"])
                if kc < 6:
                    cp_op(("pool", "dve")[kc % 2], wout[:, kc, :], st2, [f"stage{i}"], [("wout", kc)])
                else:
                    ts_op("dve", wout[:, kc, :], st2, gnw[:, kc - 6:kc - 5], None, ALU.mult, None, [f"stage{i}", "gnw"], [("wout", kc)])
            for c in range(2):
                for k in range(31):
                    ts_op(("pool", "dve")[k % 2], diag[:, c, k, :], identb[:], cw[:, c, k:k + 1], None, ALU.mult, None,
                          ["identb", "cw"], [("diag", c)])

            def pass_a(nm):
                T, xsrc, _ = seq_info(nm, l)
                sidx = 0 if nm == "l" else 1
                sc = scr[nm]
                rope = (nm == "l")
                for st in range(T // N):
                    t0 = st * N
                    for j in range(NBK):
                        tok = t0 + j * 128
                        P.dma("sp", xt[:], xsrc[tok:tok + 128, :], reads=([("xo", nm, tok)] if l > 0 else []), writes=["xt"])
                        act(x2[:], xt[:], AF.Square, ["xt"], ["x2"])
                        P.op("dve", lambda e: e.tensor_reduce(out=smallf[:, 0:1], in_=x2[:], axis=mybir.AxisListType.X, op=ALU.add), ["x2"], ["ss"])
                        ts_op("dve", smallf[:, 1:2], smallf[:, 0:1], 1.0 / D, EPS, ALU.mult, ALU.add, ["ss"], ["ss1"])
                        act(smallf[:, 2:3], smallf[:, 1:2], AF.Sqrt, ["ss1"], ["ss2"])
                        P.op("dve", lambda e: e.reciprocal(out=smallf[:, 3:4], in_=smallf[:, 2:3]), ["ss2"], ["rstd"])
                        act(x2[:], xt[:], AF.Copy, ["xt", "rstd"], ["x2"], scale=smallf[:, 3:4])
                        for k in range(8):
                            bank = 2 + k // 4
                            P.op("pe", lambda e, k=k, bank=bank: e.transpose(out=pb[bank][:, (k % 4) * 128:(k % 4 + 1) * 128],
                                                                              in_=x2[:, k * 128:(k + 1) * 128], identity=identf[:]),
                                 ["x2", "identf"], [f"pb{bank}"])
                        for k in range(8):
                            bank = 2 + k // 4
                            src = pb[bank][:, (k % 4) * 128:(k % 4 + 1) * 128]
                            dst = hT[:, k, j * 128:(j + 1) * 128]
                            if k < 4:
                                ts_op("dve", dst, src, one1p[:, k, sidx:sidx + 1], modT[:, k, sidx:sidx + 1], ALU.mult, ALU.add,
                                      [f"pb{bank}", "one1p", "modT"], ["hT"])
                            else:
                                act(dst, src, AF.Identity, [f"pb{bank}", "one1p", "modT"], ["hT"],
                                    bias=modT[:, k, sidx:sidx + 1], scale=one1p[:, k, sidx:sidx + 1])
                    P.dma("sp", sc["hT"][:, :, t0:t0 + N].rearrange("k p t -> p k t"), hT[:], reads=["hT"], writes=[("hTs", nm, st)])
                    if debug and l == 0 and nm == "l":
                        P.dma("sp", dbg["hT"][:, :, t0:t0 + N].rearrange("k p t -> p k t"), hT[:], reads=["hT"])
                    if rope:
                        P.dma("sp", cosb[:], cos_d[:, t0:t0 + N], writes=["cosb"])
                        P.dma("sp", sinb[:], sin_d[:, t0:t0 + N], writes=["sinb"])

                    def projA(ci, bank, M=128):
                        slot = 3 + ci
                        for k in range(8):
                            mm(pb[bank][0:M, 0:N], wAB[:, k, slot * 128:slot * 128 + M], hT[:, k, :], k == 0, k == 7,
                               [("wAB", slot), "hT"], [f"pb{bank}"])

                    for c in range(2):
                        projA(2 + c, 0)
                        act(sig[:], pb[0][:, 0:N], AF.Sigmoid, ["pb0"], ["sig"])
                        projA(c, 1)
                        tt_op("dve", uTt[:, c, :], pb[1][:, 0:N], sig[:], ALU.mult, ["pb1", "sig"], ["uTt"])
                    P.dma("sp", sc["uT"][:, :, 16 + t0:16 + t0 + N].rearrange("c p t -> p c t"), uTt[:], reads=["uTt"],
                          writes=[("uTs", nm, st)])
                    projA(4, 0)
                    if rope:
                        projA(5, 1)
                        tt_op("dve", t1[:], pb[0][:, 0:N], cosb[:], ALU.mult, ["pb0", "cosb"], ["t1"])
                        tt_op("dve", t2[:], pb[1][:, 0:N], sinb[:], ALU.mult, ["pb1", "sinb"], ["t2"])
                        tt_op("pool", kTt[:], t1[:], t2[:], ALU.add, ["t1", "t2"], ["kTt"])
                    else:
                        cp_op("act", kTt[:], pb[0][:, 0:N], ["pb0"], ["kTt"])
                    P.dma("sp", sc["kT"][:, t0:t0 + N], kTt[:], reads=["kTt"], writes=[("kTs", nm, st)])
                    projA(8, 0, M=32)
                    cp_op("act", lrT[:], pb[0][0:32, 0:N], ["pb0"], ["lrT"])
                    for d_ in range(2):
                        T2 = bigf[0][:, d_ * N:(d_ + 1) * N]; G = bigf[1][:, d_ * N:(d_ + 1) * N]
                        eM = bigf[2][:, d_ * N:(d_ + 1) * N]; eP = bigf[3][:, d_ * N:(d_ + 1) * N]
                        k32 = bigf[4][:, d_ * N:(d_ + 1) * N]; kd32 = bigf[5][:, d_ * N:(d_ + 1) * N]
                        kT2, kG, keM, keP, kk32, kkd = [(f"bigf{i}", d_) for i in range(6)]
                        zb = 1
                        mm(pb[zb][:, 0:N], wupb[:, d_, :], lrT[:], True, True, ["wupb", "lrT"], [f"pb{zb}"])
                        act(T2, pb[zb][:, 0:N], AF.Exp, [f"pb{zb}", "nbup"], [kT2], bias=nbup[:, d_:d_ + 1], scale=-1.0)
                        act(T2, T2, AF.Ln, [kT2], [kT2], bias=1.0)
                        P.op("dve", lambda e, T2=T2, G=G: e.tensor_tensor_scan(out=G, data0=rst[:], data1=T2, initial=0.0,
                                                                                 op0=ALU.mult, op1=ALU.add),
                             ["rst", kT2], [kG])
                        for j in range(NBK):
                            blk = st * NBK + j
                            act(dec[nm][d_][:, blk:blk + 1], G[:, j * 128 + 127:j * 128 + 128], AF.Exp, [kG], [("dec", nm, d_)], scale=-1.0 / 16)
                        if d_ == 1:
                            for j in range(NBK):
                                sl = slice(j * 128, (j + 1) * 128)
                                stt_op(eM[:, sl], T2[:, sl], G[:, j * 128 + 127:j * 128 + 128], G[:, sl], ALU.add, ALU.subtract,
                                       [kT2, kG], [keM])
                            cp_op("pool", G, eM, [keM], [kG])
                        act(eM, G, AF.Exp, [kG], [keM], scale=-1.0 / 16)
                        act(eP, G, AF.Exp, [kG], [keP], scale=1.0 / 16)
                        projA(6, 0)
                        tt_op("dve", k32, pb[0][:, 0:N], eP, ALU.mult, ["pb0", keP], [kk32])
                        cp_op("pool", kin_t[d_][:], k32, [kk32], [f"kin{d_}"])
                        projA(7, 0)
                        stt_op(qin_t[d_][:], pb[0][:, 0:N], 32.0 ** -0.5, eM, ALU.mult, ALU.mult, ["pb0", keM], [f"qin{d_}"])
                        P.dma("sp", sc["qk"][d_, :, t0:t0 + N], qin_t[d_][:], reads=[f"qin{d_}"], writes=[("qks", nm, st, d_)])
                        P.dma("sp", sc["qk"][2 + d_, :, t0:t0 + N], kin_t[d_][:], reads=[f"kin{d_}"], writes=[("qks", nm, st, 2 + d_)])
                        for j in range(NBK):
                            blk = st * NBK + j
                            sl = slice(j * 128, (j + 1) * 128)
                            ts_op("pool", kd32[:, sl], k32[:, sl], dec[nm][d_][:, blk:blk + 1], None, ALU.mult, None,
                                  [kk32, ("dec", nm, d_)], [kkd])
                            P.op("pe", lambda e, j=j, kd32=kd32, sl=sl: e.transpose(out=pb[4][:, j * 128:(j + 1) * 128], in_=kd32[:, sl], identity=identf[:]),
                                 [kkd, "identf"], ["pb4"])
                            cp_op("act", kdtok[d_][:, j, :], pb[4][:, j * 128:(j + 1) * 128], ["pb4"], [f"kdtok{d_}"])
                    for j in range(NBK):
                        blk = st * NBK + j
                        for k in range(8):
                            mm(pb[5][:, 0:384], hT[:, k, j * 128:(j + 1) * 128], wAB[:, k, 0:384], k == 0, k == 7,
                               ["hT", ("wAB", 0), ("wAB", 1), ("wAB", 2)], ["pb5"])
                        cp_op("act", vaugt[:, j, 64:192], pb[5][:, 0:128], ["pb5"], ["vaugt"])
                        cp_op("act", vgt[:, j, :], pb[5][:, 128:384], ["pb5"], ["vgt"])
                        for d_ in range(2):
                            mm(pb[6 + d_][:, 0:256], kdtok[d_][:, j, :], vgt[:, j, :], True, True, [f"kdtok{d_}", "vgt"], [f"pb{6 + d_}"])
                            for h in range(4):
                                cp_op(("dve", "act")[d_], stt[nm][d_][32 * h:32 * h + 32, blk, :],
                                      pb[6 + d_][32 * h:32 * h + 32, h * 64:(h + 1) * 64], [f"pb{6 + d_}"], [("stt", nm, d_)])
                    P.dma("sp", sc["vaug"][st * NBK:(st + 1) * NBK].rearrange("b p c -> p b c"), vaugt[:], reads=["vaugt"],
                          writes=[("vaugs", nm, st)])
                    P.dma("sp", sc["vg"][st * NBK:(st + 1) * NBK].rearrange("b p c -> p b c"), vgt[:], reads=["vgt"],
                          writes=[("vgs", nm, st)])
                nblk = T // 128
                for d_ in range(2):
                    order = list(range(nblk)) if d_ == 0 else list(range(nblk - 1, -1, -1))
                    cur = 0
                    if nm == "c":
                        memset("dve", Spp[0][:], 0.0, ["Spp0"])
                    else:
                        cp_op("dve", Spp[0][:], Sfin[d_][:], [f"Sfin{d_}"], ["Spp0"])
                    for blk in order:
                        cp_op("pool", Sin[nm][d_][:, blk, :], Spp[cur][:], [f"Spp{cur}"], [("Sin", nm, d_)])
                        stt_op(Spp[1 - cur][:], Spp[cur][:], dec[nm][d_][:, blk:blk + 1], stt[nm][d_][:, blk, :], ALU.mult, ALU.add,
                               [f"Spp{cur}", ("dec", nm, d_), ("stt", nm, d_)], [f"Spp{1 - cur}"])
                        cur = 1 - cur
                    if nm == "c":
                        cp_op("dve", Sfin[d_][:], Spp[cur][:], [f"Spp{cur}"], [f"Sfin{d_}"])

            def pass_b(nm):
                T, xsrc, xdst = seq_info(nm, l)
                sidx = 0 if nm == "l" else 1
                sc = scr[nm]
                rope = (nm == "l")
                nblk = T // 128
                last = (l == L - 1)
                for st in range(T // N):
                    t0 = st * N
                    P.dma("sp", hT[:], sc["hT"][:, :, t0:t0 + N].rearrange("k p t -> p k t"), reads=[("hTs", nm, st)], writes=["hT"])
                    if rope:
                        P.dma("sp", cosb[:], cos_d[:, t0:t0 + N], writes=["cosb"])
                        P.dma("sp", sinb[:], sin_d[:, t0:t0 + N], writes=["sinb"])

                    def projB(ci, bank):
                        for k in range(8):
                            mm(pb[bank][:, 0:N], wAB[:, k, ci * 128:(ci + 1) * 128], hT[:, k, :], k == 0, k == 7,
                               [("wAB", ci), "hT"], [f"pb{bank}"])

                    for c in range(2):
                        projB(c, c)
                        act(sga[:, c, :], pb[c][:, 0:N], AF.Silu, [f"pb{c}"], ["sga"])
                    for g in range(4):
                        projB(10 + g, g % 2)
                        act(sgb[:, g, :], pb[g % 2][:, 0:N], AF.Silu, [f"pb{g % 2}"], ["sgb"])
                    for c in range(2):
                        projB(14 + c, c)
                        act(sgc[:, c, :], pb[c][:, 0:N], AF.Silu, [f"pb{c}"], ["sgc"])
                    for g in range(4):
                        projB(2 + g, 0)
                        if rope:
                            projB(6 + g, 1)
                            tt_op("dve", t1[:], pb[0][:, 0:N], cosb[:], ALU.mult, ["pb0", "cosb"], ["t1"])
                            tt_op("dve", t2[:], pb[1][:, 0:N], sinb[:], ALU.mult, ["pb1", "sinb"], ["t2"])
                            tt_op("pool", qT[:, g, :], t1[:], t2[:], ALU.add, ["t1", "t2"], ["qT"])
                        else:
                            cp_op("act", qT[:, g, :], pb[0][:, 0:N], ["pb0"], ["qT"])
                    rd = [("uTs", nm, st)]
                    if st > 0:
                        rd.append(("uTs", nm, st - 1))
                    else:
                        rd.append(("uTs", nm, "padl"))
                    if st < T // N - 1:
                        rd.append(("uTs", nm, st + 1))
                    else:
                        rd.append(("uTs", nm, "padr"))
                    P.dma("sp", uw[:], sc["uT"][:, :, t0:t0 + N + 32].rearrange("c p t -> p c t"), reads=rd, writes=["uw"])
                    for c in range(2):
                        for k in range(31):
                            mm(pb[2][:, 0:N], diag[:, c, k, :], uw[:, c, 1 + k:1 + k + N], k == 0, k == 30, [("diag", c), "uw"], ["pb2"])
                        act(cf[:, c, :], pb[2][:, 0:N], AF.Identity, ["pb2", "cp"], ["cf"], bias=cp[:, c, 0:1])
                        act(sq[:, c, :], pb[2][:, 0:N], AF.Square, ["pb2", "cp"], ["sq"], bias=cp[:, c, 0:1])
                    for c in range(2):
                        mm(pb[3][:, 0:N], ones256[:], cf[:, c, :], c == 0, c == 1, ["ones256", "cf"], ["pb3"])
                    for c in range(2):
                        mm(pb[3][:, N:2 * N], ones256[:], sq[:, c, :], c == 0, c == 1, ["ones256", "sq"], ["pb3"])
                    act(m2[:], pb[3][:, 0:N], AF.Square, ["pb3"], ["m2"])
                    tt_op("dve", crs[:], pb[3][:, N:2 * N], m2[:], ALU.subtract, ["pb3", "m2"], ["crs"])
                    act(crs[:], crs[:], AF.Ln, ["crs"], ["crs"], bias=EPS)
                    act(crs[:], crs[:], AF.Exp, ["crs"], ["crs"], scale=-0.5)
                    for c in range(2):
                        tt_op("dve", tt[c][:], cf[:, c, :], pb[3][:, 0:N], ALU.subtract, ["cf", "pb3"], [f"tt{c}"])
                        tt_op("pool", tt[c][:], tt[c][:], crs[:], ALU.mult, [f"tt{c}", "crs"], [f"tt{c}"])
                        act(tt[c][:], tt[c][:], AF.Silu, [f"tt{c}", "cp"], [f"tt{c}"], bias=cp[:, c, 2:3], scale=cp[:, c, 1:2])
                        tt_op("pool", catT[:, c, :], tt[c][:], sga[:, c, :], ALU.mult, [f"tt{c}", "sga"], ["catT"])
                    if nm == "l":
                        b_lo = max(0, st * NBK - 1)
                        b_hi = min(nblk, st * NBK + NBK + 1)
                        w0 = st * NBK - 1
                        rdk = [("kTs", nm, s2) for s2 in range(max(0, st - 1), min(T // N, st + 2))]
                        rdv = [("vaugs", nm, s2) for s2 in range(max(0, st - 1), min(T // N, st + 2))]
                        wlo = b_lo - w0
                        whi = b_hi - w0
                        P.dma("sp", kzw[0][0:64, wlo:whi, :], sc["kT"][0:64, b_lo * 128:b_hi * 128].rearrange("p (b t) -> p b t", t=128),
                              reads=rdk, writes=["kzw0"])
                        P.dma("sp", kzw[1][64:128, wlo:whi, :], sc["kT"][64:128, b_lo * 128:b_hi * 128].rearrange("p (b t) -> p b t", t=128),
                              reads=rdk, writes=["kzw1"])
                        P.dma("sp", vaugw[:, wlo:whi, :], sc["vaug"][b_lo:b_hi].rearrange("b p c -> p b c"), reads=rdv, writes=["vaugw"])
                    ei = 0
                    for qb in range(NBK):
                        n = st * NBK + qb
                        qsl = slice(qb * 128, (qb + 1) * 128)
                        for kv in range(2):
                            keys = [("c", 0, None), ("c", 1, None)]
                            if nm == "l":
                                if n > 0:
                                    keys.append(("w", n - 1 - w0, maskP))
                                keys.append(("w", n - w0, None))
                                if n < nblk - 1:
                                    keys.append(("w", n + 1 - w0, maskN))
                            pvb = 6 + kv
                            for ki, (kind, bi, msk) in enumerate(keys):
                                if kind == "c":
                                    klhs = kzc[kv][:, bi * 128:(bi + 1) * 128]; kkey = f"kzc{kv}"
                                    vl = vaugc[:, bi, kv * 128:(kv + 1) * 128]; vkey = "vaugc"
                                else:
                                    klhs = kzw[kv][:, bi, :]; kkey = f"kzw{kv}"
                                    vl = vaugw[:, bi, kv * 128:(kv + 1) * 128]; vkey = "vaugw"
                                sb_ = 4 + (ei % 2)
                                et = eT[ei % 3]; ek = f"eT{ei % 3}"
                                ei += 1
                                mm(pb[sb_][:, :], klhs, qT[:, :, qsl], True, True, [kkey, "qT"], [f"pb{sb_}"])
                                act(et[:], pb[sb_][:, :], AF.Exp, [f"pb{sb_}"], [ek], scale=0.125)
                                if msk is not None:
                                    mkey = "maskP" if msk is maskP else "maskN"
                                    tt_op("pool", et[:].rearrange("p (g q) -> p g q", g=4), et[:].rearrange("p (g q) -> p g q", g=4),
                                          bc_mid(msk[:], 4), ALU.mult, [ek, mkey], [ek])
                                mm(pb[pvb][:, :], vl, et[:], ki == 0, ki == len(keys) - 1, [vkey, ek], [f"pb{pvb}"])
                            dlo = kv * 64
                            olo = (1 - kv) * 64
                            ds = bigf[0][dlo:dlo + 64, :].rearrange("p (g q) -> p g q", g=4)
                            rr = bigf[1][dlo:dlo + 64, :].rearrange("p (g q) -> p g q", g=4)
                            tt_op("dve", ds, pb[pvb][dlo:dlo + 64, :].rearrange("p (g q) -> p g q", g=4),
                                  bc_last(esink[dlo:dlo + 64, kv * 4:kv * 4 + 4].rearrange("p (g o) -> p g o", o=1), 128), ALU.add,
                                  [f"pb{pvb}", "esink"], [("bigf0", kv)])
                            act(ds, ds, AF.Ln, [("bigf0", kv)], [("bigf0", kv)])
                            act(rr, ds, AF.Exp, [("bigf0", kv)], [("bigf1", kv)], scale=-1.0)
                            tt_op("pool", rr, rr, sgb[dlo:dlo + 64, :, qsl], ALU.mult, [("bigf1", kv), "sgb"], [("bigf1", kv)])
                            tt_op("dve", catT[dlo:dlo + 64, 2:6, qsl], pb[pvb][olo:olo + 64, :].rearrange("p (g q) -> p g q", g=4), rr,
                                  ALU.mult, [f"pb{pvb}", ("bigf1", kv)], ["catT"])
                    for i4 in range(4):
                        P.dma("sp", qkl[i4][:], sc["qk"][i4, :, t0:t0 + N], reads=[("qks", nm, st, i4)], writes=[f"qkl{i4}"])
                    P.dma("sp", vgl[:], sc["vg"][st * NBK:(st + 1) * NBK].rearrange("b p c -> p b c"), reads=[("vgs", nm, st)], writes=["vgl"])
                    for j in range(NBK):
                        blk = st * NBK + j
                        sl = slice(j * 128, (j + 1) * 128)
                        for d_ in range(2):
                            for h in range(4):
                                cp_op("pool", bd[d_][32 * h:32 * h + 32, h, :], qkl[d_][32 * h:32 * h + 32, sl], [f"qkl{d_}"], [f"bd{d_}"])
                                cp_op("pool", sbd[d_][32 * h:32 * h + 32, h * 64:(h + 1) * 64], Sin[nm][d_][32 * h:32 * h + 32, blk, :],
                                      [("Sin", nm, d_)], [f"sbd{d_}"])
                            mm(pb[d_][:, :], qkl[2 + d_][:, sl], bd[d_][:].rearrange("p h t -> p (h t)"), True, True,
                               [f"qkl{2 + d_}", f"bd{d_}"], [f"pb{d_}"])
                            msk = maskN if d_ == 0 else maskP
                            tt_op("dve", attm[d_][:], pb[d_][:, :].rearrange("p (h t) -> p h t", h=4), bc_mid(msk[:], 4), ALU.mult,
                                  [f"pb{d_}", "maskN", "maskP"], [f"attm{d_}"])
                        for h in range(4):
                            osl = pb[2][0:64, h * 128:(h + 1) * 128]
                            mm(osl, vgl[:, j, h * 64:(h + 1) * 64], attm[0][:, h, :], True, False, ["vgl", "attm0"], ["pb2"])
                            mm(osl, vgl[:, j, h * 64:(h + 1) * 64], attm[1][:, h, :], False, False, ["vgl", "attm1"], ["pb2"])
                            mm(osl, sbd[0][:, h * 64:(h + 1) * 64], qkl[0][:, sl], False, False, ["sbd0", "qkl0"], ["pb2"])
                            mm(osl, sbd[1][:, h * 64:(h + 1) * 64], qkl[1][:, sl], False, True, ["sbd1", "qkl1"], ["pb2"])
                        osq = bigf[2]
                        act(osq[0:64, :], pb[2][0:64, :], AF.Square, ["pb2"], [("bigf2", 0)])
                        mm(pb[3][:, :], ones64[0:64, :], osq[0:64, :], True, True, ["ones64", ("bigf2", 0)], ["pb3"])
                        rsg = bigf[3]
                        act(rsg[:], pb[3][:, :], AF.Ln, ["pb3"], [("bigf3", 0)], bias=EPS)
                        act(rsg[:], rsg[:], AF.Exp, [("bigf3", 0)], [("bigf3", 0)], scale=-0.5)
                        for h in range(4):
                            e_ = h % 2
                            jc = h // 2
                            tt_op("pool", rgc[e_ * 64:e_ * 64 + 64, :], rsg[e_ * 64:e_ * 64 + 64, h * 128:(h + 1) * 128],
                                  sgc[e_ * 64:e_ * 64 + 64, jc, sl], ALU.mult, [("bigf3", 0), "sgc"], [("rgc", e_)])
                            tt_op("dve", catT[e_ * 64:e_ * 64 + 64, 6 + jc, sl], pb[2][0:64, h * 128:(h + 1) * 128],
                                  rgc[e_ * 64:e_ * 64 + 64, :], ALU.mult, ["pb2", ("rgc", e_)], ["catT"])
                    if debug and l == 0 and nm == "l":
                        P.dma("sp", dbg["cat"][:, :, t0:t0 + N].rearrange("k p t -> p k t"), catT[:], reads=["catT"])
                    for j in range(NBK):
                        tok = t0 + j * 128
                        P.dma("sp", xt[:], xsrc[tok:tok + 128, :], reads=([("xo", nm, tok)] if l > 0 else []), writes=["xt"])
                        for hf in range(2):
                            ob = 4 + hf
                            for kc in range(8):
                                mm(pb[ob][:, :], catT[:, kc, j * 128:(j + 1) * 128], wout[:, kc, hf * 512:(hf + 1) * 512], kc == 0, kc == 7,
                                   ["catT", ("wout", kc)], [f"pb{ob}"])
                            yt = bigf[4 + hf]
                            tt_op("dve", yt[:], pb[ob][:, :], gate_bc[sidx][:, hf * 512:(hf + 1) * 512], ALU.mult,
                                  [f"pb{ob}", f"gate_bc{sidx}"], [(f"bigf{4 + hf}", 0), (f"bigf{4 + hf}", 1)])
                            tt_op("pool", x2[:, hf * 512:(hf + 1) * 512], yt[:], xt[:, hf * 512:(hf + 1) * 512], ALU.add,
                                  [(f"bigf{4 + hf}", 0), (f"bigf{4 + hf}", 1), "xt"], ["x2"])
                        if last and nm == "l":
                            act(xt[:], x2[:], AF.Square, ["x2"], ["xt"])
                            P.op("dve", lambda e: e.tensor_reduce(out=smallf[:, 0:1], in_=xt[:], axis=mybir.AxisListType.X, op=ALU.add), ["xt"], ["ss"])
                            ts_op("dve", smallf[:, 1:2], smallf[:, 0:1], 1.0 / D, EPS, ALU.mult, ALU.add, ["ss"], ["ss1"])
                            act(smallf[:, 2:3], smallf[:, 1:2], AF.Sqrt, ["ss1"], ["ss2"])
                            P.op("dve", lambda e: e.reciprocal(out=smallf[:, 3:4], in_=smallf[:, 2:3]), ["ss2"], ["rstd"])
                            stt_op(xt[:], x2[:], smallf[:, 3:4], gate_bc[1][:], ALU.mult, ALU.mult, ["x2", "rstd", "gate_bc1"], ["xt"])
                            P.dma("sp", xdst[tok:tok + 128, :], xt[:], reads=["xt"], writes=[("xo", nm, tok)])
                        else:
                            P.dma("sp", xdst[tok:tok + 128, :], x2[:], reads=["x2"], writes=[("xo", nm, tok)])
                            if debug and l == 0:
                                P.dma("sp", dbg["x1" if nm == "l" else "xc1"][tok:tok + 128, :], x2[:], reads=["x2"])

            if stg_lim < 2:
                break
            pass_a("c")
            if stg_lim < 3:
                break
            pass_a("l")
            if stg_lim < 4:
                break
            P.dma("sp", kzc[0][0:64, :], scr["c"]["kT"][0:64, :], reads=[("kTs", "c", 0)], writes=["kzc0"])
            P.dma("sp", kzc[1][64:128, :], scr["c"]["kT"][64:128, :], reads=[("kTs", "c", 0)], writes=["kzc1"])
            P.dma("sp", vaugc[:], scr["c"]["vaug"].rearrange("b p c -> p b c"), reads=[("vaugs", "c", 0)], writes=["vaugc"])
            for ci, pieces in enumerate(Bchunks):
                load_chunk_cols(l, pieces, wAB[:, :, ci * 128:(ci + 1) * 128], ("wAB", ci), 128)
            if upd:
                pass_b("c")
            if stg_lim < 5:
                break
            pass_b("l")
            if stg_lim < 6:
                break

        P.analyze()
        sems_eng = {}
        for e in P.ENGS:
            for ep in range(P.n_epochs[e]):
                sems_eng[(e, ep)] = es.enter_context(nc.semaphore(f"s_{e}_{ep}"))
        sems_dma = [es.enter_context(nc.semaphore(f"d_{k}")) for k in range(N_DMA_SEMS)]
        block = es.enter_context(nc.Block())
        P.emit(block, sems_eng, sems_dma)
    return nc


def _consts():
    ident = np.eye(128, dtype=np.float32)
    j = np.arange(128)[:, None]
    i = np.arange(128)[None, :]
    maskP = (j >= i).astype(np.float32)
    maskN = (j <= i).astype(np.float32)
    rst = np.ones((128, N), np.float32)
    rst[:, 0::128] = 0.0
    t = np.arange(S)
    r = (t // 64).astype(np.float32)
    col = (t % 64).astype(np.float32)
    nf = 16
    inv = (10000.0 ** (-np.arange(nf, dtype=np.float32) / nf)).astype(np.float32)
    ang = np.concatenate([r[:, None] * inv, col[:, None] * inv], axis=-1).astype(np.float32)
    ang = np.concatenate([ang, ang], axis=-1)
    cos = np.cos(ang).astype(np.float32).T
    sin = np.sin(ang).astype(np.float32).T
    sgn = np.where(np.arange(64) < 32, -1.0, 1.0).astype(np.float32)[:, None]
    sinS = sin * sgn
    cosT = np.ascontiguousarray(np.concatenate([cos, cos], axis=0))
    sinT = np.ascontiguousarray(np.concatenate([sinS, sinS], axis=0))
    return dict(ident=ident, maskP=maskP, maskN=maskN, rst=rst, cosT=cosT, sinT=sinT)


_NC_CACHE = {}


def _prep_inputs(x, c, ctx, c_ctx, w_mod, b_mod, w_in, conv_w, conv_b, conv_ln_w, conv_ln_b,
                 attn_sink, gla_w_up, gla_b_up, gla_norm_w, w_out, final_norm_w):
    f = lambda a: np.ascontiguousarray(np.asarray(a, dtype=np.float32))
    cons = _consts()
    bmT = f(np.asarray(b_mod).reshape(L, 24, 128).transpose(0, 2, 1))
    cw = f(np.asarray(conv_w).reshape(L, 31, 2, 128).transpose(0, 3, 2, 1))
    cpar = f(np.stack([np.asarray(conv_b), np.asarray(conv_ln_w), np.asarray(conv_ln_b)], axis=-1)
             .reshape(L, 2, 128, 3).transpose(0, 2, 1, 3))
    wup = np.zeros((L, 32, 2, 128), np.float32)
    wup[:, 0:16, 0, :] = np.asarray(gla_w_up)[:, 0]
    wup[:, 16:32, 1, :] = np.asarray(gla_w_up)[:, 1]
    bup = f(np.asarray(gla_b_up).transpose(0, 2, 1))
    gnw = f(np.asarray(gla_norm_w).reshape(L, 2, 128).transpose(0, 2, 1))
    shared = dict(w_mod=f(w_mod), bmT=bmT, w_in=f(w_in), w_out=f(w_out), cw=cw, cp=cpar, sink=f(attn_sink),
                  wup=wup, bup=bup, gnw=gnw, fnw=f(final_norm_w), **cons)
    in_maps = []
    cctx = np.asarray(c_ctx, np.float32).reshape(8, 128).T
    for b in range(8):
        cc = np.stack([np.asarray(c[b], np.float32).reshape(8, 128).T, cctx], axis=-1)
        m = dict(shared)
        m["x"] = f(x[b])
        m["ctx"] = f(ctx[b])
        m["cc"] = f(cc)
        in_maps.append(m)
    return in_maps


def kernel(**inputs):
    in_maps = _prep_inputs(**inputs)
    if "nc" not in _NC_CACHE:
        _NC_CACHE["nc"] = build(False)
    nc = _NC_CACHE["nc"]
    outs = []
    G = CORES_PER_LAUNCH
    for g0 in range(0, 8, G):
        res = run_bass_kernel_spmd(nc, in_maps[g0:g0 + G], core_ids=list(range(G)))
        outs.extend(np.asarray(r["out"], dtype=np.float32) for r in res.results)
    return np.stack(outs, axis=0)
```

```python
import numpy as np
from contextlib import ExitStack
import concourse.bass as bass
import concourse.mybir as mybir
from concourse.bass_utils import run_bass_kernel_spmd

F32 = mybir.dt.float32
BF16 = mybir.dt.bfloat16
ALU = mybir.AluOpType
AF = mybir.ActivationFunctionType

N_DMA_SEMS = 2
EPOCH = 30000

D = 1024
S = 4096
CT = 256
N = 256
NBK = N // 128
L = 2
NIN = 2848
EPS = 1e-6
CORES_PER_LAUNCH = 8


class Prog:
    ENGS = ("pe", "act", "dve", "pool", "sp")

    def __init__(self, nc):
        self.nc = nc
        self.ops = []
        self.ndma = 0

    def op(self, eng, fn, reads=(), writes=()):
        self.ops.append(dict(eng=eng, fn=fn, reads=tuple(reads), writes=tuple(writes), dma=False))

    def dma(self, eng, out, in_, reads=(), writes=()):
        self.ops.append(dict(eng=eng, fn=lambda e: e.dma_start(out=out, in_=in_),
                             reads=tuple(reads), writes=tuple(writes), dma=True, didx=self.ndma))
        self.ndma += 1

    def analyze(self):
        last_w = {}
        readers = {}
        for o in self.ops:
            ex = tuple(r for r in o["reads"] if isinstance(r, str) and r.startswith("pb") and r not in o["writes"])
            if ex:
                o["writes"] = o["writes"] + ex
        for i, o in enumerate(self.ops):
            deps = set()
            for r in o["reads"]:
                if r in last_w:
                    deps.add(last_w[r])
            for w in o["writes"]:
                if w in last_w:
                    deps.add(last_w[w])
                for rd in readers.get(w, ()):
                    deps.add(rd)
            deps.discard(i)
            fd = set()
            for d in deps:
                p = self.ops[d]
                if (not p["dma"]) and (not o["dma"]) and p["eng"] == o["eng"]:
                    if o["eng"] == "pe":
                        continue
                fd.add(d)
            o["deps"] = fd
            for r in o["reads"]:
                readers.setdefault(r, []).append(i)
            for w in o["writes"]:
                last_w[w] = i
                readers[w] = []
        for o in self.ops:
            o["need_inc"] = False
        for o in self.ops:
            for d in o["deps"]:
                self.ops[d]["need_inc"] = True
        cnt = {e: 0 for e in self.ENGS}
        for o in self.ops:
            if o["dma"]:
                k = o["didx"]
                o["sem"] = ("dma", k % N_DMA_SEMS, 0)
                o["val"] = 16 * (k // N_DMA_SEMS + 1)
            elif o["need_inc"]:
                cnt[o["eng"]] += 1
                ep = (cnt[o["eng"]] - 1) // EPOCH
                o["sem"] = ("eng", o["eng"], ep)
                o["val"] = cnt[o["eng"]] - ep * EPOCH
        self.n_epochs = {e: (cnt[e] + EPOCH - 1) // EPOCH for e in self.ENGS}

    def emit(self, block, sems_eng, sems_dma):
        dma_ops = [o for o in self.ops if o["dma"]]
        by_eng = {e: [] for e in self.ENGS}
        for i, o in enumerate(self.ops):
            by_eng[o["eng"]].append(i)

        def semh(s):
            return sems_eng[(s[1], s[2])] if s[0] == "eng" else sems_dma[s[1]]

        def run(e, handle):
            waited = {}
            for i in by_eng[e]:
                o = self.ops[i]
                need = {}
                for d in o["deps"]:
                    p = self.ops[d]
                    s = p["sem"]
                    need[s] = max(need.get(s, 0), p["val"])
                if o["dma"]:
                    k = o["didx"]
                    if k >= N_DMA_SEMS:
                        s = ("dma", k % N_DMA_SEMS, 0)
                        need[s] = max(need.get(s, 0), 16 * (k // N_DMA_SEMS))
                for s, v in need.items():
                    if waited.get(s, 0) >= v:
                        continue
                    handle.wait_ge(semh(s), v)
                    waited[s] = v
                ins = o["fn"](handle)
                if o["dma"]:
                    ins.then_inc(semh(o["sem"]), 16)
                elif o["need_inc"]:
                    ins.then_inc(semh(o["sem"]), 1)
            if e == "sp":
                final = {}
                for o in dma_ops:
                    final[o["sem"]] = max(final.get(o["sem"], 0), o["val"])
                for s, v in final.items():
                    handle.wait_ge(semh(s), v)

        @block.tensor
        def _(h):
            run("pe", h)

        @block.scalar
        def _(h):
            run("act", h)

        @block.vector
        def _(h):
            run("dve", h)

        @block.gpsimd
        def _(h):
            run("pool", h)

        @block.sync
        def _(h):
            run("sp", h)


def _chunks():
    A = [[(0, 128)], [(128, 128)], [(256, 128)], [(384, 128)], [(1280, 128)],
         [(1280 + 32, 32), (1280, 32), (1280 + 96, 32), (1280 + 64, 32)],
         [(2176, 128)], [(2048, 128)], [(2560, 32)]]
    At = [(1408, 128), (2304, 256)]
    B = [[(512, 128)], [(640, 128)]]
    for g in range(4):
        B.append([(768 + g * 64, 64), (768 + (4 + g) * 64, 64)])
    for g in range(4):
        a = 768 + g * 64
        b = 768 + (4 + g) * 64
        B.append([(a + 32, 32), (a, 32), (b + 32, 32), (b, 32)])
    for g in range(4):
        B.append([(1536 + g * 64, 64), (1536 + (4 + g) * 64, 64)])
    B.append([(2592, 128)])
    B.append([(2720, 128)])
    return A, At, B


def bc_mid(ap, n):
    a = [list(x) for x in ap.ap]
    return bass.AP(ap.tensor, ap.offset, [a[0], [0, n]] + a[1:])


def bc_last(ap, n):
    a = [list(x) for x in ap.ap]
    a[-1] = [0, n]
    return bass.AP(ap.tensor, ap.offset, a)


def build(debug=False, stg_lim=99, layers=(0, 1)):
    nc = bass.Bass("TRN2", target_bir_lowering=False)
    es = ExitStack()
    P = Prog(nc)

    def din(name, shape, dt=F32):
        return nc.dram_tensor(name, list(shape), dt, kind="ExternalInput").ap()

    def dscr(name, shape, dt):
        return nc.dram_tensor(name, list(shape), dt, kind="Internal").ap()

    x_d = din("x", [S, D])
    ctx_d = din("ctx", [CT, D])
    cc_d = din("cc", [128, 8, 2])
    wmod_d = din("w_mod", [L, D, 3 * D])
    bmT_d = din("bmT", [L, 128, 24])
    win_d = din("w_in", [L, D, NIN])
    wout_d = din("w_out", [L, D, D])
    cw_d = din("cw", [L, 128, 2, 31])
    cp_d = din("cp", [L, 128, 2, 3])
    sink_d = din("sink", [L, 8])
    wup_d = din("wup", [L, 32, 2, 128])
    bup_d = din("bup", [L, 128, 2])
    gnw_d = din("gnw", [L, 128, 2])
    fnw_d = din("fnw", [D])
    ident_d = din("ident", [128, 128])
    maskP_d = din("maskP", [128, 128])
    maskN_d = din("maskN", [128, 128])
    rst_d = din("rst", [128, N])
    cos_d = din("cosT", [128, S])
    sin_d = din("sinT", [128, S])
    out_d = nc.dram_tensor("out", [S, D], F32, kind="ExternalOutput").ap()

    scr = {}
    for nm, T in (("l", S), ("c", CT)):
        scr[nm] = dict(
            T=T,
            hT=dscr(f"hT_{nm}", [8, 128, T], BF16),
            uT=dscr(f"uT_{nm}", [2, 128, T + 32], BF16),
            kT=dscr(f"kT_{nm}", [128, T], BF16),
            vaug=dscr(f"vaug_{nm}", [T // 128, 128, 256], BF16),
            vg=dscr(f"vg_{nm}", [T // 128, 128, 256], BF16),
            qk=dscr(f"qk_{nm}", [4, 128, T], BF16),
        )
    x1_d = dscr("x1", [S, D], F32)
    xc1_d = dscr("xc1", [CT, D], F32)

    dbg = {}
    if debug:
        dbg["hT"] = nc.dram_tensor("dbg_hT", [8, 128, S], BF16, kind="ExternalOutput").ap()
        dbg["cat"] = nc.dram_tensor("dbg_cat", [8, 128, S], BF16, kind="ExternalOutput").ap()
        dbg["x1"] = nc.dram_tensor("dbg_x1", [S, D], F32, kind="ExternalOutput").ap()
        dbg["xc1"] = nc.dram_tensor("dbg_xc1", [CT, D], F32, kind="ExternalOutput").ap()
        dbg["mod"] = nc.dram_tensor("dbg_mod", [128, 48], F32, kind="ExternalOutput").ap()

    with es:
        def sb(name, shape, dt=F32):
            return es.enter_context(nc.sbuf_tensor("s_" + name, list(shape), dt))

        pb = [es.enter_context(nc.psum_tensor(f"pb{i}", [128, 512], F32)) for i in range(8)]

        identf = sb("identf", [128, 128]); identb = sb("identb", [128, 128], BF16)
        maskP = sb("maskP", [128, 128], BF16); maskN = sb("maskN", [128, 128], BF16)
        rst = sb("rst", [128, N])
        ones256 = sb("ones256", [128, 128]); ones64 = sb("ones64", [128, 128]); onesf = sb("onesf", [128, 128])
        stage = [sb("stage0", [128, 8, 128]), sb("stage1", [128, 8, 128])]
        scc = sb("scc", [128, 8, 2]); bmT = sb("bmT", [128, 24]); modT = sb("modT", [128, 24, 2]); one1p = sb("one1p", [128, 8, 2])
        gate_bc = [sb("gate_bc0", [128, D]), sb("gate_bc1", [128, D])]
        cw = sb("cw", [128, 2, 31]); cp = sb("cp", [128, 2, 3]); esink = sb("esink", [128, 8])
        wupf = sb("wupf", [32, 2, 128]); wupb = sb("wupb", [32, 2, 128], BF16)
        nbup = sb("nbup", [128, 2]); gnw = sb("gnw", [128, 2])
        dg = sb("dg", [128, 128])
        wAB = sb("wAB", [128, 8, 2048], BF16)
        wout = sb("wout", [128, 8, D], BF16)
        diag = sb("diag", [128, 2, 31, 128], BF16)
        Sin = {"l": [sb("SinLf", [128, 32, 64], BF16), sb("SinLb", [128, 32, 64], BF16)],
               "c": [sb("SinCf", [128, 2, 64], BF16), sb("SinCb", [128, 2, 64], BF16)]}
        stt = {"l": [sb("stLf", [128, 32, 64]), sb("stLb", [128, 32, 64])],
               "c": [sb("stCf", [128, 2, 64]), sb("stCb", [128, 2, 64])]}
        dec = {"l": [sb("decLf", [128, 32]), sb("decLb", [128, 32])],
               "c": [sb("decCf", [128, 2]), sb("decCb", [128, 2])]}
        Spp = [sb("Spp0", [128, 64]), sb("Spp1", [128, 64])]
        Sfin = [sb("Sfinf", [128, 64]), sb("Sfinb", [128, 64])]
        kzc = [sb("kzc0", [128, CT], BF16), sb("kzc1", [128, CT], BF16)]
        vaugc = sb("vaugc", [128, 2, 256], BF16)
        hTs2 = [sb("hT0", [128, 8, N], BF16), sb("hT1", [128, 8, N], BF16)]
        xts = [sb("xt0", [128, D]), sb("xt1", [128, D])]; x2 = sb("x2", [128, D])
        smallf = sb("smallf", [128, 8])
        cs2 = [sb("cs0", [128, 2, N]), sb("cs1", [128, 2, N])]; t1 = sb("t1", [128, N]); t2 = sb("t2", [128, N])
        bigf = [sb(f"bigf{i}", [128, 512]) for i in range(6)]
        sig = sb("sig", [128, N]); uTt = sb("uTt", [128, 2, N], BF16); kTt = sb("kTt", [128, N], BF16)
        qin_t = [sb("qin_f", [128, N], BF16), sb("qin_b", [128, N], BF16)]
        kin_t = [sb("kin_f", [128, N], BF16), sb("kin_b", [128, N], BF16)]
        kdtok = [sb("kdtok_f", [128, NBK, 128], BF16), sb("kdtok_b", [128, NBK, 128], BF16)]
        vaugt = sb("vaugt", [128, NBK, 256], BF16); vgt = sb("vgt", [128, NBK, 256], BF16)
        lrT = sb("lrT", [32, N], BF16)
        zpad = sb("zpad", [128, 2, 16], BF16)
        sga = sb("sga", [128, 2, N]); qT = sb("qT", [128, 4, N], BF16); sgb = sb("sgb", [128, 4, N], BF16); sgc = sb("sgc", [128, 2, N])
        uw2 = [sb("uw0", [128, 2, N + 32], BF16), sb("uw1", [128, 2, N + 32], BF16)]
        cf = sb("cf", [128, 2, N]); sq = sb("sq", [128, 2, N]); m2 = sb("m2", [128, N]); crs = sb("crs", [128, N])
        tt = [sb("tt0", [128, N]), sb("tt1", [128, N])]
        catT = sb("catT", [128, 8, N], BF16)
        kzw2 = [[sb(f"kzw{b_}_{kv_}", [128, 4, 128], BF16) for kv_ in range(2)] for b_ in range(2)]
        vaugw2 = [sb("vaugw0", [128, 4, 256], BF16), sb("vaugw1", [128, 4, 256], BF16)]
        eT = [sb(f"eT{i}", [128, 512], BF16) for i in range(3)]
        qkl2 = [[sb(f"qkl{b_}_{i}", [128, N], BF16) for i in range(4)] for b_ in range(2)]
        vgl2 = [sb("vgl0", [128, NBK, 256], BF16), sb("vgl1", [128, NBK, 256], BF16)]
        bd = [sb("bd_f", [128, 4, 128], BF16), sb("bd_b", [128, 4, 128], BF16)]
        attm = [sb("attm_f", [128, 4, 128], BF16), sb("attm_b", [128, 4, 128], BF16)]
        sbd = [sb("sbd_f", [128, 256], BF16), sb("sbd_b", [128, 256], BF16)]
        rgc = sb("rgc", [128, 128])

        q_rr = {"n": 0}

        def dq():
            q_rr["n"] += 1
            return "sp"

        def act(out, in_, func, reads, writes, bias=None, scale=None, accum=None):
            kw = {}
            if bias is not None:
                kw["bias"] = bias
            if scale is not None:
                kw["scale"] = scale
            if accum is not None:
                kw["accum_out"] = accum
            P.op("act", lambda e: e.activation(out=out, in_=in_, func=func, **kw), reads, writes)

        def tt_op(eng, out, in0, in1, op, reads, writes):
            P.op(eng, lambda e: e.tensor_tensor(out=out, in0=in0, in1=in1, op=op), reads, writes)

        def ts_op(eng, out, in0, s1, s2, op0, op1, reads, writes):
            if op1 is None and eng == "pool" and op0 == ALU.mult:
                P.op(eng, lambda e: e.tensor_scalar(out=out, in0=in0, scalar1=s1, scalar2=1.0, op0=ALU.mult, op1=ALU.mult), reads, writes)
            elif op1 is None:
                P.op(eng, lambda e: e.tensor_scalar(out=out, in0=in0, scalar1=s1, scalar2=None, op0=op0), reads, writes)
            else:
                P.op(eng, lambda e: e.tensor_scalar(out=out, in0=in0, scalar1=s1, scalar2=s2, op0=op0, op1=op1), reads, writes)

        def stt_op(out, in0, scalar, in1, op0, op1, reads, writes):
            P.op("dve", lambda e: e.scalar_tensor_tensor(out=out, in0=in0, scalar=scalar, in1=in1, op0=op0, op1=op1), reads, writes)

        def cp_op(eng, out, in_, reads, writes):
            if eng == "act":
                act(out, in_, AF.Copy, reads, writes)
            else:
                P.op(eng, lambda e: e.tensor_copy(out=out, in_=in_), reads, writes)

        def mm(out, lhsT, rhs, start, stop, reads, writes):
            P.op("pe", lambda e: e.matmul(out, lhsT=lhsT, rhs=rhs, start=start, stop=stop), reads, writes)

        def memset(eng, ap, v, writes):
            P.op(eng, lambda e: e.memset(ap, v), (), writes)

        P.dma("sp", identf[:], ident_d, writes=["identf"])
        cp_op("dve", identb[:], identf[:], ["identf"], ["identb"])
        P.dma("sp", stage[0][:, 0, :], maskP_d, writes=["stage0"])
        cp_op("dve", maskP[:], stage[0][:, 0, :], ["stage0"], ["maskP"])
        P.dma("sp", stage[1][:, 0, :], maskN_d, writes=["stage1"])
        cp_op("dve", maskN[:], stage[1][:, 0, :], ["stage1"], ["maskN"])
        P.dma("sp", rst[:], rst_d, writes=["rst"])
        memset("pool", ones256[:], 1.0 / 256.0, ["ones256"])
        memset("pool", ones64[:], 1.0 / 64.0, ["ones64"])
        memset("pool", onesf[:], 1.0, ["onesf"])
        memset("pool", zpad[:], 0.0, ["zpad"])
        for d_ in range(2):
            memset("pool", bd[d_][:], 0.0, [f"bd{d_}"])
            memset("pool", sbd[d_][:], 0.0, [f"sbd{d_}"])
            for b_ in range(2):
                memset("pool", kzw2[b_][d_][:], 0.0, [f"kzw{b_}_{d_}"])
            memset("pool", kzc[d_][:], 0.0, [f"kzc{d_}"])
        memset("pool", vaugt[:, :, 0:64], 1.0, ["vaugt"])
        memset("pool", vaugt[:, :, 192:256], 1.0, ["vaugt"])
        for nm in ("l", "c"):
            T = scr[nm]["T"]
            P.dma("sp", scr[nm]["uT"][:, :, 0:16].rearrange("c p t -> p c t"), zpad[:], reads=["zpad"], writes=[("uTs", nm, "padl")])
            P.dma("sp", scr[nm]["uT"][:, :, 16 + T:32 + T].rearrange("c p t -> p c t"), zpad[:], reads=["zpad"], writes=[("uTs", nm, "padr")])

        Achunks, Atok, Bchunks = _chunks()
        stg_rr = {"n": 0}
        cast_rr = {"n": 0}

        def load_chunk_cols(l, pieces, dst_ap, dst_key, width):
            i = stg_rr["n"] % 2
            stg_rr["n"] += 1
            off = 0
            for (cs, w) in pieces:
                P.dma(dq(), stage[i][:, :, off:off + w], win_d[l, :, cs:cs + w].rearrange("(k p) n -> p k n", p=128),
                      writes=[f"stage{i}"])
                off += w
            eng = ("pool", "dve", "act")[cast_rr["n"] % 3]
            cast_rr["n"] += 1
            cp_op(eng, dst_ap, stage[i][:, :, 0:width], [f"stage{i}"], [dst_key])

        def seq_info(nm, l):
            T = scr[nm]["T"]
            if nm == "l":
                xsrc = x_d if (l == 0 or 0 not in layers) else x1_d
                xdst = x1_d if l == 0 else out_d
            else:
                xsrc = ctx_d if (l == 0 or 0 not in layers) else xc1_d
                xdst = xc1_d
            return T, xsrc, xdst

        for l in layers:
            upd = (l == 0)
            if l == layers[0]:
                P.dma("sp", scc[:], cc_d, writes=["scc"])
                act(scc[:], scc[:], AF.Silu, ["scc"], ["scc"])
            P.dma("sp", bmT[:], bmT_d[l], writes=["bmT"])
            P.dma("sp", cw[:], cw_d[l], writes=["cw"])
            P.dma("sp", cp[:], cp_d[l], writes=["cp"])
            P.dma("sp", esink[:], bass.AP(sink_d.tensor, sink_d[l].offset, [[0, 128], [1, 8]]), writes=["esink"])
            act(esink[:], esink[:], AF.Exp, ["esink"], ["esink"])
            P.dma("sp", wupf[:], wup_d[l], writes=["wupf"])
            cp_op("dve", wupb[:], wupf[:], ["wupf"], ["wupb"])
            P.dma("sp", nbup[:], bup_d[l], writes=["nbup"])
            ts_op("dve", nbup[:], nbup[:], -1.0, None, ALU.mult, None, ["nbup"], ["nbup"])
            P.dma("sp", gnw[:], gnw_d[l], writes=["gnw"])
            modps = pb[6]
            for j in range(24):
                i = stg_rr["n"] % 2
                stg_rr["n"] += 1
                P.dma(dq(), stage[i][:], wmod_d[l, :, j * 128:(j + 1) * 128].rearrange("(k p) n -> p k n", p=128),
                      writes=[f"stage{i}"])
                for k in range(8):
                    mm(modps[:, 2 * j:2 * j + 2], stage[i][:, k, :], scc[:, k, :], k == 0, k == 7,
                       [f"stage{i}", "scc"], ["pb6"])
            tt_op("dve", modT[:], modps[:, 0:48].rearrange("p (j s) -> p j s", s=2), bc_last(bmT[:].rearrange("p (j o) -> p j o", o=1), 2),
                  ALU.add, ["pb6", "bmT"], ["modT"])
            ts_op("dve", one1p[:], modT[:, 8:16, :], 1.0, None, ALU.add, None, ["modT"], ["one1p"])
            if debug and l == 0:
                P.dma("sp", dbg["mod"], modT[:].rearrange("p j s -> p (j s)"), reads=["modT"])
            for s_ in range(2):
                if s_ == 1 and not upd:
                    continue
                for k in range(8):
                    ts_op("dve", dg[:], identf[:], modT[:, 16 + k, s_:s_ + 1], None, ALU.mult, None, ["identf", "modT"], ["dg"])
                    bank = pb[4 + (k // 4)]
                    mm(bank[:, (k % 4) * 128:(k % 4 + 1) * 128], onesf[:], dg[:], True, True, ["onesf", "dg"], [f"pb{4 + k // 4}"])
                cp_op("act", gate_bc[s_][:, 0:512], pb[4][:], ["pb4"], [f"gate_bc{s_}"])
                cp_op("act", gate_bc[s_][:, 512:1024], pb[5][:], ["pb5"], [f"gate_bc{s_}"])
            if l == L - 1:
                P.dma("sp", gate_bc[1][:], bass.AP(fnw_d.tensor, 0, [[0, 128], [1, D]]), writes=["gate_bc1"])
            if stg_lim < 1:
                break
            for ci, pieces in enumerate(Achunks):
                wdt = sum(w for _, w in pieces)
                slot = 3 + ci
                load_chunk_cols(l, pieces, wAB[:, :, slot * 128:slot * 128 + wdt], ("wAB", slot), wdt)
            off = 0
            for (cs, w) in Atok:
                for sub in range(w // 128):
                    slot = (off // 128)
                    load_chunk_cols(l, [(cs + sub * 128, 128)], wAB[:, :, slot * 128:(slot + 1) * 128], ("wAB", slot), 128)
                    off += 128
            for kc in range(8):
                i = stg_rr["n"] % 2
                stg_rr["n"] += 1
                st2 = stage[i][:].rearrange("p k n -> p (k n)")
                if kc < 2:
                    P.dma(dq(), st2, wout_d[l, kc * 128:(kc + 1) * 128, :], writes=[f"stage{i}"])
                elif kc < 6:
                    g = kc - 2
                    P.dma(dq(), st2[0:64, :], wout_d[l, 256 + g * 64:256 + (g + 1) * 64, :], writes=[f"stage{i}"])
                    P.dma(dq(), st2[64:128, :], wout_d[l, 256 + (4 + g) * 64:256 + (5 + g) * 64, :], writes=[f"stage{i}"])
                else:
                    P.dma(dq(), st2, wout_d[l, 768 + (kc - 6) * 128:768 + (kc - 5) * 128, :], writes=[f"stage{i}"])
                if kc < 6:
                    cp_op(("pool", "dve")[kc % 2], wout[:, kc, :], st2, [f"stage{i}"], [("wout", kc)])
                else:
                    ts_op("dve", wout[:, kc, :], st2, gnw[:, kc - 6:kc - 5], None, ALU.mult, None, [f"stage{i}", "gnw"], [("wout", kc)])
            for c in range(2):
                for k in range(31):
                    ts_op(("pool", "dve")[k % 2], diag[:, c, k, :], identb[:], cw[:, c, k:k + 1], None, ALU.mult, None,
                          ["identb", "cw"], [("diag", c)])

            def pass_a(nm):
                T, xsrc, _ = seq_info(nm, l)
                sidx = 0 if nm == "l" else 1
                sc = scr[nm]
                rope = (nm == "l")
                def a_loads(st):
                    for j in range(NBK):
                        tok = st * N + j * 128
                        P.dma("sp", xts[j][:], xsrc[tok:tok + 128, :], reads=([("xo", nm, tok)] if (l > 0 and 0 in layers) else []), writes=[f"xt{j}"])

                a_loads(0)
                hT = hTs2[0]
                cosb = cs2[0][:, 0, :]; sinb = cs2[0][:, 1, :]
                for st in range(T // N):
                    t0 = st * N
                    for j in range(NBK):
                        tok = t0 + j * 128
                        xt = xts[j]; Kxt = f"xt{j}"
                        act(x2[:], xt[:], AF.Square, [Kxt], ["x2"])
                        P.op("dve", lambda e: e.tensor_reduce(out=smallf[:, 0:1], in_=x2[:], axis=mybir.AxisListType.X, op=ALU.add), ["x2"], ["ss"])
                        ts_op("dve", smallf[:, 1:2], smallf[:, 0:1], 1.0 / D, EPS, ALU.mult, ALU.add, ["ss"], ["ss1"])
                        act(smallf[:, 2:3], smallf[:, 1:2], AF.Sqrt, ["ss1"], ["ss2"])
                        P.op("dve", lambda e: e.reciprocal(out=smallf[:, 3:4], in_=smallf[:, 2:3]), ["ss2"], ["rstd"])
                        act(x2[:], xt[:], AF.Copy, [Kxt, "rstd"], ["x2"], scale=smallf[:, 3:4])
                        for k in range(8):
                            bank = 2 + k // 4
                            P.op("pe", lambda e, k=k, bank=bank: e.transpose(out=pb[bank][:, (k % 4) * 128:(k % 4 + 1) * 128],
                                                                              in_=x2[:, k * 128:(k + 1) * 128], identity=identf[:]),
                                 ["x2", "identf"], [f"pb{bank}"])
                        for k in range(8):
                            bank = 2 + k // 4
                            src = pb[bank][:, (k % 4) * 128:(k % 4 + 1) * 128]
                            dst = hT[:, k, j * 128:(j + 1) * 128]
                            if k < 4:
                                ts_op("dve", dst, src, one1p[:, k, sidx:sidx + 1], modT[:, k, sidx:sidx + 1], ALU.mult, ALU.add,
                                      [f"pb{bank}", "one1p", "modT"], ["hT0"])
                            else:
                                act(dst, src, AF.Identity, [f"pb{bank}", "one1p", "modT"], ["hT0"],
                                    bias=modT[:, k, sidx:sidx + 1], scale=one1p[:, k, sidx:sidx + 1])
                    if st + 1 < T // N:
                        a_loads(st + 1)
                    P.dma("sp", sc["hT"][:, :, t0:t0 + N].rearrange("k p t -> p k t"), hT[:], reads=["hT0"], writes=[("hTs", nm, st)])
                    if debug and l == 0 and nm == "l":
                        P.dma("sp", dbg["hT"][:, :, t0:t0 + N].rearrange("k p t -> p k t"), hT[:], reads=["hT0"])
                    if rope:
                        P.dma("sp", cosb, cos_d[:, t0:t0 + N], writes=["cosb0"])
                        P.dma("sp", sinb, sin_d[:, t0:t0 + N], writes=["sinb0"])

                    def projA(ci, bank, M=128):
                        slot = 3 + ci
                        for k in range(8):
                            mm(pb[bank][0:M, 0:N], wAB[:, k, slot * 128:slot * 128 + M], hT[:, k, :], k == 0, k == 7,
                               [("wAB", slot), "hT0"], [f"pb{bank}"])

                    for c in range(2):
                        projA(2 + c, 0)
                        act(sig[:], pb[0][:, 0:N], AF.Sigmoid, ["pb0"], ["sig"])
                        projA(c, 1)
                        tt_op("dve", uTt[:, c, :], pb[1][:, 0:N], sig[:], ALU.mult, ["pb1", "sig"], ["uTt"])
                    P.dma("sp", sc["uT"][:, :, 16 + t0:16 + t0 + N].rearrange("c p t -> p c t"), uTt[:], reads=["uTt"],
                          writes=[("uTs", nm, st)])
                    projA(4, 0)
                    if rope:
                        projA(5, 1)
                        tt_op("dve", t1[:], pb[0][:, 0:N], cosb, ALU.mult, ["pb0", "cosb0"], ["t1"])
                        tt_op("dve", t2[:], pb[1][:, 0:N], sinb, ALU.mult, ["pb1", "sinb0"], ["t2"])
                        tt_op("pool", kTt[:], t1[:], t2[:], ALU.add, ["t1", "t2"], ["kTt"])
                    else:
                        cp_op("act", kTt[:], pb[0][:, 0:N], ["pb0"], ["kTt"])
                    P.dma("sp", sc["kT"][:, t0:t0 + N], kTt[:], reads=["kTt"], writes=[("kTs", nm, st)])
                    projA(8, 0, M=32)
                    cp_op("act", lrT[:], pb[0][0:32, 0:N], ["pb0"], ["lrT"])
                    for d_ in range(2):
                        T2 = bigf[0][:, d_ * N:(d_ + 1) * N]; G = bigf[1][:, d_ * N:(d_ + 1) * N]
                        eM = bigf[2][:, d_ * N:(d_ + 1) * N]; eP = bigf[3][:, d_ * N:(d_ + 1) * N]
                        k32 = bigf[4][:, d_ * N:(d_ + 1) * N]; kd32 = bigf[5][:, d_ * N:(d_ + 1) * N]
                        kT2, kG, keM, keP, kk32, kkd = [(f"bigf{i}", d_) for i in range(6)]
                        zb = 1
                        mm(pb[zb][:, 0:N], wupb[:, d_, :], lrT[:], True, True, ["wupb", "lrT"], [f"pb{zb}"])
                        act(T2, pb[zb][:, 0:N], AF.Exp, [f"pb{zb}", "nbup"], [kT2], bias=nbup[:, d_:d_ + 1], scale=-1.0)
                        act(T2, T2, AF.Ln, [kT2], [kT2], bias=1.0)
                        P.op("dve", lambda e, T2=T2, G=G: e.tensor_tensor_scan(out=G, data0=rst[:], data1=T2, initial=0.0,
                                                                                 op0=ALU.mult, op1=ALU.add),
                             ["rst", kT2], [kG])
                        for j in range(NBK):
                            blk = st * NBK + j
                            act(dec[nm][d_][:, blk:blk + 1], G[:, j * 128 + 127:j * 128 + 128], AF.Exp, [kG], [("dec", nm, d_)], scale=-1.0 / 16)
                        if d_ == 1:
                            for j in range(NBK):
                                sl = slice(j * 128, (j + 1) * 128)
                                stt_op(eM[:, sl], T2[:, sl], G[:, j * 128 + 127:j * 128 + 128], G[:, sl], ALU.add, ALU.subtract,
                                       [kT2, kG], [keM])
                            cp_op("pool", G, eM, [keM], [kG])
                        act(eM, G, AF.Exp, [kG], [keM], scale=-1.0 / 16)
                        act(eP, G, AF.Exp, [kG], [keP], scale=1.0 / 16)
                        projA(6, 0)
                        tt_op("dve", k32, pb[0][:, 0:N], eP, ALU.mult, ["pb0", keP], [kk32])
                        cp_op("pool", kin_t[d_][:], k32, [kk32], [f"kin{d_}"])
                        projA(7, 0)
                        stt_op(qin_t[d_][:], pb[0][:, 0:N], 32.0 ** -0.5, eM, ALU.mult, ALU.mult, ["pb0", keM], [f"qin{d_}"])
                        P.dma("sp", sc["qk"][d_, :, t0:t0 + N], qin_t[d_][:], reads=[f"qin{d_}"], writes=[("qks", nm, st, d_)])
                        P.dma("sp", sc["qk"][2 + d_, :, t0:t0 + N], kin_t[d_][:], reads=[f"kin{d_}"], writes=[("qks", nm, st, 2 + d_)])
                        for j in range(NBK):
                            blk = st * NBK + j
                            sl = slice(j * 128, (j + 1) * 128)
                            ts_op("pool", kd32[:, sl], k32[:, sl], dec[nm][d_][:, blk:blk + 1], None, ALU.mult, None,
                                  [kk32, ("dec", nm, d_)], [kkd])
                            P.op("pe", lambda e, j=j, kd32=kd32, sl=sl: e.transpose(out=pb[4][:, j * 128:(j + 1) * 128], in_=kd32[:, sl], identity=identf[:]),
                                 [kkd, "identf"], ["pb4"])
                            cp_op("act", kdtok[d_][:, j, :], pb[4][:, j * 128:(j + 1) * 128], ["pb4"], [f"kdtok{d_}"])
                    for j in range(NBK):
                        blk = st * NBK + j
                        for k in range(8):
                            mm(pb[5][:, 0:384], hT[:, k, j * 128:(j + 1) * 128], wAB[:, k, 0:384], k == 0, k == 7,
                               ["hT0", ("wAB", 0), ("wAB", 1), ("wAB", 2)], ["pb5"])
                        cp_op("act", vaugt[:, j, 64:192], pb[5][:, 0:128], ["pb5"], ["vaugt"])
                        cp_op("act", vgt[:, j, :], pb[5][:, 128:384], ["pb5"], ["vgt"])
                        for d_ in range(2):
                            mm(pb[6 + d_][:, 0:256], kdtok[d_][:, j, :], vgt[:, j, :], True, True, [f"kdtok{d_}", "vgt"], [f"pb{6 + d_}"])
                            for h in range(4):
                                cp_op(("dve", "act")[d_], stt[nm][d_][32 * h:32 * h + 32, blk, :],
                                      pb[6 + d_][32 * h:32 * h + 32, h * 64:(h + 1) * 64], [f"pb{6 + d_}"], [("stt", nm, d_)])
                    P.dma("sp", sc["vaug"][st * NBK:(st + 1) * NBK].rearrange("b p c -> p b c"), vaugt[:], reads=["vaugt"],
                          writes=[("vaugs", nm, st)])
                    P.dma("sp", sc["vg"][st * NBK:(st + 1) * NBK].rearrange("b p c -> p b c"), vgt[:], reads=["vgt"],
                          writes=[("vgs", nm, st)])
                nblk = T // 128
                for d_ in range(2):
                    order = list(range(nblk)) if d_ == 0 else list(range(nblk - 1, -1, -1))
                    cur = 0
                    if nm == "c":
                        memset("dve", Spp[0][:], 0.0, ["Spp0"])
                    else:
                        cp_op("dve", Spp[0][:], Sfin[d_][:], [f"Sfin{d_}"], ["Spp0"])
                    for blk in order:
                        cp_op("pool", Sin[nm][d_][:, blk, :], Spp[cur][:], [f"Spp{cur}"], [("Sin", nm, d_)])
                        stt_op(Spp[1 - cur][:], Spp[cur][:], dec[nm][d_][:, blk:blk + 1], stt[nm][d_][:, blk, :], ALU.mult, ALU.add,
                               [f"Spp{cur}", ("dec", nm, d_), ("stt", nm, d_)], [f"Spp{1 - cur}"])
                        cur = 1 - cur
                    if nm == "c":
                        cp_op("dve", Sfin[d_][:], Spp[cur][:], [f"Spp{cur}"], [f"Sfin{d_}"])

            def pass_b(nm):
                T, xsrc, xdst = seq_info(nm, l)
                sidx = 0 if nm == "l" else 1
                sc = scr[nm]
                rope = (nm == "l")
                nblk = T // 128
                last = (l == L - 1)
                def b_loads(st, bs):
                    t0 = st * N
                    P.dma("sp", hTs2[bs][:], sc["hT"][:, :, t0:t0 + N].rearrange("k p t -> p k t"), reads=[("hTs", nm, st)], writes=[f"hT{bs}"])
                    if rope:
                        P.dma("sp", cs2[bs][:, 0, :], cos_d[:, t0:t0 + N], writes=[f"cosb{bs}"])
                        P.dma("sp", cs2[bs][:, 1, :], sin_d[:, t0:t0 + N], writes=[f"sinb{bs}"])
                    rd = [("uTs", nm, st)]
                    rd.append(("uTs", nm, st - 1) if st > 0 else ("uTs", nm, "padl"))
                    rd.append(("uTs", nm, st + 1) if st < T // N - 1 else ("uTs", nm, "padr"))
                    P.dma("sp", uw2[bs][:], sc["uT"][:, :, t0:t0 + N + 32].rearrange("c p t -> p c t"), reads=rd, writes=[f"uw{bs}"])
                    if nm == "l":
                        b_lo = max(0, st * NBK - 1)
                        b_hi = min(nblk, st * NBK + NBK + 1)
                        w0 = st * NBK - 1
                        rdk = [("kTs", nm, s2) for s2 in range(max(0, st - 1), min(T // N, st + 2))]
                        rdv = [("vaugs", nm, s2) for s2 in range(max(0, st - 1), min(T // N, st + 2))]
                        wlo = b_lo - w0
                        whi = b_hi - w0
                        P.dma("sp", kzw2[bs][0][0:64, wlo:whi, :], sc["kT"][0:64, b_lo * 128:b_hi * 128].rearrange("p (b t) -> p b t", t=128),
                              reads=rdk, writes=[f"kzw{bs}_0"])
                        P.dma("sp", kzw2[bs][1][64:128, wlo:whi, :], sc["kT"][64:128, b_lo * 128:b_hi * 128].rearrange("p (b t) -> p b t", t=128),
                              reads=rdk, writes=[f"kzw{bs}_1"])
                        P.dma("sp", vaugw2[bs][:, wlo:whi, :], sc["vaug"][b_lo:b_hi].rearrange("b p c -> p b c"), reads=rdv, writes=[f"vaugw{bs}"])
                    for i4 in range(4):
                        P.dma("sp", qkl2[bs][i4][:], sc["qk"][i4, :, t0:t0 + N], reads=[("qks", nm, st, i4)], writes=[f"qkl{bs}_{i4}"])
                    P.dma("sp", vgl2[bs][:], sc["vg"][st * NBK:(st + 1) * NBK].rearrange("b p c -> p b c"), reads=[("vgs", nm, st)], writes=[f"vgl{bs}"])

                b_loads(0, 0)
                for st in range(T // N):
                    t0 = st * N
                    bs = st % 2
                    if st + 1 < T // N:
                        b_loads(st + 1, 1 - bs)
                    for j in range(NBK):
                        tok = t0 + j * 128
                        P.dma("sp", xts[j][:], xsrc[tok:tok + 128, :], reads=([("xo", nm, tok)] if (l > 0 and 0 in layers) else []), writes=[f"xt{j}"])
                    hT = hTs2[bs]; KhT = f"hT{bs}"
                    cosb = cs2[bs][:, 0, :]; sinb = cs2[bs][:, 1, :]; Kcos = f"cosb{bs}"; Ksin = f"sinb{bs}"
                    uw = uw2[bs]; Kuw = f"uw{bs}"
                    kzw = kzw2[bs]; vaugw = vaugw2[bs]; Kvw = f"vaugw{bs}"
                    qkl = qkl2[bs]; vgl = vgl2[bs]; Kvgl = f"vgl{bs}"

                    def projB(ci, bank):
                        for k in range(8):
                            mm(pb[bank][:, 0:N], wAB[:, k, ci * 128:(ci + 1) * 128], hT[:, k, :], k == 0, k == 7,
                               [("wAB", ci), KhT], [f"pb{bank}"])

                    for c in range(2):
                        projB(c, c)
                        act(sga[:, c, :], pb[c][:, 0:N], AF.Silu, [f"pb{c}"], ["sga"])
                    for g in range(4):
                        projB(10 + g, g % 2)
                        act(sgb[:, g, :], pb[g % 2][:, 0:N], AF.Silu, [f"pb{g % 2}"], ["sgb"])
                    for c in range(2):
                        projB(14 + c, c)
                        act(sgc[:, c, :], pb[c][:, 0:N], AF.Silu, [f"pb{c}"], ["sgc"])
                    for g in range(4):
                        projB(2 + g, 0)
                        if rope:
                            projB(6 + g, 1)
                            tt_op("dve", t1[:], pb[0][:, 0:N], cosb, ALU.mult, ["pb0", Kcos], ["t1"])
                            tt_op("dve", t2[:], pb[1][:, 0:N], sinb, ALU.mult, ["pb1", Ksin], ["t2"])
                            tt_op("pool", qT[:, g, :], t1[:], t2[:], ALU.add, ["t1", "t2"], ["qT"])
                        else:
                            cp_op("act", qT[:, g, :], pb[0][:, 0:N], ["pb0"], ["qT"])
                    for c in range(2):
                        for k in range(31):
                            mm(pb[2][:, 0:N], diag[:, c, k, :], uw[:, c, 1 + k:1 + k + N], k == 0, k == 30, [("diag", c), Kuw], ["pb2"])
                        act(cf[:, c, :], pb[2][:, 0:N], AF.Identity, ["pb2", "cp"], ["cf"], bias=cp[:, c, 0:1])
                        act(sq[:, c, :], pb[2][:, 0:N], AF.Square, ["pb2", "cp"], ["sq"], bias=cp[:, c, 0:1])
                    for c in range(2):
                        mm(pb[3][:, 0:N], ones256[:], cf[:, c, :], c == 0, c == 1, ["ones256", "cf"], ["pb3"])
                    for c in range(2):
                        mm(pb[3][:, N:2 * N], ones256[:], sq[:, c, :], c == 0, c == 1, ["ones256", "sq"], ["pb3"])
                    act(m2[:], pb[3][:, 0:N], AF.Square, ["pb3"], ["m2"])
                    tt_op("dve", crs[:], pb[3][:, N:2 * N], m2[:], ALU.subtract, ["pb3", "m2"], ["crs"])
                    act(crs[:], crs[:], AF.Ln, ["crs"], ["crs"], bias=EPS)
                    act(crs[:], crs[:], AF.Exp, ["crs"], ["crs"], scale=-0.5)
                    for c in range(2):
                        tt_op("dve", tt[c][:], cf[:, c, :], pb[3][:, 0:N], ALU.subtract, ["cf", "pb3"], [f"tt{c}"])
                        tt_op("pool", tt[c][:], tt[c][:], crs[:], ALU.mult, [f"tt{c}", "crs"], [f"tt{c}"])
                        act(tt[c][:], tt[c][:], AF.Silu, [f"tt{c}", "cp"], [f"tt{c}"], bias=cp[:, c, 2:3], scale=cp[:, c, 1:2])
                        tt_op("pool", catT[:, c, :], tt[c][:], sga[:, c, :], ALU.mult, [f"tt{c}", "sga"], ["catT"])
                    w0 = st * NBK - 1
                    ei = 0
                    for qb in range(NBK):
                        n = st * NBK + qb
                        qsl = slice(qb * 128, (qb + 1) * 128)
                        for kv in range(2):
                            keys = [("c", 0, None), ("c", 1, None)]
                            if nm == "l":
                                if n > 0:
                                    keys.append(("w", n - 1 - w0, maskP))
                                keys.append(("w", n - w0, None))
                                if n < nblk - 1:
                                    keys.append(("w", n + 1 - w0, maskN))
                            pvb = 6 + kv
                            for ki, (kind, bi, msk) in enumerate(keys):
                                if kind == "c":
                                    klhs = kzc[kv][:, bi * 128:(bi + 1) * 128]; kkey = f"kzc{kv}"
                                    vl = vaugc[:, bi, kv * 128:(kv + 1) * 128]; vkey = "vaugc"
                                else:
                                    klhs = kzw[kv][:, bi, :]; kkey = f"kzw{bs}_{kv}"
                                    vl = vaugw[:, bi, kv * 128:(kv + 1) * 128]; vkey = Kvw
                                sb_ = 4 + (ei % 2)
                                et = eT[ei % 3]; ek = f"eT{ei % 3}"
                                ei += 1
                                mm(pb[sb_][:, :], klhs, qT[:, :, qsl], True, True, [kkey, "qT"], [f"pb{sb_}"])
                                act(et[:], pb[sb_][:, :], AF.Exp, [f"pb{sb_}"], [ek], scale=0.125)
                                if msk is not None:
                                    mkey = "maskP" if msk is maskP else "maskN"
                                    tt_op("pool", et[:].rearrange("p (g q) -> p g q", g=4), et[:].rearrange("p (g q) -> p g q", g=4),
                                          bc_mid(msk[:], 4), ALU.mult, [ek, mkey], [ek])
                                mm(pb[pvb][:, :], vl, et[:], ki == 0, ki == len(keys) - 1, [vkey, ek], [f"pb{pvb}"])
                            dlo = kv * 64
                            olo = (1 - kv) * 64
                            ds = bigf[0][dlo:dlo + 64, :].rearrange("p (g q) -> p g q", g=4)
                            rr = bigf[1][dlo:dlo + 64, :].rearrange("p (g q) -> p g q", g=4)
                            tt_op("dve", ds, pb[pvb][dlo:dlo + 64, :].rearrange("p (g q) -> p g q", g=4),
                                  bc_last(esink[dlo:dlo + 64, kv * 4:kv * 4 + 4].rearrange("p (g o) -> p g o", o=1), 128), ALU.add,
                                  [f"pb{pvb}", "esink"], [("bigf0", kv)])
                            act(ds, ds, AF.Ln, [("bigf0", kv)], [("bigf0", kv)])
                            act(rr, ds, AF.Exp, [("bigf0", kv)], [("bigf1", kv)], scale=-1.0)
                            tt_op("pool", rr, rr, sgb[dlo:dlo + 64, :, qsl], ALU.mult, [("bigf1", kv), "sgb"], [("bigf1", kv)])
                            tt_op("dve", catT[dlo:dlo + 64, 2:6, qsl], pb[pvb][olo:olo + 64, :].rearrange("p (g q) -> p g q", g=4), rr,
                                  ALU.mult, [f"pb{pvb}", ("bigf1", kv)], ["catT"])
                    for j in range(NBK):
                        blk = st * NBK + j
                        sl = slice(j * 128, (j + 1) * 128)
                        for d_ in range(2):
                            for h in range(4):
                                cp_op("pool", bd[d_][32 * h:32 * h + 32, h, :], qkl[d_][32 * h:32 * h + 32, sl], [f"qkl{bs}_{d_}"], [f"bd{d_}"])
                                cp_op("pool", sbd[d_][32 * h:32 * h + 32, h * 64:(h + 1) * 64], Sin[nm][d_][32 * h:32 * h + 32, blk, :],
                                      [("Sin", nm, d_)], [f"sbd{d_}"])
                            mm(pb[d_][:, :], qkl[2 + d_][:, sl], bd[d_][:].rearrange("p h t -> p (h t)"), True, True,
                               [f"qkl{bs}_{2 + d_}", f"bd{d_}"], [f"pb{d_}"])
                            msk = maskN if d_ == 0 else maskP
                            tt_op("dve", attm[d_][:], pb[d_][:, :].rearrange("p (h t) -> p h t", h=4), bc_mid(msk[:], 4), ALU.mult,
                                  [f"pb{d_}", "maskN", "maskP"], [f"attm{d_}"])
                        for h in range(4):
                            osl = pb[2][0:64, h * 128:(h + 1) * 128]
                            mm(osl, vgl[:, j, h * 64:(h + 1) * 64], attm[0][:, h, :], True, False, [Kvgl, "attm0"], ["pb2"])
                            mm(osl, vgl[:, j, h * 64:(h + 1) * 64], attm[1][:, h, :], False, False, [Kvgl, "attm1"], ["pb2"])
                            mm(osl, sbd[0][:, h * 64:(h + 1) * 64], qkl[0][:, sl], False, False, ["sbd0", f"qkl{bs}_0"], ["pb2"])
                            mm(osl, sbd[1][:, h * 64:(h + 1) * 64], qkl[1][:, sl], False, True, ["sbd1", f"qkl{bs}_1"], ["pb2"])
                        osq = bigf[2]
                        act(osq[0:64, :], pb[2][0:64, :], AF.Square, ["pb2"], [("bigf2", 0)])
                        mm(pb[3][:, :], ones64[0:64, :], osq[0:64, :], True, True, ["ones64", ("bigf2", 0)], ["pb3"])
                        rsg = bigf[3]
                        act(rsg[:], pb[3][:, :], AF.Ln, ["pb3"], [("bigf3", 0)], bias=EPS)
                        act(rsg[:], rsg[:], AF.Exp, [("bigf3", 0)], [("bigf3", 0)], scale=-0.5)
                        for h in range(4):
                            e_ = h % 2
                            jc = h // 2
                            tt_op("pool", rgc[e_ * 64:e_ * 64 + 64, :], rsg[e_ * 64:e_ * 64 + 64, h * 128:(h + 1) * 128],
                                  sgc[e_ * 64:e_ * 64 + 64, jc, sl], ALU.mult, [("bigf3", 0), "sgc"], [("rgc", e_)])
                            tt_op("dve", catT[e_ * 64:e_ * 64 + 64, 6 + jc, sl], pb[2][0:64, h * 128:(h + 1) * 128],
                                  rgc[e_ * 64:e_ * 64 + 64, :], ALU.mult, ["pb2", ("rgc", e_)], ["catT"])
                    if debug and l == 0 and nm == "l":
                        P.dma("sp", dbg["cat"][:, :, t0:t0 + N].rearrange("k p t -> p k t"), catT[:], reads=["catT"])
                    for j in range(NBK):
                        tok = t0 + j * 128
                        xt = xts[j]; Kxt = f"xt{j}"
                        for hf in range(2):
                            ob = 4 + hf
                            for kc in range(8):
                                mm(pb[ob][:, :], catT[:, kc, j * 128:(j + 1) * 128], wout[:, kc, hf * 512:(hf + 1) * 512], kc == 0, kc == 7,
                                   ["catT", ("wout", kc)], [f"pb{ob}"])
                            yt = bigf[4 + hf]
                            tt_op("dve", yt[:], pb[ob][:, :], gate_bc[sidx][:, hf * 512:(hf + 1) * 512], ALU.mult,
                                  [f"pb{ob}", f"gate_bc{sidx}"], [(f"bigf{4 + hf}", 0), (f"bigf{4 + hf}", 1)])
                            tt_op("pool", x2[:, hf * 512:(hf + 1) * 512], yt[:], xt[:, hf * 512:(hf + 1) * 512], ALU.add,
                                  [(f"bigf{4 + hf}", 0), (f"bigf{4 + hf}", 1), Kxt], ["x2"])
                        if last and nm == "l":
                            act(xt[:], x2[:], AF.Square, ["x2"], [Kxt])
                            P.op("dve", lambda e, xt=xt: e.tensor_reduce(out=smallf[:, 0:1], in_=xt[:], axis=mybir.AxisListType.X, op=ALU.add), [Kxt], ["ss"])
                            ts_op("dve", smallf[:, 1:2], smallf[:, 0:1], 1.0 / D, EPS, ALU.mult, ALU.add, ["ss"], ["ss1"])
                            act(smallf[:, 2:3], smallf[:, 1:2], AF.Sqrt, ["ss1"], ["ss2"])
                            P.op("dve", lambda e: e.reciprocal(out=smallf[:, 3:4], in_=smallf[:, 2:3]), ["ss2"], ["rstd"])
                            stt_op(xt[:], x2[:], smallf[:, 3:4], gate_bc[1][:], ALU.mult, ALU.mult, ["x2", "rstd", "gate_bc1"], [Kxt])
                            P.dma("sp", xdst[tok:tok + 128, :], xt[:], reads=[Kxt], writes=[("xo", nm, tok)])
                        else:
                            P.dma("sp", xdst[tok:tok + 128, :], x2[:], reads=["x2"], writes=[("xo", nm, tok)])
                            if debug and l == 0:
                                P.dma("sp", dbg["x1" if nm == "l" else "xc1"][tok:tok + 128, :], x2[:], reads=["x2"])

            if stg_lim < 2:
                break
            pass_a("c")
            if stg_lim < 3:
                break
            pass_a("l")
            if stg_lim < 4:
                break
            P.dma("sp", kzc[0][0:64, :], scr["c"]["kT"][0:64, :], reads=[("kTs", "c", 0)], writes=["kzc0"])
            P.dma("sp", kzc[1][64:128, :], scr["c"]["kT"][64:128, :], reads=[("kTs", "c", 0)], writes=["kzc1"])
            P.dma("sp", vaugc[:], scr["c"]["vaug"].rearrange("b p c -> p b c"), reads=[("vaugs", "c", 0)], writes=["vaugc"])
            for ci, pieces in enumerate(Bchunks):
                load_chunk_cols(l, pieces, wAB[:, :, ci * 128:(ci + 1) * 128], ("wAB", ci), 128)
            if upd:
                pass_b("c")
            if stg_lim < 5:
                break
            pass_b("l")
            if stg_lim < 6:
                break

        P.analyze()
        sems_eng = {}
        for e in P.ENGS:
            for ep in range(P.n_epochs[e]):
                sems_eng[(e, ep)] = es.enter_context(nc.semaphore(f"s_{e}_{ep}"))
        sems_dma = [es.enter_context(nc.semaphore(f"d_{k}")) for k in range(N_DMA_SEMS)]
        block = es.enter_context(nc.Block())
        P.emit(block, sems_eng, sems_dma)
    return nc


def _consts():
    ident = np.eye(128, dtype=np.float32)
    j = np.arange(128)[:, None]
    i = np.arange(128)[None, :]
    maskP = (j >= i).astype(np.float32)
    maskN = (j <= i).astype(np.float32)
    rst = np.ones((128, N), np.float32)
    rst[:, 0::128] = 0.0
    t = np.arange(S)
    r = (t // 64).astype(np.float32)
    col = (t % 64).astype(np.float32)
    nf = 16
    inv = (10000.0 ** (-np.arange(nf, dtype=np.float32) / nf)).astype(np.float32)
    ang = np.concatenate([r[:, None] * inv, col[:, None] * inv], axis=-1).astype(np.float32)
    ang = np.concatenate([ang, ang], axis=-1)
    cos = np.cos(ang).astype(np.float32).T
    sin = np.sin(ang).astype(np.float32).T
    sgn = np.where(np.arange(64) < 32, -1.0, 1.0).astype(np.float32)[:, None]
    sinS = sin * sgn
    cosT = np.ascontiguousarray(np.concatenate([cos, cos], axis=0))
    sinT = np.ascontiguousarray(np.concatenate([sinS, sinS], axis=0))
    return dict(ident=ident, maskP=maskP, maskN=maskN, rst=rst, cosT=cosT, sinT=sinT)


_NC_CACHE = {}


def _prep_inputs(x, c, ctx, c_ctx, w_mod, b_mod, w_in, conv_w, conv_b, conv_ln_w, conv_ln_b,
                 attn_sink, gla_w_up, gla_b_up, gla_norm_w, w_out, final_norm_w):
    f = lambda a: np.ascontiguousarray(np.asarray(a, dtype=np.float32))
    cons = _consts()
    bmT = f(np.asarray(b_mod).reshape(L, 24, 128).transpose(0, 2, 1))
    cw = f(np.asarray(conv_w).reshape(L, 31, 2, 128).transpose(0, 3, 2, 1))
    cpar = f(np.stack([np.asarray(conv_b), np.asarray(conv_ln_w), np.asarray(conv_ln_b)], axis=-1)
             .reshape(L, 2, 128, 3).transpose(0, 2, 1, 3))
    wup = np.zeros((L, 32, 2, 128), np.float32)
    wup[:, 0:16, 0, :] = np.asarray(gla_w_up)[:, 0]
    wup[:, 16:32, 1, :] = np.asarray(gla_w_up)[:, 1]
    bup = f(np.asarray(gla_b_up).transpose(0, 2, 1))
    gnw = f(np.asarray(gla_norm_w).reshape(L, 2, 128).transpose(0, 2, 1))
    shared = dict(w_mod=f(w_mod), bmT=bmT, w_in=f(w_in), w_out=f(w_out), cw=cw, cp=cpar, sink=f(attn_sink),
                  wup=wup, bup=bup, gnw=gnw, fnw=f(final_norm_w), **cons)
    in_maps = []
    cctx = np.asarray(c_ctx, np.float32).reshape(8, 128).T
    for b in range(8):
        cc = np.stack([np.asarray(c[b], np.float32).reshape(8, 128).T, cctx], axis=-1)
        m = dict(shared)
        m["x"] = f(x[b])
        m["ctx"] = f(ctx[b])
        m["cc"] = f(cc)
        in_maps.append(m)
    return in_maps


def kernel(**inputs):
    in_maps = _prep_inputs(**inputs)
    if "nc" not in _NC_CACHE:
        _NC_CACHE["nc"] = build(False)
    nc = _NC_CACHE["nc"]
    outs = []
    G = CORES_PER_LAUNCH
    for g0 in range(0, 8, G):
        res = run_bass_kernel_spmd(nc, in_maps[g0:g0 + G], core_ids=list(range(G)))
        outs.extend(np.asarray(r["out"], dtype=np.float32) for r in res.results)
    return np.stack(outs, axis=0)
```

```python
import numpy as np
from contextlib import ExitStack
import concourse.bass as bass
import concourse.mybir as mybir
from concourse.bass_utils import run_bass_kernel_spmd

F32 = mybir.dt.float32
BF16 = mybir.dt.bfloat16
ALU = mybir.AluOpType
AF = mybir.ActivationFunctionType

N_DMA_SEMS = 2
EPOCH = 30000

D = 1024
S = 4096
CT = 256
N = 256
NBK = N // 128
L = 2
NIN = 2848
EPS = 1e-6
CORES_PER_LAUNCH = 8
RESCHEDULE = True


class Prog:
    ENGS = ("pe", "act", "dve", "pool", "sp")

    def __init__(self, nc):
        self.nc = nc
        self.ops = []
        self.ndma = 0

    def op(self, eng, fn, reads=(), writes=(), cost=0.3):
        self.ops.append(dict(eng=eng, fn=fn, reads=tuple(reads), writes=tuple(writes), dma=False, cost=cost))

    def dma(self, eng, out, in_, reads=(), writes=()):
        self.ops.append(dict(eng=eng, fn=lambda e: e.dma_start(out=out, in_=in_),
                             reads=tuple(reads), writes=tuple(writes), dma=True, didx=self.ndma,
                             cost=2.5 + _fsz(out) * 128 * 2.0 / 150e3))
        self.ndma += 1

    def reschedule(self):
        import heapq
        ops = self.ops
        n = len(ops)
        for o in ops:
            ex = tuple(r for r in o["reads"] if isinstance(r, str) and r.startswith("pb") and r not in o["writes"])
            if ex:
                o["writes"] = o["writes"] + ex
        last_w = {}
        readers = {}
        deps = [None] * n
        succ = [[] for _ in range(n)]
        for i, o in enumerate(ops):
            d = set()
            for r in o["reads"]:
                if r in last_w:
                    d.add(last_w[r])
            for w in o["writes"]:
                if w in last_w:
                    d.add(last_w[w])
                for rd in readers.get(w, ()):
                    d.add(rd)
            d.discard(i)
            deps[i] = d
            for j in d:
                succ[j].append(i)
            for r in o["reads"]:
                readers.setdefault(r, []).append(i)
            for w in o["writes"]:
                last_w[w] = i
                readers[w] = []
        nleft = [len(d) for d in deps]
        ready_t = [0.0] * n
        fin = [0.0] * n
        start = [0.0] * n
        heaps = {e: [] for e in self.ENGS}
        for i in range(n):
            if nleft[i] == 0:
                heapq.heappush(heaps[ops[i]["eng"]], (0.0, i))
        free = {e: 0.0 for e in self.ENGS}
        dma_fin = []
        done = 0
        while done < n:
            best = None
            for e in self.ENGS:
                if not heaps[e]:
                    continue
                rt, i = heaps[e][0]
                st = max(rt, free[e])
                if ops[i]["dma"] and len(dma_fin) >= N_DMA_SEMS:
                    st = max(st, dma_fin[-N_DMA_SEMS])
                if best is None or (st, i) < best[:2]:
                    best = (st, i, e)
            st, i, e = best
            heapq.heappop(heaps[e])
            o = ops[i]
            start[i] = st
            if o["dma"]:
                free[e] = st + 0.06
                fin[i] = st + o["cost"]
                dma_fin.append(fin[i])
                dma_fin.sort()
            else:
                free[e] = st + o["cost"]
                fin[i] = st + o["cost"] + 0.05
            done += 1
            for k in succ[i]:
                nleft[k] -= 1
                ready_t[k] = max(ready_t[k], fin[i])
                if nleft[k] == 0:
                    heapq.heappush(heaps[ops[k]["eng"]], (ready_t[k], k))
        order = sorted(range(n), key=lambda i: (start[i], i))
        pos = {i: p for p, i in enumerate(order)}
        for i in range(n):
            for j in deps[i]:
                assert pos[j] < pos[i]
        self.ops = [ops[i] for i in order]
        k = 0
        for o in self.ops:
            if o["dma"]:
                o["didx"] = k
                k += 1
        self.est_us = max(fin) if n else 0.0

    def analyze(self):
        last_w = {}
        readers = {}
        for o in self.ops:
            ex = tuple(r for r in o["reads"] if isinstance(r, str) and r.startswith("pb") and r not in o["writes"])
            if ex:
                o["writes"] = o["writes"] + ex
        for i, o in enumerate(self.ops):
            deps = set()
            for r in o["reads"]:
                if r in last_w:
                    deps.add(last_w[r])
            for w in o["writes"]:
                if w in last_w:
                    deps.add(last_w[w])
                for rd in readers.get(w, ()):
                    deps.add(rd)
            deps.discard(i)
            fd = set()
            for d in deps:
                p = self.ops[d]
                if (not p["dma"]) and (not o["dma"]) and p["eng"] == o["eng"]:
                    if o["eng"] == "pe":
                        continue
                fd.add(d)
            o["deps"] = fd
            for r in o["reads"]:
                readers.setdefault(r, []).append(i)
            for w in o["writes"]:
                last_w[w] = i
                readers[w] = []
        for o in self.ops:
            o["need_inc"] = False
        for o in self.ops:
            for d in o["deps"]:
                self.ops[d]["need_inc"] = True
        cnt = {e: 0 for e in self.ENGS}
        for o in self.ops:
            if o["dma"]:
                k = o["didx"]
                o["sem"] = ("dma", k % N_DMA_SEMS, 0)
                o["val"] = 16 * (k // N_DMA_SEMS + 1)
            elif o["need_inc"]:
                cnt[o["eng"]] += 1
                ep = (cnt[o["eng"]] - 1) // EPOCH
                o["sem"] = ("eng", o["eng"], ep)
                o["val"] = cnt[o["eng"]] - ep * EPOCH
        self.n_epochs = {e: (cnt[e] + EPOCH - 1) // EPOCH for e in self.ENGS}

    def emit(self, block, sems_eng, sems_dma):
        dma_ops = [o for o in self.ops if o["dma"]]
        by_eng = {e: [] for e in self.ENGS}
        for i, o in enumerate(self.ops):
            by_eng[o["eng"]].append(i)

        def semh(s):
            return sems_eng[(s[1], s[2])] if s[0] == "eng" else sems_dma[s[1]]

        def run(e, handle):
            waited = {}
            for i in by_eng[e]:
                o = self.ops[i]
                need = {}
                for d in o["deps"]:
                    p = self.ops[d]
                    s = p["sem"]
                    need[s] = max(need.get(s, 0), p["val"])
                if o["dma"]:
                    k = o["didx"]
                    if k >= N_DMA_SEMS:
                        s = ("dma", k % N_DMA_SEMS, 0)
                        need[s] = max(need.get(s, 0), 16 * (k // N_DMA_SEMS))
                for s, v in need.items():
                    if waited.get(s, 0) >= v:
                        continue
                    handle.wait_ge(semh(s), v)
                    waited[s] = v
                ins = o["fn"](handle)
                if o["dma"]:
                    ins.then_inc(semh(o["sem"]), 16)
                elif o["need_inc"]:
                    ins.then_inc(semh(o["sem"]), 1)
            if e == "sp":
                final = {}
                for o in dma_ops:
                    final[o["sem"]] = max(final.get(o["sem"], 0), o["val"])
                for s, v in final.items():
                    handle.wait_ge(semh(s), v)

        @block.tensor
        def _(h):
            run("pe", h)

        @block.scalar
        def _(h):
            run("act", h)

        @block.vector
        def _(h):
            run("dve", h)

        @block.gpsimd
        def _(h):
            run("pool", h)

        @block.sync
        def _(h):
            run("sp", h)


def _chunks():
    A = [[(0, 128)], [(128, 128)], [(256, 128)], [(384, 128)], [(1280, 128)],
         [(1280 + 32, 32), (1280, 32), (1280 + 96, 32), (1280 + 64, 32)],
         [(2176, 128)], [(2048, 128)], [(2560, 32)]]
    At = [(1408, 128), (2304, 256)]
    B = [[(512, 128)], [(640, 128)]]
    for g in range(4):
        B.append([(768 + g * 64, 64), (768 + (4 + g) * 64, 64)])
    for g in range(4):
        a = 768 + g * 64
        b = 768 + (4 + g) * 64
        B.append([(a + 32, 32), (a, 32), (b + 32, 32), (b, 32)])
    for g in range(4):
        B.append([(1536 + g * 64, 64), (1536 + (4 + g) * 64, 64)])
    B.append([(2592, 128)])
    B.append([(2720, 128)])
    return A, At, B


def _fsz(ap):
    n = 1
    for _, c in list(ap.ap)[1:]:
        n *= int(c)
    return n


def bc_mid(ap, n):
    a = [list(x) for x in ap.ap]
    return bass.AP(ap.tensor, ap.offset, [a[0], [0, n]] + a[1:])


def bc_last(ap, n):
    a = [list(x) for x in ap.ap]
    a[-1] = [0, n]
    return bass.AP(ap.tensor, ap.offset, a)


def build(debug=False, stg_lim=99, layers=(0, 1)):
    nc = bass.Bass("TRN2", target_bir_lowering=False)
    es = ExitStack()
    P = Prog(nc)

    def din(name, shape, dt=F32):
        return nc.dram_tensor(name, list(shape), dt, kind="ExternalInput").ap()

    def dscr(name, shape, dt):
        return nc.dram_tensor(name, list(shape), dt, kind="Internal").ap()

    x_d = din("x", [S, D])
    ctx_d = din("ctx", [CT, D])
    cc_d = din("cc", [128, 8, 2])
    wmod_d = din("w_mod", [L, D, 3 * D])
    bmT_d = din("bmT", [L, 128, 24])
    win_d = din("w_in", [L, D, NIN])
    wout_d = din("w_out", [L, D, D])
    cw_d = din("cw", [L, 128, 2, 31])
    cp_d = din("cp", [L, 128, 2, 3])
    sink_d = din("sink", [L, 8])
    wup_d = din("wup", [L, 32, 2, 128])
    bup_d = din("bup", [L, 128, 2])
    gnw_d = din("gnw", [L, 128, 2])
    fnw_d = din("fnw", [D])
    ident_d = din("ident", [128, 128])
    maskP_d = din("maskP", [128, 128])
    maskN_d = din("maskN", [128, 128])
    rst_d = din("rst", [128, N])
    cos_d = din("cosT", [128, S])
    sin_d = din("sinT", [128, S])
    out_d = nc.dram_tensor("out", [S, D], F32, kind="ExternalOutput").ap()

    scr = {}
    for nm, T in (("l", S), ("c", CT)):
        scr[nm] = dict(
            T=T,
            hT=dscr(f"hT_{nm}", [8, 128, T], BF16),
            uT=dscr(f"uT_{nm}", [2, 128, T + 32], BF16),
            kT=dscr(f"kT_{nm}", [128, T], BF16),
            vaug=dscr(f"vaug_{nm}", [T // 128, 128, 256], BF16),
            vg=dscr(f"vg_{nm}", [T // 128, 128, 256], BF16),
            qk=dscr(f"qk_{nm}", [4, 128, T], BF16),
        )
    x1_d = dscr("x1", [S, D], F32)
    xc1_d = dscr("xc1", [CT, D], F32)

    dbg = {}
    if debug:
        dbg["hT"] = nc.dram_tensor("dbg_hT", [8, 128, S], BF16, kind="ExternalOutput").ap()
        dbg["cat"] = nc.dram_tensor("dbg_cat", [8, 128, S], BF16, kind="ExternalOutput").ap()
        dbg["x1"] = nc.dram_tensor("dbg_x1", [S, D], F32, kind="ExternalOutput").ap()
        dbg["xc1"] = nc.dram_tensor("dbg_xc1", [CT, D], F32, kind="ExternalOutput").ap()
        dbg["mod"] = nc.dram_tensor("dbg_mod", [128, 48], F32, kind="ExternalOutput").ap()

    with es:
        def sb(name, shape, dt=F32):
            return es.enter_context(nc.sbuf_tensor("s_" + name, list(shape), dt))

        pb = [es.enter_context(nc.psum_tensor(f"pb{i}", [128, 512], F32)) for i in range(8)]

        identf = sb("identf", [128, 128]); identb = sb("identb", [128, 128], BF16)
        maskP = sb("maskP", [128, 128], BF16); maskN = sb("maskN", [128, 128], BF16)
        rst = sb("rst", [128, N])
        ones256 = sb("ones256", [128, 128]); ones64 = sb("ones64", [128, 128]); onesf = sb("onesf", [128, 128])
        stage = [sb("stage0", [128, 8, 128]), sb("stage1", [128, 8, 128])]
        scc = sb("scc", [128, 8, 2]); bmT = sb("bmT", [128, 24]); modT = sb("modT", [128, 24, 2]); one1p = sb("one1p", [128, 8, 2])
        gate_bc = [sb("gate_bc0", [128, D]), sb("gate_bc1", [128, D])]
        cw = sb("cw", [128, 2, 31]); cp = sb("cp", [128, 2, 3]); esink = sb("esink", [128, 8])
        wupf = sb("wupf", [32, 2, 128]); wupb = sb("wupb", [32, 2, 128], BF16)
        nbup = sb("nbup", [128, 2]); gnw = sb("gnw", [128, 2])
        dg = sb("dg", [128, 128])
        wAB = sb("wAB", [128, 8, 2048], BF16)
        wout = sb("wout", [128, 8, D], BF16)
        diag = sb("diag", [128, 2, 31, 128], BF16)
        Sin = {"l": [sb("SinLf", [128, 32, 64], BF16), sb("SinLb", [128, 32, 64], BF16)],
               "c": [sb("SinCf", [128, 2, 64], BF16), sb("SinCb", [128, 2, 64], BF16)]}
        stt = {"l": [sb("stLf", [128, 32, 64]), sb("stLb", [128, 32, 64])],
               "c": [sb("stCf", [128, 2, 64]), sb("stCb", [128, 2, 64])]}
        dec = {"l": [sb("decLf", [128, 32]), sb("decLb", [128, 32])],
               "c": [sb("decCf", [128, 2]), sb("decCb", [128, 2])]}
        Spp = [sb("Spp0", [128, 64]), sb("Spp1", [128, 64])]
        Sfin = [sb("Sfinf", [128, 64]), sb("Sfinb", [128, 64])]
        kzc = [sb("kzc0", [128, CT], BF16), sb("kzc1", [128, CT], BF16)]
        vaugc = sb("vaugc", [128, 2, 256], BF16)
        hTs2 = [sb("hT0", [128, 8, N], BF16), sb("hT1", [128, 8, N], BF16)]
        xts = [sb("xt0", [128, D]), sb("xt1", [128, D])]; x2 = sb("x2", [128, D])
        smallf = sb("smallf", [128, 8])
        cs2 = [sb("cs0", [128, 2, N]), sb("cs1", [128, 2, N])]; t1 = sb("t1", [128, N]); t2 = sb("t2", [128, N])
        bigf = [sb(f"bigf{i}", [128, 512]) for i in range(6)]
        sig = sb("sig", [128, N]); uTt = sb("uTt", [128, 2, N], BF16); kTt = sb("kTt", [128, N], BF16)
        qin_t = [sb("qin_f", [128, N], BF16), sb("qin_b", [128, N], BF16)]
        kin_t = [sb("kin_f", [128, N], BF16), sb("kin_b", [128, N], BF16)]
        kdtok = [sb("kdtok_f", [128, NBK, 128], BF16), sb("kdtok_b", [128, NBK, 128], BF16)]
        vaugt = sb("vaugt", [128, NBK, 256], BF16); vgt = sb("vgt", [128, NBK, 256], BF16)
        lrT = sb("lrT", [32, N], BF16)
        zpad = sb("zpad", [128, 2, 16], BF16)
        sga = sb("sga", [128, 2, N]); qT = sb("qT", [128, 4, N], BF16); sgb = sb("sgb", [128, 4, N], BF16); sgc = sb("sgc", [128, 2, N])
        uw2 = [sb("uw0", [128, 2, N + 32], BF16), sb("uw1", [128, 2, N + 32], BF16)]
        cf = sb("cf", [128, 2, N]); sq = sb("sq", [128, 2, N]); m2 = sb("m2", [128, N]); crs = sb("crs", [128, N])
        tt = [sb("tt0", [128, N]), sb("tt1", [128, N])]
        catT = sb("catT", [128, 8, N], BF16)
        kzw2 = [[sb(f"kzw{b_}_{kv_}", [128, 4, 128], BF16) for kv_ in range(2)] for b_ in range(2)]
        vaugw2 = [sb("vaugw0", [128, 4, 256], BF16), sb("vaugw1", [128, 4, 256], BF16)]
        eT = [sb(f"eT{i}", [128, 512], BF16) for i in range(3)]
        qkl2 = [[sb(f"qkl{b_}_{i}", [128, N], BF16) for i in range(4)] for b_ in range(2)]
        vgl2 = [sb("vgl0", [128, NBK, 256], BF16), sb("vgl1", [128, NBK, 256], BF16)]
        bd = [sb("bd_f", [128, 4, 128], BF16), sb("bd_b", [128, 4, 128], BF16)]
        attm = [sb("attm_f", [128, 4, 128], BF16), sb("attm_b", [128, 4, 128], BF16)]
        sbd = [sb("sbd_f", [128, 256], BF16), sb("sbd_b", [128, 256], BF16)]
        rgc = sb("rgc", [128, 128])

        q_rr = {"n": 0}

        def dq():
            q_rr["n"] += 1
            return "sp"

        def act(out, in_, func, reads, writes, bias=None, scale=None, accum=None):
            kw = {}
            if bias is not None:
                kw["bias"] = bias
            if scale is not None:
                kw["scale"] = scale
            if accum is not None:
                kw["accum_out"] = accum
            P.op("act", lambda e: e.activation(out=out, in_=in_, func=func, **kw), reads, writes, cost=0.22 + _fsz(out) / 1400.0)

        def tt_op(eng, out, in0, in1, op, reads, writes):
            P.op(eng, lambda e: e.tensor_tensor(out=out, in0=in0, in1=in1, op=op), reads, writes,
                 cost=(0.07 + _fsz(out) / 960.0) if eng == "dve" else (0.2 + _fsz(out) / 600.0))

        def ts_op(eng, out, in0, s1, s2, op0, op1, reads, writes):
            if op1 is None and eng == "pool" and op0 == ALU.mult:
                P.op(eng, lambda e: e.tensor_scalar(out=out, in0=in0, scalar1=s1, scalar2=1.0, op0=ALU.mult, op1=ALU.mult), reads, writes,
                     cost=0.2 + _fsz(out) / 600.0)
            elif op1 is None:
                P.op(eng, lambda e: e.tensor_scalar(out=out, in0=in0, scalar1=s1, scalar2=None, op0=op0), reads, writes,
                     cost=(0.07 + _fsz(out) / 960.0) if eng == "dve" else (0.2 + _fsz(out) / 600.0))
            else:
                P.op(eng, lambda e: e.tensor_scalar(out=out, in0=in0, scalar1=s1, scalar2=s2, op0=op0, op1=op1), reads, writes,
                     cost=(0.07 + _fsz(out) / 960.0) if eng == "dve" else (0.2 + _fsz(out) / 600.0))

        def stt_op(out, in0, scalar, in1, op0, op1, reads, writes):
            P.op("dve", lambda e: e.scalar_tensor_tensor(out=out, in0=in0, scalar=scalar, in1=in1, op0=op0, op1=op1), reads, writes,
                 cost=0.07 + _fsz(out) / 960.0)

        def cp_op(eng, out, in_, reads, writes):
            if eng == "act":
                act(out, in_, AF.Copy, reads, writes)
            else:
                P.op(eng, lambda e: e.tensor_copy(out=out, in_=in_), reads, writes,
                     cost=(0.07 + _fsz(out) / 960.0) if eng == "dve" else (0.2 + _fsz(out) / 600.0))

        def mm(out, lhsT, rhs, start, stop, reads, writes):
            P.op("pe", lambda e: e.matmul(out, lhsT=lhsT, rhs=rhs, start=start, stop=stop), reads, writes,
                 cost=0.03 + max(64, _fsz(rhs)) * (4 if rhs.dtype == F32 else 1) / 1400.0)

        def memset(eng, ap, v, writes):
            P.op(eng, lambda e: e.memset(ap, v), (), writes, cost=0.2 + _fsz(ap) / 600.0)

        P.dma("sp", identf[:], ident_d, writes=["identf"])
        cp_op("dve", identb[:], identf[:], ["identf"], ["identb"])
        P.dma("sp", stage[0][:, 0, :], maskP_d, writes=["stage0"])
        cp_op("dve", maskP[:], stage[0][:, 0, :], ["stage0"], ["maskP"])
        P.dma("sp", stage[1][:, 0, :], maskN_d, writes=["stage1"])
        cp_op("dve", maskN[:], stage[1][:, 0, :], ["stage1"], ["maskN"])
        P.dma("sp", rst[:], rst_d, writes=["rst"])
        memset("pool", ones256[:], 1.0 / 256.0, ["ones256"])
        memset("pool", ones64[:], 1.0 / 64.0, ["ones64"])
        memset("pool", onesf[:], 1.0, ["onesf"])
        memset("pool", zpad[:], 0.0, ["zpad"])
        for d_ in range(2):
            memset("pool", bd[d_][:], 0.0, [f"bd{d_}"])
            memset("pool", sbd[d_][:], 0.0, [f"sbd{d_}"])
            for b_ in range(2):
                memset("pool", kzw2[b_][d_][:], 0.0, [f"kzw{b_}_{d_}"])
            memset("pool", kzc[d_][:], 0.0, [f"kzc{d_}"])
        memset("pool", vaugt[:, :, 0:64], 1.0, ["vaugt"])
        memset("pool", vaugt[:, :, 192:256], 1.0, ["vaugt"])
        for nm in ("l", "c"):
            T = scr[nm]["T"]
            P.dma("sp", scr[nm]["uT"][:, :, 0:16].rearrange("c p t -> p c t"), zpad[:], reads=["zpad"], writes=[("uTs", nm, "padl")])
            P.dma("sp", scr[nm]["uT"][:, :, 16 + T:32 + T].rearrange("c p t -> p c t"), zpad[:], reads=["zpad"], writes=[("uTs", nm, "padr")])

        Achunks, Atok, Bchunks = _chunks()
        stg_rr = {"n": 0}
        cast_rr = {"n": 0}

        def load_chunk_cols(l, pieces, dst_ap, dst_key, width):
            i = stg_rr["n"] % 2
            stg_rr["n"] += 1
            off = 0
            for (cs, w) in pieces:
                P.dma(dq(), stage[i][:, :, off:off + w], win_d[l, :, cs:cs + w].rearrange("(k p) n -> p k n", p=128),
                      writes=[f"stage{i}"])
                off += w
            eng = ("pool", "dve", "act")[cast_rr["n"] % 3]
            cast_rr["n"] += 1
            cp_op(eng, dst_ap, stage[i][:, :, 0:width], [f"stage{i}"], [dst_key])

        def seq_info(nm, l):
            T = scr[nm]["T"]
            if nm == "l":
                xsrc = x_d if (l == 0 or 0 not in layers) else x1_d
                xdst = x1_d if l == 0 else out_d
            else:
                xsrc = ctx_d if (l == 0 or 0 not in layers) else xc1_d
                xdst = xc1_d
            return T, xsrc, xdst

        for l in layers:
            upd = (l == 0)
            if l == layers[0]:
                P.dma("sp", scc[:], cc_d, writes=["scc"])
                act(scc[:], scc[:], AF.Silu, ["scc"], ["scc"])
            P.dma("sp", bmT[:], bmT_d[l], writes=["bmT"])
            P.dma("sp", cw[:], cw_d[l], writes=["cw"])
            P.dma("sp", cp[:], cp_d[l], writes=["cp"])
            P.dma("sp", esink[:], bass.AP(sink_d.tensor, sink_d[l].offset, [[0, 128], [1, 8]]), writes=["esink"])
            act(esink[:], esink[:], AF.Exp, ["esink"], ["esink"])
            P.dma("sp", wupf[:], wup_d[l], writes=["wupf"])
            cp_op("dve", wupb[:], wupf[:], ["wupf"], ["wupb"])
            P.dma("sp", nbup[:], bup_d[l], writes=["nbup"])
            ts_op("dve", nbup[:], nbup[:], -1.0, None, ALU.mult, None, ["nbup"], ["nbup"])
            P.dma("sp", gnw[:], gnw_d[l], writes=["gnw"])
            modps = pb[6]
            for j in range(24):
                i = stg_rr["n"] % 2
                stg_rr["n"] += 1
                P.dma(dq(), stage[i][:], wmod_d[l, :, j * 128:(j + 1) * 128].rearrange("(k p) n -> p k n", p=128),
                      writes=[f"stage{i}"])
                for k in range(8):
                    mm(modps[:, 2 * j:2 * j + 2], stage[i][:, k, :], scc[:, k, :], k == 0, k == 7,
                       [f"stage{i}", "scc"], ["pb6"])
            tt_op("dve", modT[:], modps[:, 0:48].rearrange("p (j s) -> p j s", s=2), bc_last(bmT[:].rearrange("p (j o) -> p j o", o=1), 2),
                  ALU.add, ["pb6", "bmT"], ["modT"])
            ts_op("dve", one1p[:], modT[:, 8:16, :], 1.0, None, ALU.add, None, ["modT"], ["one1p"])
            if debug and l == 0:
                P.dma("sp", dbg["mod"], modT[:].rearrange("p j s -> p (j s)"), reads=["modT"])
            for s_ in range(2):
                if s_ == 1 and not upd:
                    continue
                for k in range(8):
                    ts_op("dve", dg[:], identf[:], modT[:, 16 + k, s_:s_ + 1], None, ALU.mult, None, ["identf", "modT"], ["dg"])
                    bank = pb[4 + (k // 4)]
                    mm(bank[:, (k % 4) * 128:(k % 4 + 1) * 128], onesf[:], dg[:], True, True, ["onesf", "dg"], [f"pb{4 + k // 4}"])
                cp_op("act", gate_bc[s_][:, 0:512], pb[4][:], ["pb4"], [f"gate_bc{s_}"])
                cp_op("act", gate_bc[s_][:, 512:1024], pb[5][:], ["pb5"], [f"gate_bc{s_}"])
            if l == L - 1:
                P.dma("sp", gate_bc[1][:], bass.AP(fnw_d.tensor, 0, [[0, 128], [1, D]]), writes=["gate_bc1"])
            if stg_lim < 1:
                break
            for ci, pieces in enumerate(Achunks):
                wdt = sum(w for _, w in pieces)
                slot = 3 + ci
                load_chunk_cols(l, pieces, wAB[:, :, slot * 128:slot * 128 + wdt], ("wAB", slot), wdt)
            off = 0
            for (cs, w) in Atok:
                for sub in range(w // 128):
                    slot = (off // 128)
                    load_chunk_cols(l, [(cs + sub * 128, 128)], wAB[:, :, slot * 128:(slot + 1) * 128], ("wAB", slot), 128)
                    off += 128
            for kc in range(8):
                i = stg_rr["n"] % 2
                stg_rr["n"] += 1
                st2 = stage[i][:].rearrange("p k n -> p (k n)")
                if kc < 2:
                    P.dma(dq(), st2, wout_d[l, kc * 128:(kc + 1) * 128, :], writes=[f"stage{i}"])
                elif kc < 6:
                    g = kc - 2
                    P.dma(dq(), st2[0:64, :], wout_d[l, 256 + g * 64:256 + (g + 1) * 64, :], writes=[f"stage{i}"])
                    P.dma(dq(), st2[64:128, :], wout_d[l, 256 + (4 + g) * 64:256 + (5 + g) * 64, :], writes=[f"stage{i}"])
                else:
                    P.dma(dq(), st2, wout_d[l, 768 + (kc - 6) * 128:768 + (kc - 5) * 128, :], writes=[f"stage{i}"])
                if kc < 6:
                    cp_op(("pool", "dve")[kc % 2], wout[:, kc, :], st2, [f"stage{i}"], [("wout", kc)])
                else:
                    ts_op("dve", wout[:, kc, :], st2, gnw[:, kc - 6:kc - 5], None, ALU.mult, None, [f"stage{i}", "gnw"], [("wout", kc)])
            for c in range(2):
                for k in range(31):
                    ts_op(("pool", "dve")[k % 2], diag[:, c, k, :], identb[:], cw[:, c, k:k + 1], None, ALU.mult, None,
                          ["identb", "cw"], [("diag", c)])

            def pass_a(nm):
                T, xsrc, _ = seq_info(nm, l)
                sidx = 0 if nm == "l" else 1
                sc = scr[nm]
                rope = (nm == "l")
                def a_loads(st):
                    for j in range(NBK):
                        tok = st * N + j * 128
                        P.dma("sp", xts[j][:], xsrc[tok:tok + 128, :], reads=([("xo", nm, tok)] if (l > 0 and 0 in layers) else []), writes=[f"xt{j}"])

                a_loads(0)
                hT = hTs2[0]
                cosb = cs2[0][:, 0, :]; sinb = cs2[0][:, 1, :]
                for st in range(T // N):
                    t0 = st * N
                    for j in range(NBK):
                        tok = t0 + j * 128
                        xt = xts[j]; Kxt = f"xt{j}"
                        act(x2[:], xt[:], AF.Square, [Kxt], ["x2"])
                        P.op("dve", lambda e: e.tensor_reduce(out=smallf[:, 0:1], in_=x2[:], axis=mybir.AxisListType.X, op=ALU.add), ["x2"], ["ss"])
                        ts_op("dve", smallf[:, 1:2], smallf[:, 0:1], 1.0 / D, EPS, ALU.mult, ALU.add, ["ss"], ["ss1"])
                        act(smallf[:, 2:3], smallf[:, 1:2], AF.Sqrt, ["ss1"], ["ss2"])
                        P.op("dve", lambda e: e.reciprocal(out=smallf[:, 3:4], in_=smallf[:, 2:3]), ["ss2"], ["rstd"])
                        act(x2[:], xt[:], AF.Copy, [Kxt, "rstd"], ["x2"], scale=smallf[:, 3:4])
                        for k in range(8):
                            bank = 2 + k // 4
                            P.op("pe", lambda e, k=k, bank=bank: e.transpose(out=pb[bank][:, (k % 4) * 128:(k % 4 + 1) * 128],
                                                                              in_=x2[:, k * 128:(k + 1) * 128], identity=identf[:]),
                                 ["x2", "identf"], [f"pb{bank}"])
                        for k in range(8):
                            bank = 2 + k // 4
                            src = pb[bank][:, (k % 4) * 128:(k % 4 + 1) * 128]
                            dst = hT[:, k, j * 128:(j + 1) * 128]
                            if k < 4:
                                ts_op("dve", dst, src, one1p[:, k, sidx:sidx + 1], modT[:, k, sidx:sidx + 1], ALU.mult, ALU.add,
                                      [f"pb{bank}", "one1p", "modT"], ["hT0"])
                            else:
                                act(dst, src, AF.Identity, [f"pb{bank}", "one1p", "modT"], ["hT0"],
                                    bias=modT[:, k, sidx:sidx + 1], scale=one1p[:, k, sidx:sidx + 1])
                    if st + 1 < T // N:
                        a_loads(st + 1)
                    P.dma("sp", sc["hT"][:, :, t0:t0 + N].rearrange("k p t -> p k t"), hT[:], reads=["hT0"], writes=[("hTs", nm, st)])
                    if debug and l == 0 and nm == "l":
                        P.dma("sp", dbg["hT"][:, :, t0:t0 + N].rearrange("k p t -> p k t"), hT[:], reads=["hT0"])
                    if rope:
                        P.dma("sp", cosb, cos_d[:, t0:t0 + N], writes=["cosb0"])
                        P.dma("sp", sinb, sin_d[:, t0:t0 + N], writes=["sinb0"])

                    def projA(ci, bank, M=128):
                        slot = 3 + ci
                        for k in range(8):
                            mm(pb[bank][0:M, 0:N], wAB[:, k, slot * 128:slot * 128 + M], hT[:, k, :], k == 0, k == 7,
                               [("wAB", slot), "hT0"], [f"pb{bank}"])

                    for c in range(2):
                        projA(2 + c, 0)
                        act(sig[:], pb[0][:, 0:N], AF.Sigmoid, ["pb0"], ["sig"])
                        projA(c, 1)
                        tt_op("dve", uTt[:, c, :], pb[1][:, 0:N], sig[:], ALU.mult, ["pb1", "sig"], ["uTt"])
                    P.dma("sp", sc["uT"][:, :, 16 + t0:16 + t0 + N].rearrange("c p t -> p c t"), uTt[:], reads=["uTt"],
                          writes=[("uTs", nm, st)])
                    projA(4, 0)
                    if rope:
                        projA(5, 1)
                        tt_op("dve", t1[:], pb[0][:, 0:N], cosb, ALU.mult, ["pb0", "cosb0"], ["t1"])
                        tt_op("dve", t2[:], pb[1][:, 0:N], sinb, ALU.mult, ["pb1", "sinb0"], ["t2"])
                        tt_op("pool", kTt[:], t1[:], t2[:], ALU.add, ["t1", "t2"], ["kTt"])
                    else:
                        cp_op("act", kTt[:], pb[0][:, 0:N], ["pb0"], ["kTt"])
                    P.dma("sp", sc["kT"][:, t0:t0 + N], kTt[:], reads=["kTt"], writes=[("kTs", nm, st)])
                    projA(8, 0, M=32)
                    cp_op("act", lrT[:], pb[0][0:32, 0:N], ["pb0"], ["lrT"])
                    for d_ in range(2):
                        T2 = bigf[0][:, d_ * N:(d_ + 1) * N]; G = bigf[1][:, d_ * N:(d_ + 1) * N]
                        eM = bigf[2][:, d_ * N:(d_ + 1) * N]; eP = bigf[3][:, d_ * N:(d_ + 1) * N]
                        k32 = bigf[4][:, d_ * N:(d_ + 1) * N]; kd32 = bigf[5][:, d_ * N:(d_ + 1) * N]
                        kT2, kG, keM, keP, kk32, kkd = [(f"bigf{i}", d_) for i in range(6)]
                        zb = 1
                        mm(pb[zb][:, 0:N], wupb[:, d_, :], lrT[:], True, True, ["wupb", "lrT"], [f"pb{zb}"])
                        act(T2, pb[zb][:, 0:N], AF.Exp, [f"pb{zb}", "nbup"], [kT2], bias=nbup[:, d_:d_ + 1], scale=-1.0)
                        act(T2, T2, AF.Ln, [kT2], [kT2], bias=1.0)
                        P.op("dve", lambda e, T2=T2, G=G: e.tensor_tensor_scan(out=G, data0=rst[:], data1=T2, initial=0.0,
                                                                                 op0=ALU.mult, op1=ALU.add),
                             ["rst", kT2], [kG])
                        for j in range(NBK):
                            blk = st * NBK + j
                            act(dec[nm][d_][:, blk:blk + 1], G[:, j * 128 + 127:j * 128 + 128], AF.Exp, [kG], [("dec", nm, d_)], scale=-1.0 / 16)
                        if d_ == 1:
                            for j in range(NBK):
                                sl = slice(j * 128, (j + 1) * 128)
                                stt_op(eM[:, sl], T2[:, sl], G[:, j * 128 + 127:j * 128 + 128], G[:, sl], ALU.add, ALU.subtract,
                                       [kT2, kG], [keM])
                            cp_op("pool", G, eM, [keM], [kG])
                        act(eM, G, AF.Exp, [kG], [keM], scale=-1.0 / 16)
                        act(eP, G, AF.Exp, [kG], [keP], scale=1.0 / 16)
                        projA(6, 0)
                        tt_op("dve", k32, pb[0][:, 0:N], eP, ALU.mult, ["pb0", keP], [kk32])
                        cp_op("pool", kin_t[d_][:], k32, [kk32], [f"kin{d_}"])
                        projA(7, 0)
                        stt_op(qin_t[d_][:], pb[0][:, 0:N], 32.0 ** -0.5, eM, ALU.mult, ALU.mult, ["pb0", keM], [f"qin{d_}"])
                        P.dma("sp", sc["qk"][d_, :, t0:t0 + N], qin_t[d_][:], reads=[f"qin{d_}"], writes=[("qks", nm, st, d_)])
                        P.dma("sp", sc["qk"][2 + d_, :, t0:t0 + N], kin_t[d_][:], reads=[f"kin{d_}"], writes=[("qks", nm, st, 2 + d_)])
                        for j in range(NBK):
                            blk = st * NBK + j
                            sl = slice(j * 128, (j + 1) * 128)
                            ts_op("pool", kd32[:, sl], k32[:, sl], dec[nm][d_][:, blk:blk + 1], None, ALU.mult, None,
                                  [kk32, ("dec", nm, d_)], [kkd])
                            P.op("pe", lambda e, j=j, kd32=kd32, sl=sl: e.transpose(out=pb[4][:, j * 128:(j + 1) * 128], in_=kd32[:, sl], identity=identf[:]),
                                 [kkd, "identf"], ["pb4"])
                            cp_op("act", kdtok[d_][:, j, :], pb[4][:, j * 128:(j + 1) * 128], ["pb4"], [f"kdtok{d_}"])
                    for j in range(NBK):
                        blk = st * NBK + j
                        for k in range(8):
                            mm(pb[5][:, 0:384], hT[:, k, j * 128:(j + 1) * 128], wAB[:, k, 0:384], k == 0, k == 7,
                               ["hT0", ("wAB", 0), ("wAB", 1), ("wAB", 2)], ["pb5"])
                        cp_op("act", vaugt[:, j, 64:192], pb[5][:, 0:128], ["pb5"], ["vaugt"])
                        cp_op("act", vgt[:, j, :], pb[5][:, 128:384], ["pb5"], ["vgt"])
                        for d_ in range(2):
                            mm(pb[6 + d_][:, 0:256], kdtok[d_][:, j, :], vgt[:, j, :], True, True, [f"kdtok{d_}", "vgt"], [f"pb{6 + d_}"])
                            for h in range(4):
                                cp_op(("dve", "act")[d_], stt[nm][d_][32 * h:32 * h + 32, blk, :],
                                      pb[6 + d_][32 * h:32 * h + 32, h * 64:(h + 1) * 64], [f"pb{6 + d_}"], [("stt", nm, d_)])
                    P.dma("sp", sc["vaug"][st * NBK:(st + 1) * NBK].rearrange("b p c -> p b c"), vaugt[:], reads=["vaugt"],
                          writes=[("vaugs", nm, st)])
                    P.dma("sp", sc["vg"][st * NBK:(st + 1) * NBK].rearrange("b p c -> p b c"), vgt[:], reads=["vgt"],
                          writes=[("vgs", nm, st)])
                nblk = T // 128
                for d_ in range(2):
                    order = list(range(nblk)) if d_ == 0 else list(range(nblk - 1, -1, -1))
                    cur = 0
                    if nm == "c":
                        memset("dve", Spp[0][:], 0.0, ["Spp0"])
                    else:
                        cp_op("dve", Spp[0][:], Sfin[d_][:], [f"Sfin{d_}"], ["Spp0"])
                    for blk in order:
                        cp_op("pool", Sin[nm][d_][:, blk, :], Spp[cur][:], [f"Spp{cur}"], [("Sin", nm, d_)])
                        stt_op(Spp[1 - cur][:], Spp[cur][:], dec[nm][d_][:, blk:blk + 1], stt[nm][d_][:, blk, :], ALU.mult, ALU.add,
                               [f"Spp{cur}", ("dec", nm, d_), ("stt", nm, d_)], [f"Spp{1 - cur}"])
                        cur = 1 - cur
                    if nm == "c":
                        cp_op("dve", Sfin[d_][:], Spp[cur][:], [f"Spp{cur}"], [f"Sfin{d_}"])

            def pass_b(nm):
                T, xsrc, xdst = seq_info(nm, l)
                sidx = 0 if nm == "l" else 1
                sc = scr[nm]
                rope = (nm == "l")
                nblk = T // 128
                last = (l == L - 1)
                def b_loads(st, bs):
                    t0 = st * N
                    P.dma("sp", hTs2[bs][:], sc["hT"][:, :, t0:t0 + N].rearrange("k p t -> p k t"), reads=[("hTs", nm, st)], writes=[f"hT{bs}"])
                    if rope:
                        P.dma("sp", cs2[bs][:, 0, :], cos_d[:, t0:t0 + N], writes=[f"cosb{bs}"])
                        P.dma("sp", cs2[bs][:, 1, :], sin_d[:, t0:t0 + N], writes=[f"sinb{bs}"])
                    rd = [("uTs", nm, st)]
                    rd.append(("uTs", nm, st - 1) if st > 0 else ("uTs", nm, "padl"))
                    rd.append(("uTs", nm, st + 1) if st < T // N - 1 else ("uTs", nm, "padr"))
                    P.dma("sp", uw2[bs][:], sc["uT"][:, :, t0:t0 + N + 32].rearrange("c p t -> p c t"), reads=rd, writes=[f"uw{bs}"])
                    if nm == "l":
                        b_lo = max(0, st * NBK - 1)
                        b_hi = min(nblk, st * NBK + NBK + 1)
                        w0 = st * NBK - 1
                        rdk = [("kTs", nm, s2) for s2 in range(max(0, st - 1), min(T // N, st + 2))]
                        rdv = [("vaugs", nm, s2) for s2 in range(max(0, st - 1), min(T // N, st + 2))]
                        wlo = b_lo - w0
                        whi = b_hi - w0
                        P.dma("sp", kzw2[bs][0][0:64, wlo:whi, :], sc["kT"][0:64, b_lo * 128:b_hi * 128].rearrange("p (b t) -> p b t", t=128),
                              reads=rdk, writes=[f"kzw{bs}_0"])
                        P.dma("sp", kzw2[bs][1][64:128, wlo:whi, :], sc["kT"][64:128, b_lo * 128:b_hi * 128].rearrange("p (b t) -> p b t", t=128),
                              reads=rdk, writes=[f"kzw{bs}_1"])
                        P.dma("sp", vaugw2[bs][:, wlo:whi, :], sc["vaug"][b_lo:b_hi].rearrange("b p c -> p b c"), reads=rdv, writes=[f"vaugw{bs}"])
                    for i4 in range(4):
                        P.dma("sp", qkl2[bs][i4][:], sc["qk"][i4, :, t0:t0 + N], reads=[("qks", nm, st, i4)], writes=[f"qkl{bs}_{i4}"])
                    P.dma("sp", vgl2[bs][:], sc["vg"][st * NBK:(st + 1) * NBK].rearrange("b p c -> p b c"), reads=[("vgs", nm, st)], writes=[f"vgl{bs}"])

                b_loads(0, 0)
                for st in range(T // N):
                    t0 = st * N
                    bs = st % 2
                    if st + 1 < T // N:
                        b_loads(st + 1, 1 - bs)
                    for j in range(NBK):
                        tok = t0 + j * 128
                        P.dma("sp", xts[j][:], xsrc[tok:tok + 128, :], reads=([("xo", nm, tok)] if (l > 0 and 0 in layers) else []), writes=[f"xt{j}"])
                    hT = hTs2[bs]; KhT = f"hT{bs}"
                    cosb = cs2[bs][:, 0, :]; sinb = cs2[bs][:, 1, :]; Kcos = f"cosb{bs}"; Ksin = f"sinb{bs}"
                    uw = uw2[bs]; Kuw = f"uw{bs}"
                    kzw = kzw2[bs]; vaugw = vaugw2[bs]; Kvw = f"vaugw{bs}"
                    qkl = qkl2[bs]; vgl = vgl2[bs]; Kvgl = f"vgl{bs}"

                    def projB(ci, bank):
                        for k in range(8):
                            mm(pb[bank][:, 0:N], wAB[:, k, ci * 128:(ci + 1) * 128], hT[:, k, :], k == 0, k == 7,
                               [("wAB", ci), KhT], [f"pb{bank}"])

                    for c in range(2):
                        projB(c, c)
                        act(sga[:, c, :], pb[c][:, 0:N], AF.Silu, [f"pb{c}"], ["sga"])
                    for g in range(4):
                        projB(10 + g, g % 2)
                        act(sgb[:, g, :], pb[g % 2][:, 0:N], AF.Silu, [f"pb{g % 2}"], ["sgb"])
                    for c in range(2):
                        projB(14 + c, c)
                        act(sgc[:, c, :], pb[c][:, 0:N], AF.Silu, [f"pb{c}"], ["sgc"])
                    for g in range(4):
                        projB(2 + g, 0)
                        if rope:
                            projB(6 + g, 1)
                            tt_op("dve", t1[:], pb[0][:, 0:N], cosb, ALU.mult, ["pb0", Kcos], ["t1"])
                            tt_op("dve", t2[:], pb[1][:, 0:N], sinb, ALU.mult, ["pb1", Ksin], ["t2"])
                            tt_op("pool", qT[:, g, :], t1[:], t2[:], ALU.add, ["t1", "t2"], ["qT"])
                        else:
                            cp_op("act", qT[:, g, :], pb[0][:, 0:N], ["pb0"], ["qT"])
                    for c in range(2):
                        for k in range(31):
                            mm(pb[2][:, 0:N], diag[:, c, k, :], uw[:, c, 1 + k:1 + k + N], k == 0, k == 30, [("diag", c), Kuw], ["pb2"])
                        act(cf[:, c, :], pb[2][:, 0:N], AF.Identity, ["pb2", "cp"], ["cf"], bias=cp[:, c, 0:1])
                        act(sq[:, c, :], pb[2][:, 0:N], AF.Square, ["pb2", "cp"], ["sq"], bias=cp[:, c, 0:1])
                    for c in range(2):
                        mm(pb[3][:, 0:N], ones256[:], cf[:, c, :], c == 0, c == 1, ["ones256", "cf"], ["pb3"])
                    for c in range(2):
                        mm(pb[3][:, N:2 * N], ones256[:], sq[:, c, :], c == 0, c == 1, ["ones256", "sq"], ["pb3"])
                    act(m2[:], pb[3][:, 0:N], AF.Square, ["pb3"], ["m2"])
                    tt_op("dve", crs[:], pb[3][:, N:2 * N], m2[:], ALU.subtract, ["pb3", "m2"], ["crs"])
                    act(crs[:], crs[:], AF.Ln, ["crs"], ["crs"], bias=EPS)
                    act(crs[:], crs[:], AF.Exp, ["crs"], ["crs"], scale=-0.5)
                    for c in range(2):
                        tt_op("dve", tt[c][:], cf[:, c, :], pb[3][:, 0:N], ALU.subtract, ["cf", "pb3"], [f"tt{c}"])
                        tt_op("pool", tt[c][:], tt[c][:], crs[:], ALU.mult, [f"tt{c}", "crs"], [f"tt{c}"])
                        act(tt[c][:], tt[c][:], AF.Silu, [f"tt{c}", "cp"], [f"tt{c}"], bias=cp[:, c, 2:3], scale=cp[:, c, 1:2])
                        tt_op("pool", catT[:, c, :], tt[c][:], sga[:, c, :], ALU.mult, [f"tt{c}", "sga"], ["catT"])
                    w0 = st * NBK - 1
                    ei = 0
                    for qb in range(NBK):
                        n = st * NBK + qb
                        qsl = slice(qb * 128, (qb + 1) * 128)
                        for kv in range(2):
                            keys = [("c", 0, None), ("c", 1, None)]
                            if nm == "l":
                                if n > 0:
                                    keys.append(("w", n - 1 - w0, maskP))
                                keys.append(("w", n - w0, None))
                                if n < nblk - 1:
                                    keys.append(("w", n + 1 - w0, maskN))
                            pvb = 6 + kv
                            for ki, (kind, bi, msk) in enumerate(keys):
                                if kind == "c":
                                    klhs = kzc[kv][:, bi * 128:(bi + 1) * 128]; kkey = f"kzc{kv}"
                                    vl = vaugc[:, bi, kv * 128:(kv + 1) * 128]; vkey = "vaugc"
                                else:
                                    klhs = kzw[kv][:, bi, :]; kkey = f"kzw{bs}_{kv}"
                                    vl = vaugw[:, bi, kv * 128:(kv + 1) * 128]; vkey = Kvw
                                sb_ = 4 + (ei % 2)
                                et = eT[ei % 3]; ek = f"eT{ei % 3}"
                                ei += 1
                                mm(pb[sb_][:, :], klhs, qT[:, :, qsl], True, True, [kkey, "qT"], [f"pb{sb_}"])
                                act(et[:], pb[sb_][:, :], AF.Exp, [f"pb{sb_}"], [ek], scale=0.125)
                                if msk is not None:
                                    mkey = "maskP" if msk is maskP else "maskN"
                                    tt_op("pool", et[:].rearrange("p (g q) -> p g q", g=4), et[:].rearrange("p (g q) -> p g q", g=4),
                                          bc_mid(msk[:], 4), ALU.mult, [ek, mkey], [ek])
                                mm(pb[pvb][:, :], vl, et[:], ki == 0, ki == len(keys) - 1, [vkey, ek], [f"pb{pvb}"])
                            dlo = kv * 64
                            olo = (1 - kv) * 64
                            ds = bigf[0][dlo:dlo + 64, :].rearrange("p (g q) -> p g q", g=4)
                            rr = bigf[1][dlo:dlo + 64, :].rearrange("p (g q) -> p g q", g=4)
                            tt_op("dve", ds, pb[pvb][dlo:dlo + 64, :].rearrange("p (g q) -> p g q", g=4),
                                  bc_last(esink[dlo:dlo + 64, kv * 4:kv * 4 + 4].rearrange("p (g o) -> p g o", o=1), 128), ALU.add,
                                  [f"pb{pvb}", "esink"], [("bigf0", kv)])
                            act(ds, ds, AF.Ln, [("bigf0", kv)], [("bigf0", kv)])
                            act(rr, ds, AF.Exp, [("bigf0", kv)], [("bigf1", kv)], scale=-1.0)
                            tt_op("pool", rr, rr, sgb[dlo:dlo + 64, :, qsl], ALU.mult, [("bigf1", kv), "sgb"], [("bigf1", kv)])
                            tt_op("dve", catT[dlo:dlo + 64, 2:6, qsl], pb[pvb][olo:olo + 64, :].rearrange("p (g q) -> p g q", g=4), rr,
                                  ALU.mult, [f"pb{pvb}", ("bigf1", kv)], ["catT"])
                    for j in range(NBK):
                        blk = st * NBK + j
                        sl = slice(j * 128, (j + 1) * 128)
                        for d_ in range(2):
                            for h in range(4):
                                cp_op("pool", bd[d_][32 * h:32 * h + 32, h, :], qkl[d_][32 * h:32 * h + 32, sl], [f"qkl{bs}_{d_}"], [f"bd{d_}"])
                                cp_op("pool", sbd[d_][32 * h:32 * h + 32, h * 64:(h + 1) * 64], Sin[nm][d_][32 * h:32 * h + 32, blk, :],
                                      [("Sin", nm, d_)], [f"sbd{d_}"])
                            mm(pb[d_][:, :], qkl[2 + d_][:, sl], bd[d_][:].rearrange("p h t -> p (h t)"), True, True,
                               [f"qkl{bs}_{2 + d_}", f"bd{d_}"], [f"pb{d_}"])
                            msk = maskN if d_ == 0 else maskP
                            tt_op("dve", attm[d_][:], pb[d_][:, :].rearrange("p (h t) -> p h t", h=4), bc_mid(msk[:], 4), ALU.mult,
                                  [f"pb{d_}", "maskN", "maskP"], [f"attm{d_}"])
                        for h in range(4):
                            osl = pb[2][0:64, h * 128:(h + 1) * 128]
                            mm(osl, vgl[:, j, h * 64:(h + 1) * 64], attm[0][:, h, :], True, False, [Kvgl, "attm0"], ["pb2"])
                            mm(osl, vgl[:, j, h * 64:(h + 1) * 64], attm[1][:, h, :], False, False, [Kvgl, "attm1"], ["pb2"])
                            mm(osl, sbd[0][:, h * 64:(h + 1) * 64], qkl[0][:, sl], False, False, ["sbd0", f"qkl{bs}_0"], ["pb2"])
                            mm(osl, sbd[1][:, h * 64:(h + 1) * 64], qkl[1][:, sl], False, True, ["sbd1", f"qkl{bs}_1"], ["pb2"])
                        osq = bigf[2]
                        act(osq[0:64, :], pb[2][0:64, :], AF.Square, ["pb2"], [("bigf2", 0)])
                        mm(pb[3][:, :], ones64[0:64, :], osq[0:64, :], True, True, ["ones64", ("bigf2", 0)], ["pb3"])
                        rsg = bigf[3]
                        act(rsg[:], pb[3][:, :], AF.Ln, ["pb3"], [("bigf3", 0)], bias=EPS)
                        act(rsg[:], rsg[:], AF.Exp, [("bigf3", 0)], [("bigf3", 0)], scale=-0.5)
                        for h in range(4):
                            e_ = h % 2
                            jc = h // 2
                            tt_op("pool", rgc[e_ * 64:e_ * 64 + 64, :], rsg[e_ * 64:e_ * 64 + 64, h * 128:(h + 1) * 128],
                                  sgc[e_ * 64:e_ * 64 + 64, jc, sl], ALU.mult, [("bigf3", 0), "sgc"], [("rgc", e_)])
                            tt_op("dve", catT[e_ * 64:e_ * 64 + 64, 6 + jc, sl], pb[2][0:64, h * 128:(h + 1) * 128],
                                  rgc[e_ * 64:e_ * 64 + 64, :], ALU.mult, ["pb2", ("rgc", e_)], ["catT"])
                    if debug and l == 0 and nm == "l":
                        P.dma("sp", dbg["cat"][:, :, t0:t0 + N].rearrange("k p t -> p k t"), catT[:], reads=["catT"])
                    for j in range(NBK):
                        tok = t0 + j * 128
                        xt = xts[j]; Kxt = f"xt{j}"
                        for hf in range(2):
                            ob = 4 + hf
                            for kc in range(8):
                                mm(pb[ob][:, :], catT[:, kc, j * 128:(j + 1) * 128], wout[:, kc, hf * 512:(hf + 1) * 512], kc == 0, kc == 7,
                                   ["catT", ("wout", kc)], [f"pb{ob}"])
                            yt = bigf[4 + hf]
                            tt_op("dve", yt[:], pb[ob][:, :], gate_bc[sidx][:, hf * 512:(hf + 1) * 512], ALU.mult,
                                  [f"pb{ob}", f"gate_bc{sidx}"], [(f"bigf{4 + hf}", 0), (f"bigf{4 + hf}", 1)])
                            tt_op("pool", x2[:, hf * 512:(hf + 1) * 512], yt[:], xt[:, hf * 512:(hf + 1) * 512], ALU.add,
                                  [(f"bigf{4 + hf}", 0), (f"bigf{4 + hf}", 1), Kxt], ["x2"])
                        if last and nm == "l":
                            act(xt[:], x2[:], AF.Square, ["x2"], [Kxt])
                            P.op("dve", lambda e, xt=xt: e.tensor_reduce(out=smallf[:, 0:1], in_=xt[:], axis=mybir.AxisListType.X, op=ALU.add), [Kxt], ["ss"])
                            ts_op("dve", smallf[:, 1:2], smallf[:, 0:1], 1.0 / D, EPS, ALU.mult, ALU.add, ["ss"], ["ss1"])
                            act(smallf[:, 2:3], smallf[:, 1:2], AF.Sqrt, ["ss1"], ["ss2"])
                            P.op("dve", lambda e: e.reciprocal(out=smallf[:, 3:4], in_=smallf[:, 2:3]), ["ss2"], ["rstd"])
                            stt_op(xt[:], x2[:], smallf[:, 3:4], gate_bc[1][:], ALU.mult, ALU.mult, ["x2", "rstd", "gate_bc1"], [Kxt])
                            P.dma("sp", xdst[tok:tok + 128, :], xt[:], reads=[Kxt], writes=[("xo", nm, tok)])
                        else:
                            P.dma("sp", xdst[tok:tok + 128, :], x2[:], reads=["x2"], writes=[("xo", nm, tok)])
                            if debug and l == 0:
                                P.dma("sp", dbg["x1" if nm == "l" else "xc1"][tok:tok + 128, :], x2[:], reads=["x2"])

            if stg_lim < 2:
                break
            pass_a("c")
            if stg_lim < 3:
                break
            pass_a("l")
            if stg_lim < 4:
                break
            P.dma("sp", kzc[0][0:64, :], scr["c"]["kT"][0:64, :], reads=[("kTs", "c", 0)], writes=["kzc0"])
            P.dma("sp", kzc[1][64:128, :], scr["c"]["kT"][64:128, :], reads=[("kTs", "c", 0)], writes=["kzc1"])
            P.dma("sp", vaugc[:], scr["c"]["vaug"].rearrange("b p c -> p b c"), reads=[("vaugs", "c", 0)], writes=["vaugc"])
            for ci, pieces in enumerate(Bchunks):
                load_chunk_cols(l, pieces, wAB[:, :, ci * 128:(ci + 1) * 128], ("wAB", ci), 128)
            if upd:
                pass_b("c")
            if stg_lim < 5:
                break
            pass_b("l")
            if stg_lim < 6:
                break

        if RESCHEDULE:
            P.reschedule()
        P.analyze()
        sems_eng = {}
        for e in P.ENGS:
            for ep in range(P.n_epochs[e]):
                sems_eng[(e, ep)] = es.enter_context(nc.semaphore(f"s_{e}_{ep}"))
        sems_dma = [es.enter_context(nc.semaphore(f"d_{k}")) for k in range(N_DMA_SEMS)]
        block = es.enter_context(nc.Block())
        P.emit(block, sems_eng, sems_dma)
    return nc


def _consts():
    ident = np.eye(128, dtype=np.float32)
    j = np.arange(128)[:, None]
    i = np.arange(128)[None, :]
    maskP = (j >= i).astype(np.float32)
    maskN = (j <= i).astype(np.float32)
    rst = np.ones((128, N), np.float32)
    rst[:, 0::128] = 0.0
    t = np.arange(S)
    r = (t // 64).astype(np.float32)
    col = (t % 64).astype(np.float32)
    nf = 16
    inv = (10000.0 ** (-np.arange(nf, dtype=np.float32) / nf)).astype(np.float32)
    ang = np.concatenate([r[:, None] * inv, col[:, None] * inv], axis=-1).astype(np.float32)
    ang = np.concatenate([ang, ang], axis=-1)
    cos = np.cos(ang).astype(np.float32).T
    sin = np.sin(ang).astype(np.float32).T
    sgn = np.where(np.arange(64) < 32, -1.0, 1.0).astype(np.float32)[:, None]
    sinS = sin * sgn
    cosT = np.ascontiguousarray(np.concatenate([cos, cos], axis=0))
    sinT = np.ascontiguousarray(np.concatenate([sinS, sinS], axis=0))
    return dict(ident=ident, maskP=maskP, maskN=maskN, rst=rst, cosT=cosT, sinT=sinT)


_NC_CACHE = {}


def _prep_inputs(x, c, ctx, c_ctx, w_mod, b_mod, w_in, conv_w, conv_b, conv_ln_w, conv_ln_b,
                 attn_sink, gla_w_up, gla_b_up, gla_norm_w, w_out, final_norm_w):
    f = lambda a: np.ascontiguousarray(np.asarray(a, dtype=np.float32))
    cons = _consts()
    bmT = f(np.asarray(b_mod).reshape(L, 24, 128).transpose(0, 2, 1))
    cw = f(np.asarray(conv_w).reshape(L, 31, 2, 128).transpose(0, 3, 2, 1))
    cpar = f(np.stack([np.asarray(conv_b), np.asarray(conv_ln_w), np.asarray(conv_ln_b)], axis=-1)
             .reshape(L, 2, 128, 3).transpose(0, 2, 1, 3))
    wup = np.zeros((L, 32, 2, 128), np.float32)
    wup[:, 0:16, 0, :] = np.asarray(gla_w_up)[:, 0]
    wup[:, 16:32, 1, :] = np.asarray(gla_w_up)[:, 1]
    bup = f(np.asarray(gla_b_up).transpose(0, 2, 1))
    gnw = f(np.asarray(gla_norm_w).reshape(L, 2, 128).transpose(0, 2, 1))
    shared = dict(w_mod=f(w_mod), bmT=bmT, w_in=f(w_in), w_out=f(w_out), cw=cw, cp=cpar, sink=f(attn_sink),
                  wup=wup, bup=bup, gnw=gnw, fnw=f(final_norm_w), **cons)
    in_maps = []
    cctx = np.asarray(c_ctx, np.float32).reshape(8, 128).T
    for b in range(8):
        cc = np.stack([np.asarray(c[b], np.float32).reshape(8, 128).T, cctx], axis=-1)
        m = dict(shared)
        m["x"] = f(x[b])
        m["ctx"] = f(ctx[b])
        m["cc"] = f(cc)
        in_maps.append(m)
    return in_maps


def kernel(**inputs):
    in_maps = _prep_inputs(**inputs)
    if "nc" not in _NC_CACHE:
        _NC_CACHE["nc"] = build(False)
    nc = _NC_CACHE["nc"]
    outs = []
    G = CORES_PER_LAUNCH
    for g0 in range(0, 8, G):
        res = run_bass_kernel_spmd(nc, in_maps[g0:g0 + G], core_ids=list(range(G)))
        outs.extend(np.asarray(r["out"], dtype=np.float32) for r in res.results)
    return np.stack(outs, axis=0)
```

```python
import numpy as np
from contextlib import ExitStack
import concourse.bass as bass
import concourse.mybir as mybir
from concourse.bass_utils import run_bass_kernel_spmd

F32 = mybir.dt.float32
BF16 = mybir.dt.bfloat16
ALU = mybir.AluOpType
AF = mybir.ActivationFunctionType

N_DMA_SEMS = 2
EPOCH = 30000

D = 1024
S = 4096
CT = 256
N = 256
NBK = N // 128
L = 2
NIN = 2848
EPS = 1e-6
CORES_PER_LAUNCH = 8
RESCHEDULE = True
SCHED_PRIO = True
SCHED_SLACK = 0.3


class Prog:
    ENGS = ("pe", "act", "dve", "pool", "sp")

    def __init__(self, nc):
        self.nc = nc
        self.ops = []
        self.ndma = 0

    def op(self, eng, fn, reads=(), writes=(), cost=0.3):
        self.ops.append(dict(eng=eng, fn=fn, reads=tuple(reads), writes=tuple(writes), dma=False, cost=cost))

    def dma(self, eng, out, in_, reads=(), writes=()):
        self.ops.append(dict(eng=eng, fn=lambda e: e.dma_start(out=out, in_=in_),
                             reads=tuple(reads), writes=tuple(writes), dma=True, didx=self.ndma,
                             cost=2.5 + _fsz(out) * 128 * 2.0 / 150e3))
        self.ndma += 1

    def reschedule(self):
        import heapq
        ops = self.ops
        n = len(ops)
        for o in ops:
            ex = tuple(r for r in o["reads"] if isinstance(r, str) and r.startswith("pb") and r not in o["writes"])
            if ex:
                o["writes"] = o["writes"] + ex
        last_w = {}
        readers = {}
        deps = [None] * n
        succ = [[] for _ in range(n)]
        for i, o in enumerate(ops):
            d = set()
            for r in o["reads"]:
                if r in last_w:
                    d.add(last_w[r])
            for w in o["writes"]:
                if w in last_w:
                    d.add(last_w[w])
                for rd in readers.get(w, ()):
                    d.add(rd)
            d.discard(i)
            deps[i] = d
            for j in d:
                succ[j].append(i)
            for r in o["reads"]:
                readers.setdefault(r, []).append(i)
            for w in o["writes"]:
                last_w[w] = i
                readers[w] = []
        bl = [0.0] * n
        for i in range(n - 1, -1, -1):
            m = 0.0
            for k in succ[i]:
                if bl[k] > m:
                    m = bl[k]
            bl[i] = m + ops[i]["cost"] + 0.05
        nleft = [len(d) for d in deps]
        ready_t = [0.0] * n
        fin = [0.0] * n
        start = [0.0] * n
        pend = {e: [] for e in self.ENGS}
        avail = {e: [] for e in self.ENGS}
        for i in range(n):
            if nleft[i] == 0:
                heapq.heappush(pend[ops[i]["eng"]], (0.0, i))
        free = {e: 0.0 for e in self.ENGS}
        dma_fin = []
        done = 0
        while done < n:
            best = None
            for e in self.ENGS:
                while pend[e] and pend[e][0][0] <= free[e] + SCHED_SLACK:
                    rt, i = heapq.heappop(pend[e])
                    heapq.heappush(avail[e], (-bl[i] if SCHED_PRIO else rt, i, rt))
                if avail[e]:
                    _, i, rt = avail[e][0]
                    st = max(rt, free[e])
                elif pend[e]:
                    rt, i = pend[e][0]
                    st = max(rt, free[e])
                else:
                    continue
                if ops[i]["dma"] and len(dma_fin) >= N_DMA_SEMS:
                    st = max(st, dma_fin[-N_DMA_SEMS])
                if best is None or (st, i) < best[:2]:
                    best = (st, i, e)
            st, i, e = best
            if avail[e] and avail[e][0][1] == i:
                heapq.heappop(avail[e])
            else:
                heapq.heappop(pend[e])
            o = ops[i]
            start[i] = st
            if o["dma"]:
                free[e] = st + 0.06
                fin[i] = st + o["cost"]
                dma_fin.append(fin[i])
                dma_fin.sort()
            else:
                free[e] = st + o["cost"]
                fin[i] = st + o["cost"] + 0.05
            done += 1
            for k in succ[i]:
                nleft[k] -= 1
                ready_t[k] = max(ready_t[k], fin[i])
                if nleft[k] == 0:
                    heapq.heappush(pend[ops[k]["eng"]], (ready_t[k], k))
        order = sorted(range(n), key=lambda i: (start[i], i))
        pos = {i: p for p, i in enumerate(order)}
        for i in range(n):
            for j in deps[i]:
                assert pos[j] < pos[i]
        self.ops = [ops[i] for i in order]
        k = 0
        for o in self.ops:
            if o["dma"]:
                o["didx"] = k
                k += 1
        self.est_us = max(fin) if n else 0.0

    def analyze(self):
        last_w = {}
        readers = {}
        for o in self.ops:
            ex = tuple(r for r in o["reads"] if isinstance(r, str) and r.startswith("pb") and r not in o["writes"])
            if ex:
                o["writes"] = o["writes"] + ex
        for i, o in enumerate(self.ops):
            deps = set()
            for r in o["reads"]:
                if r in last_w:
                    deps.add(last_w[r])
            for w in o["writes"]:
                if w in last_w:
                    deps.add(last_w[w])
                for rd in readers.get(w, ()):
                    deps.add(rd)
            deps.discard(i)
            fd = set()
            for d in deps:
                p = self.ops[d]
                if (not p["dma"]) and (not o["dma"]) and p["eng"] == o["eng"]:
                    if o["eng"] == "pe":
                        continue
                fd.add(d)
            o["deps"] = fd
            for r in o["reads"]:
                readers.setdefault(r, []).append(i)
            for w in o["writes"]:
                last_w[w] = i
                readers[w] = []
        for o in self.ops:
            o["need_inc"] = False
        for o in self.ops:
            for d in o["deps"]:
                self.ops[d]["need_inc"] = True
        cnt = {e: 0 for e in self.ENGS}
        for o in self.ops:
            if o["dma"]:
                k = o["didx"]
                o["sem"] = ("dma", k % N_DMA_SEMS, 0)
                o["val"] = 16 * (k // N_DMA_SEMS + 1)
            elif o["need_inc"]:
                cnt[o["eng"]] += 1
                ep = (cnt[o["eng"]] - 1) // EPOCH
                o["sem"] = ("eng", o["eng"], ep)
                o["val"] = cnt[o["eng"]] - ep * EPOCH
        self.n_epochs = {e: (cnt[e] + EPOCH - 1) // EPOCH for e in self.ENGS}

    def emit(self, block, sems_eng, sems_dma):
        dma_ops = [o for o in self.ops if o["dma"]]
        by_eng = {e: [] for e in self.ENGS}
        for i, o in enumerate(self.ops):
            by_eng[o["eng"]].append(i)

        def semh(s):
            return sems_eng[(s[1], s[2])] if s[0] == "eng" else sems_dma[s[1]]

        def run(e, handle):
            waited = {}
            for i in by_eng[e]:
                o = self.ops[i]
                need = {}
                for d in o["deps"]:
                    p = self.ops[d]
                    s = p["sem"]
                    need[s] = max(need.get(s, 0), p["val"])
                if o["dma"]:
                    k = o["didx"]
                    if k >= N_DMA_SEMS:
                        s = ("dma", k % N_DMA_SEMS, 0)
                        need[s] = max(need.get(s, 0), 16 * (k // N_DMA_SEMS))
                for s, v in need.items():
                    if waited.get(s, 0) >= v:
                        continue
                    handle.wait_ge(semh(s), v)
                    waited[s] = v
                ins = o["fn"](handle)
                if o["dma"]:
                    ins.then_inc(semh(o["sem"]), 16)
                elif o["need_inc"]:
                    ins.then_inc(semh(o["sem"]), 1)
            if e == "sp":
                final = {}
                for o in dma_ops:
                    final[o["sem"]] = max(final.get(o["sem"], 0), o["val"])
                for s, v in final.items():
                    handle.wait_ge(semh(s), v)

        @block.tensor
        def _(h):
            run("pe", h)

        @block.scalar
        def _(h):
            run("act", h)

        @block.vector
        def _(h):
            run("dve", h)

        @block.gpsimd
        def _(h):
            run("pool", h)

        @block.sync
        def _(h):
            run("sp", h)


def _chunks():
    A = [[(0, 128)], [(128, 128)], [(256, 128)], [(384, 128)], [(1280, 128)],
         [(1280 + 32, 32), (1280, 32), (1280 + 96, 32), (1280 + 64, 32)],
         [(2176, 128)], [(2048, 128)], [(2560, 32)]]
    At = [(1408, 128), (2304, 256)]
    B = [[(512, 128)], [(640, 128)]]
    for g in range(4):
        B.append([(768 + g * 64, 64), (768 + (4 + g) * 64, 64)])
    for g in range(4):
        a = 768 + g * 64
        b = 768 + (4 + g) * 64
        B.append([(a + 32, 32), (a, 32), (b + 32, 32), (b, 32)])
    for g in range(4):
        B.append([(1536 + g * 64, 64), (1536 + (4 + g) * 64, 64)])
    B.append([(2592, 128)])
    B.append([(2720, 128)])
    return A, At, B


def _fsz(ap):
    n = 1
    for _, c in list(ap.ap)[1:]:
        n *= int(c)
    return n


def bc_mid(ap, n):
    a = [list(x) for x in ap.ap]
    return bass.AP(ap.tensor, ap.offset, [a[0], [0, n]] + a[1:])


def bc_last(ap, n):
    a = [list(x) for x in ap.ap]
    a[-1] = [0, n]
    return bass.AP(ap.tensor, ap.offset, a)


def build(debug=False, stg_lim=99, layers=(0, 1)):
    nc = bass.Bass("TRN2", target_bir_lowering=False)
    es = ExitStack()
    P = Prog(nc)

    def din(name, shape, dt=F32):
        return nc.dram_tensor(name, list(shape), dt, kind="ExternalInput").ap()

    def dscr(name, shape, dt):
        return nc.dram_tensor(name, list(shape), dt, kind="Internal").ap()

    x_d = din("x", [S, D])
    ctx_d = din("ctx", [CT, D])
    cc_d = din("cc", [128, 8, 2])
    wmod_d = din("w_mod", [L, D, 3 * D])
    bmT_d = din("bmT", [L, 128, 24])
    win_d = din("w_in", [L, D, NIN])
    wout_d = din("w_out", [L, D, D])
    cw_d = din("cw", [L, 128, 2, 31])
    cp_d = din("cp", [L, 128, 2, 3])
    sink_d = din("sink", [L, 8])
    wup_d = din("wup", [L, 32, 2, 128])
    bup_d = din("bup", [L, 128, 2])
    gnw_d = din("gnw", [L, 128, 2])
    fnw_d = din("fnw", [D])
    ident_d = din("ident", [128, 128])
    maskP_d = din("maskP", [128, 128])
    maskN_d = din("maskN", [128, 128])
    rst_d = din("rst", [128, N])
    cos_d = din("cosT", [128, S])
    sin_d = din("sinT", [128, S])
    out_d = nc.dram_tensor("out", [S, D], F32, kind="ExternalOutput").ap()

    scr = {}
    for nm, T in (("l", S), ("c", CT)):
        scr[nm] = dict(
            T=T,
            hT=dscr(f"hT_{nm}", [8, 128, T], BF16),
            uT=dscr(f"uT_{nm}", [2, 128, T + 32], BF16),
            kT=dscr(f"kT_{nm}", [128, T], BF16),
            vaug=dscr(f"vaug_{nm}", [T // 128, 128, 256], BF16),
            vg=dscr(f"vg_{nm}", [T // 128, 128, 256], BF16),
            qk=dscr(f"qk_{nm}", [4, 128, T], BF16),
        )
    x1_d = dscr("x1", [S, D], F32)
    xc1_d = dscr("xc1", [CT, D], F32)

    dbg = {}
    if debug:
        dbg["hT"] = nc.dram_tensor("dbg_hT", [8, 128, S], BF16, kind="ExternalOutput").ap()
        dbg["cat"] = nc.dram_tensor("dbg_cat", [8, 128, S], BF16, kind="ExternalOutput").ap()
        dbg["x1"] = nc.dram_tensor("dbg_x1", [S, D], F32, kind="ExternalOutput").ap()
        dbg["xc1"] = nc.dram_tensor("dbg_xc1", [CT, D], F32, kind="ExternalOutput").ap()
        dbg["mod"] = nc.dram_tensor("dbg_mod", [128, 48], F32, kind="ExternalOutput").ap()

    with es:
        def sb(name, shape, dt=F32):
            return es.enter_context(nc.sbuf_tensor("s_" + name, list(shape), dt))

        pb = [es.enter_context(nc.psum_tensor(f"pb{i}", [128, 512], F32)) for i in range(8)]

        identf = sb("identf", [128, 128]); identb = sb("identb", [128, 128], BF16)
        maskP = sb("maskP", [128, 128], BF16); maskN = sb("maskN", [128, 128], BF16)
        rst = sb("rst", [128, N])
        ones256 = sb("ones256", [128, 128], BF16); ones64 = sb("ones64", [128, 128], BF16); onesf = sb("onesf", [128, 128])
        stage = [sb("stage0", [128, 8, 128]), sb("stage1", [128, 8, 128])]
        scc = sb("scc", [128, 8, 2]); bmT = sb("bmT", [128, 24]); modT = sb("modT", [128, 24, 2]); one1p = sb("one1p", [128, 8, 2])
        gate_bc = [sb("gate_bc0", [128, D]), sb("gate_bc1", [128, D])]
        cw = sb("cw", [128, 2, 31]); cp = sb("cp", [128, 2, 3]); esink = sb("esink", [128, 8])
        wupf = sb("wupf", [32, 2, 128]); wupb = sb("wupb", [32, 2, 128], BF16)
        nbup = sb("nbup", [128, 2]); gnw = sb("gnw", [128, 2])
        dg = sb("dg", [128, 128])
        wAB = sb("wAB", [128, 8, 2048], BF16)
        wout = sb("wout", [128, 8, D], BF16)
        diag = sb("diag", [128, 2, 31, 128], BF16)
        Sin = {"l": [sb("SinLf", [128, 32, 64], BF16), sb("SinLb", [128, 32, 64], BF16)],
               "c": [sb("SinCf", [128, 2, 64], BF16), sb("SinCb", [128, 2, 64], BF16)]}
        stt = {"l": [sb("stLf", [128, 32, 64]), sb("stLb", [128, 32, 64])],
               "c": [sb("stCf", [128, 2, 64]), sb("stCb", [128, 2, 64])]}
        dec = {"l": [sb("decLf", [128, 32]), sb("decLb", [128, 32])],
               "c": [sb("decCf", [128, 2]), sb("decCb", [128, 2])]}
        Spp = [sb("Spp0", [128, 64]), sb("Spp1", [128, 64])]
        Sfin = [sb("Sfinf", [128, 64]), sb("Sfinb", [128, 64])]
        kzc = [sb("kzc0", [128, CT], BF16), sb("kzc1", [128, CT], BF16)]
        vaugc = sb("vaugc", [128, 2, 256], BF16)
        hTs2 = [sb("hT0", [128, 8, N], BF16), sb("hT1", [128, 8, N], BF16)]
        xts = [sb("xt0", [128, D]), sb("xt1", [128, D])]; x2 = sb("x2", [128, D])
        smallf = sb("smallf", [128, 8])
        cs2 = [sb("cs0", [128, 2, N]), sb("cs1", [128, 2, N])]; t1 = sb("t1", [128, N]); t2 = sb("t2", [128, N])
        bigf = [sb(f"bigf{i}", [128, 512]) for i in range(6)]
        sig = sb("sig", [128, N]); uTt = sb("uTt", [128, 2, N], BF16); kTt = sb("kTt", [128, N], BF16)
        qin_t = [sb("qin_f", [128, N], BF16), sb("qin_b", [128, N], BF16)]
        kin_t = [sb("kin_f", [128, N], BF16), sb("kin_b", [128, N], BF16)]
        kdtok = [sb("kdtok_f", [128, NBK, 128], BF16), sb("kdtok_b", [128, NBK, 128], BF16)]
        vaugt = sb("vaugt", [128, NBK, 256], BF16); vgt = sb("vgt", [128, NBK, 256], BF16)
        lrT = sb("lrT", [32, N], BF16)
        zpad = sb("zpad", [128, 2, 16], BF16)
        sga = sb("sga", [128, 2, N]); qT = sb("qT", [128, 4, N], BF16); sgb = sb("sgb", [128, 4, N], BF16); sgc = sb("sgc", [128, 2, N])
        uw2 = [sb("uw0", [128, 2, N + 32], BF16), sb("uw1", [128, 2, N + 32], BF16)]
        cf = sb("cf", [128, 2, N]); sq = sb("sq", [128, 2, N], BF16); cfb = sb("cfb", [128, 2, N], BF16); m2 = sb("m2", [128, N]); crs = sb("crs", [128, N])
        osqb = sb("osqb", [128, 512], BF16)
        tt = [sb("tt0", [128, N]), sb("tt1", [128, N])]
        catT = sb("catT", [128, 8, N], BF16)
        kzw2 = [[sb(f"kzw{b_}_{kv_}", [128, 4, 128], BF16) for kv_ in range(2)] for b_ in range(2)]
        vaugw2 = [sb("vaugw0", [128, 4, 256], BF16), sb("vaugw1", [128, 4, 256], BF16)]
        eT = [sb(f"eT{i}", [128, 512], BF16) for i in range(3)]
        qkl2 = [[sb(f"qkl{b_}_{i}", [128, N], BF16) for i in range(4)] for b_ in range(2)]
        vgl2 = [sb("vgl0", [128, NBK, 256], BF16), sb("vgl1", [128, NBK, 256], BF16)]
        bd = [sb("bd_f", [128, 4, 128], BF16), sb("bd_b", [128, 4, 128], BF16)]
        attm = [sb("attm_f", [128, 4, 128], BF16), sb("attm_b", [128, 4, 128], BF16)]
        sbd = [sb("sbd_f", [128, 256], BF16), sb("sbd_b", [128, 256], BF16)]
        rgc = sb("rgc", [128, 128])

        q_rr = {"n": 0}

        def dq():
            q_rr["n"] += 1
            return "sp"

        def act(out, in_, func, reads, writes, bias=None, scale=None, accum=None):
            kw = {}
            if bias is not None:
                kw["bias"] = bias
            if scale is not None:
                kw["scale"] = scale
            if accum is not None:
                kw["accum_out"] = accum
            P.op("act", lambda e: e.activation(out=out, in_=in_, func=func, **kw), reads, writes, cost=0.22 + _fsz(out) / 1400.0)

        def tt_op(eng, out, in0, in1, op, reads, writes):
            P.op(eng, lambda e: e.tensor_tensor(out=out, in0=in0, in1=in1, op=op), reads, writes,
                 cost=(0.07 + _fsz(out) / 960.0) if eng == "dve" else (0.2 + _fsz(out) / 600.0))

        def ts_op(eng, out, in0, s1, s2, op0, op1, reads, writes):
            if op1 is None and eng == "pool" and op0 == ALU.mult:
                P.op(eng, lambda e: e.tensor_scalar(out=out, in0=in0, scalar1=s1, scalar2=1.0, op0=ALU.mult, op1=ALU.mult), reads, writes,
                     cost=0.2 + _fsz(out) / 600.0)
            elif op1 is None:
                P.op(eng, lambda e: e.tensor_scalar(out=out, in0=in0, scalar1=s1, scalar2=None, op0=op0), reads, writes,
                     cost=(0.07 + _fsz(out) / 960.0) if eng == "dve" else (0.2 + _fsz(out) / 600.0))
            else:
                P.op(eng, lambda e: e.tensor_scalar(out=out, in0=in0, scalar1=s1, scalar2=s2, op0=op0, op1=op1), reads, writes,
                     cost=(0.07 + _fsz(out) / 960.0) if eng == "dve" else (0.2 + _fsz(out) / 600.0))

        def stt_op(out, in0, scalar, in1, op0, op1, reads, writes):
            P.op("dve", lambda e: e.scalar_tensor_tensor(out=out, in0=in0, scalar=scalar, in1=in1, op0=op0, op1=op1), reads, writes,
                 cost=0.07 + _fsz(out) / 960.0)

        def cp_op(eng, out, in_, reads, writes):
            if eng == "act":
                act(out, in_, AF.Copy, reads, writes)
            else:
                P.op(eng, lambda e: e.tensor_copy(out=out, in_=in_), reads, writes,
                     cost=(0.07 + _fsz(out) / 960.0) if eng == "dve" else (0.2 + _fsz(out) / 600.0))

        def mm(out, lhsT, rhs, start, stop, reads, writes):
            P.op("pe", lambda e: e.matmul(out, lhsT=lhsT, rhs=rhs, start=start, stop=stop), reads, writes,
                 cost=0.03 + max(64, _fsz(rhs)) * (4 if rhs.dtype == F32 else 1) / 1400.0)

        def memset(eng, ap, v, writes):
            P.op(eng, lambda e: e.memset(ap, v), (), writes, cost=0.2 + _fsz(ap) / 600.0)

        P.dma("sp", identf[:], ident_d, writes=["identf"])
        cp_op("dve", identb[:], identf[:], ["identf"], ["identb"])
        P.dma("sp", stage[0][:, 0, :], maskP_d, writes=["stage0"])
        cp_op("dve", maskP[:], stage[0][:, 0, :], ["stage0"], ["maskP"])
        P.dma("sp", stage[1][:, 0, :], maskN_d, writes=["stage1"])
        cp_op("dve", maskN[:], stage[1][:, 0, :], ["stage1"], ["maskN"])
        P.dma("sp", rst[:], rst_d, writes=["rst"])
        memset("pool", ones256[:], 1.0 / 256.0, ["ones256"])
        memset("pool", ones64[:], 1.0 / 64.0, ["ones64"])
        memset("pool", onesf[:], 1.0, ["onesf"])
        memset("pool", zpad[:], 0.0, ["zpad"])
        for d_ in range(2):
            memset("pool", bd[d_][:], 0.0, [f"bd{d_}"])
            memset("pool", sbd[d_][:], 0.0, [f"sbd{d_}"])
            for b_ in range(2):
                memset("pool", kzw2[b_][d_][:], 0.0, [f"kzw{b_}_{d_}"])
            memset("pool", kzc[d_][:], 0.0, [f"kzc{d_}"])
        memset("pool", vaugt[:, :, 0:64], 1.0, ["vaugt"])
        memset("pool", vaugt[:, :, 192:256], 1.0, ["vaugt"])
        for nm in ("l", "c"):
            T = scr[nm]["T"]
            P.dma("sp", scr[nm]["uT"][:, :, 0:16].rearrange("c p t -> p c t"), zpad[:], reads=["zpad"], writes=[("uTs", nm, "padl")])
            P.dma("sp", scr[nm]["uT"][:, :, 16 + T:32 + T].rearrange("c p t -> p c t"), zpad[:], reads=["zpad"], writes=[("uTs", nm, "padr")])

        Achunks, Atok, Bchunks = _chunks()
        stg_rr = {"n": 0}
        cast_rr = {"n": 0}

        def load_chunk_cols(l, pieces, dst_ap, dst_key, width):
            i = stg_rr["n"] % 2
            stg_rr["n"] += 1
            off = 0
            for (cs, w) in pieces:
                P.dma(dq(), stage[i][:, :, off:off + w], win_d[l, :, cs:cs + w].rearrange("(k p) n -> p k n", p=128),
                      writes=[f"stage{i}"])
                off += w
            eng = ("pool", "dve", "act")[cast_rr["n"] % 3]
            cast_rr["n"] += 1
            cp_op(eng, dst_ap, stage[i][:, :, 0:width], [f"stage{i}"], [dst_key])

        def seq_info(nm, l):
            T = scr[nm]["T"]
            if nm == "l":
                xsrc = x_d if (l == 0 or 0 not in layers) else x1_d
                xdst = x1_d if l == 0 else out_d
            else:
                xsrc = ctx_d if (l == 0 or 0 not in layers) else xc1_d
                xdst = xc1_d
            return T, xsrc, xdst

        for l in layers:
            upd = (l == 0)
            if l == layers[0]:
                P.dma("sp", scc[:], cc_d, writes=["scc"])
                act(scc[:], scc[:], AF.Silu, ["scc"], ["scc"])
            P.dma("sp", bmT[:], bmT_d[l], writes=["bmT"])
            P.dma("sp", cw[:], cw_d[l], writes=["cw"])
            P.dma("sp", cp[:], cp_d[l], writes=["cp"])
            P.dma("sp", esink[:], bass.AP(sink_d.tensor, sink_d[l].offset, [[0, 128], [1, 8]]), writes=["esink"])
            act(esink[:], esink[:], AF.Exp, ["esink"], ["esink"])
            P.dma("sp", wupf[:], wup_d[l], writes=["wupf"])
            cp_op("dve", wupb[:], wupf[:], ["wupf"], ["wupb"])
            P.dma("sp", nbup[:], bup_d[l], writes=["nbup"])
            ts_op("dve", nbup[:], nbup[:], -1.0, None, ALU.mult, None, ["nbup"], ["nbup"])
            P.dma("sp", gnw[:], gnw_d[l], writes=["gnw"])
            modps = pb[6]
            for j in range(24):
                i = stg_rr["n"] % 2
                stg_rr["n"] += 1
                P.dma(dq(), stage[i][:], wmod_d[l, :, j * 128:(j + 1) * 128].rearrange("(k p) n -> p k n", p=128),
                      writes=[f"stage{i}"])
                for k in range(8):
                    mm(modps[:, 2 * j:2 * j + 2], stage[i][:, k, :], scc[:, k, :], k == 0, k == 7,
                       [f"stage{i}", "scc"], ["pb6"])
            tt_op("dve", modT[:], modps[:, 0:48].rearrange("p (j s) -> p j s", s=2), bc_last(bmT[:].rearrange("p (j o) -> p j o", o=1), 2),
                  ALU.add, ["pb6", "bmT"], ["modT"])
            ts_op("dve", one1p[:], modT[:, 8:16, :], 1.0, None, ALU.add, None, ["modT"], ["one1p"])
            if debug and l == 0:
                P.dma("sp", dbg["mod"], modT[:].rearrange("p j s -> p (j s)"), reads=["modT"])
            for s_ in range(2):
                if s_ == 1 and not upd:
                    continue
                for k in range(8):
                    ts_op("dve", dg[:], identf[:], modT[:, 16 + k, s_:s_ + 1], None, ALU.mult, None, ["identf", "modT"], ["dg"])
                    bank = pb[4 + (k // 4)]
                    mm(bank[:, (k % 4) * 128:(k % 4 + 1) * 128], onesf[:], dg[:], True, True, ["onesf", "dg"], [f"pb{4 + k // 4}"])
                cp_op("act", gate_bc[s_][:, 0:512], pb[4][:], ["pb4"], [f"gate_bc{s_}"])
                cp_op("act", gate_bc[s_][:, 512:1024], pb[5][:], ["pb5"], [f"gate_bc{s_}"])
            if l == L - 1:
                P.dma("sp", gate_bc[1][:], bass.AP(fnw_d.tensor, 0, [[0, 128], [1, D]]), writes=["gate_bc1"])
            if stg_lim < 1:
                break
            for ci, pieces in enumerate(Achunks):
                wdt = sum(w for _, w in pieces)
                slot = 3 + ci
                load_chunk_cols(l, pieces, wAB[:, :, slot * 128:slot * 128 + wdt], ("wAB", slot), wdt)
            off = 0
            for (cs, w) in Atok:
                for sub in range(w // 128):
                    slot = (off // 128)
                    load_chunk_cols(l, [(cs + sub * 128, 128)], wAB[:, :, slot * 128:(slot + 1) * 128], ("wAB", slot), 128)
                    off += 128
            for kc in range(8):
                i = stg_rr["n"] % 2
                stg_rr["n"] += 1
                st2 = stage[i][:].rearrange("p k n -> p (k n)")
                if kc < 2:
                    P.dma(dq(), st2, wout_d[l, kc * 128:(kc + 1) * 128, :], writes=[f"stage{i}"])
                elif kc < 6:
                    g = kc - 2
                    P.dma(dq(), st2[0:64, :], wout_d[l, 256 + g * 64:256 + (g + 1) * 64, :], writes=[f"stage{i}"])
                    P.dma(dq(), st2[64:128, :], wout_d[l, 256 + (4 + g) * 64:256 + (5 + g) * 64, :], writes=[f"stage{i}"])
                else:
                    P.dma(dq(), st2, wout_d[l, 768 + (kc - 6) * 128:768 + (kc - 5) * 128, :], writes=[f"stage{i}"])
                if kc < 6:
                    cp_op(("pool", "dve")[kc % 2], wout[:, kc, :], st2, [f"stage{i}"], [("wout", kc)])
                else:
                    ts_op("dve", wout[:, kc, :], st2, gnw[:, kc - 6:kc - 5], None, ALU.mult, None, [f"stage{i}", "gnw"], [("wout", kc)])
            for c in range(2):
                for k in range(31):
                    ts_op(("pool", "dve")[k % 2], diag[:, c, k, :], identb[:], cw[:, c, k:k + 1], None, ALU.mult, None,
                          ["identb", "cw"], [("diag", c)])

            def pass_a(nm):
                T, xsrc, _ = seq_info(nm, l)
                sidx = 0 if nm == "l" else 1
                sc = scr[nm]
                rope = (nm == "l")
                def a_loads(st):
                    for j in range(NBK):
                        tok = st * N + j * 128
                        P.dma("sp", xts[j][:], xsrc[tok:tok + 128, :], reads=([("xo", nm, tok)] if (l > 0 and 0 in layers) else []), writes=[f"xt{j}"])

                a_loads(0)
                hT = hTs2[0]
                cosb = cs2[0][:, 0, :]; sinb = cs2[0][:, 1, :]
                for st in range(T // N):
                    t0 = st * N
                    for j in range(NBK):
                        tok = t0 + j * 128
                        xt = xts[j]; Kxt = f"xt{j}"
                        act(x2[:], xt[:], AF.Square, [Kxt], ["x2"])
                        P.op("dve", lambda e: e.tensor_reduce(out=smallf[:, 0:1], in_=x2[:], axis=mybir.AxisListType.X, op=ALU.add), ["x2"], ["ss"])
                        ts_op("dve", smallf[:, 1:2], smallf[:, 0:1], 1.0 / D, EPS, ALU.mult, ALU.add, ["ss"], ["ss1"])
                        act(smallf[:, 2:3], smallf[:, 1:2], AF.Sqrt, ["ss1"], ["ss2"])
                        P.op("dve", lambda e: e.reciprocal(out=smallf[:, 3:4], in_=smallf[:, 2:3]), ["ss2"], ["rstd"])
                        act(x2[:], xt[:], AF.Copy, [Kxt, "rstd"], ["x2"], scale=smallf[:, 3:4])
                        for k in range(8):
                            bank = 2 + k // 4
                            P.op("pe", lambda e, k=k, bank=bank: e.transpose(out=pb[bank][:, (k % 4) * 128:(k % 4 + 1) * 128],
                                                                              in_=x2[:, k * 128:(k + 1) * 128], identity=identf[:]),
                                 ["x2", "identf"], [f"pb{bank}"])
                        for k in range(8):
                            bank = 2 + k // 4
                            src = pb[bank][:, (k % 4) * 128:(k % 4 + 1) * 128]
                            dst = hT[:, k, j * 128:(j + 1) * 128]
                            if k < 4:
                                ts_op("dve", dst, src, one1p[:, k, sidx:sidx + 1], modT[:, k, sidx:sidx + 1], ALU.mult, ALU.add,
                                      [f"pb{bank}", "one1p", "modT"], ["hT0"])
                            else:
                                act(dst, src, AF.Identity, [f"pb{bank}", "one1p", "modT"], ["hT0"],
                                    bias=modT[:, k, sidx:sidx + 1], scale=one1p[:, k, sidx:sidx + 1])
                    if st + 1 < T // N:
                        a_loads(st + 1)
                    P.dma("sp", sc["hT"][:, :, t0:t0 + N].rearrange("k p t -> p k t"), hT[:], reads=["hT0"], writes=[("hTs", nm, st)])
                    if debug and l == 0 and nm == "l":
                        P.dma("sp", dbg["hT"][:, :, t0:t0 + N].rearrange("k p t -> p k t"), hT[:], reads=["hT0"])
                    if rope:
                        P.dma("sp", cosb, cos_d[:, t0:t0 + N], writes=["cosb0"])
                        P.dma("sp", sinb, sin_d[:, t0:t0 + N], writes=["sinb0"])

                    def projA(ci, bank, M=128):
                        slot = 3 + ci
                        for k in range(8):
                            mm(pb[bank][0:M, 0:N], wAB[:, k, slot * 128:slot * 128 + M], hT[:, k, :], k == 0, k == 7,
                               [("wAB", slot), "hT0"], [f"pb{bank}"])

                    for c in range(2):
                        projA(2 + c, 0)
                        act(sig[:], pb[0][:, 0:N], AF.Sigmoid, ["pb0"], ["sig"])
                        projA(c, 1)
                        tt_op("dve", uTt[:, c, :], pb[1][:, 0:N], sig[:], ALU.mult, ["pb1", "sig"], ["uTt"])
                    P.dma("sp", sc["uT"][:, :, 16 + t0:16 + t0 + N].rearrange("c p t -> p c t"), uTt[:], reads=["uTt"],
                          writes=[("uTs", nm, st)])
                    projA(4, 0)
                    if rope:
                        projA(5, 1)
                        tt_op("dve", t1[:], pb[0][:, 0:N], cosb, ALU.mult, ["pb0", "cosb0"], ["t1"])
                        tt_op("dve", t2[:], pb[1][:, 0:N], sinb, ALU.mult, ["pb1", "sinb0"], ["t2"])
                        tt_op("pool", kTt[:], t1[:], t2[:], ALU.add, ["t1", "t2"], ["kTt"])
                    else:
                        cp_op("act", kTt[:], pb[0][:, 0:N], ["pb0"], ["kTt"])
                    P.dma("sp", sc["kT"][:, t0:t0 + N], kTt[:], reads=["kTt"], writes=[("kTs", nm, st)])
                    projA(8, 0, M=32)
                    cp_op("act", lrT[:], pb[0][0:32, 0:N], ["pb0"], ["lrT"])
                    projA(6, 0)
                    cp_op("act", t1[:], pb[0][:, 0:N], ["pb0"], ["t1"])
                    projA(7, 0)
                    cp_op("act", t2[:], pb[0][:, 0:N], ["pb0"], ["t2"])
                    for d_ in range(2):
                        T2 = bigf[0][:, d_ * N:(d_ + 1) * N]; G = bigf[1][:, d_ * N:(d_ + 1) * N]
                        eM = bigf[2][:, d_ * N:(d_ + 1) * N]; eP = bigf[3][:, d_ * N:(d_ + 1) * N]
                        k32 = bigf[4][:, d_ * N:(d_ + 1) * N]; kd32 = bigf[5][:, d_ * N:(d_ + 1) * N]
                        kT2, kG, keM, keP, kk32, kkd = [(f"bigf{i}", d_) for i in range(6)]
                        zb = 1
                        mm(pb[zb][:, 0:N], wupb[:, d_, :], lrT[:], True, True, ["wupb", "lrT"], [f"pb{zb}"])
                        act(T2, pb[zb][:, 0:N], AF.Exp, [f"pb{zb}", "nbup"], [kT2], bias=nbup[:, d_:d_ + 1], scale=-1.0)
                        act(T2, T2, AF.Ln, [kT2], [kT2], bias=1.0)
                        P.op("dve", lambda e, T2=T2, G=G: e.tensor_tensor_scan(out=G, data0=rst[:], data1=T2, initial=0.0,
                                                                                 op0=ALU.mult, op1=ALU.add),
                             ["rst", kT2], [kG])
                        for j in range(NBK):
                            blk = st * NBK + j
                            act(dec[nm][d_][:, blk:blk + 1], G[:, j * 128 + 127:j * 128 + 128], AF.Exp, [kG], [("dec", nm, d_)], scale=-1.0 / 16)
                        if d_ == 1:
                            for j in range(NBK):
                                sl = slice(j * 128, (j + 1) * 128)
                                stt_op(eM[:, sl], T2[:, sl], G[:, j * 128 + 127:j * 128 + 128], G[:, sl], ALU.add, ALU.subtract,
                                       [kT2, kG], [keM])
                            cp_op("pool", G, eM, [keM], [kG])
                        act(eM, G, AF.Exp, [kG], [keM], scale=-1.0 / 16)
                        act(eP, G, AF.Exp, [kG], [keP], scale=1.0 / 16)
                        tt_op("dve", k32, t1[:], eP, ALU.mult, ["t1", keP], [kk32])
                        cp_op("pool", kin_t[d_][:], k32, [kk32], [f"kin{d_}"])
                        stt_op(qin_t[d_][:], t2[:], 32.0 ** -0.5, eM, ALU.mult, ALU.mult, ["t2", keM], [f"qin{d_}"])
                        P.dma("sp", sc["qk"][d_, :, t0:t0 + N], qin_t[d_][:], reads=[f"qin{d_}"], writes=[("qks", nm, st, d_)])
                        P.dma("sp", sc["qk"][2 + d_, :, t0:t0 + N], kin_t[d_][:], reads=[f"kin{d_}"], writes=[("qks", nm, st, 2 + d_)])
                        for j in range(NBK):
                            blk = st * NBK + j
                            sl = slice(j * 128, (j + 1) * 128)
                            ts_op("pool", kd32[:, sl], k32[:, sl], dec[nm][d_][:, blk:blk + 1], None, ALU.mult, None,
                                  [kk32, ("dec", nm, d_)], [kkd])
                            P.op("pe", lambda e, j=j, kd32=kd32, sl=sl: e.transpose(out=pb[4][:, j * 128:(j + 1) * 128], in_=kd32[:, sl], identity=identf[:]),
                                 [kkd, "identf"], ["pb4"])
                            cp_op("act", kdtok[d_][:, j, :], pb[4][:, j * 128:(j + 1) * 128], ["pb4"], [f"kdtok{d_}"])
                    for j in range(NBK):
                        blk = st * NBK + j
                        for k in range(8):
                            mm(pb[5][:, 0:384], hT[:, k, j * 128:(j + 1) * 128], wAB[:, k, 0:384], k == 0, k == 7,
                               ["hT0", ("wAB", 0), ("wAB", 1), ("wAB", 2)], ["pb5"])
                        cp_op("act", vaugt[:, j, 64:192], pb[5][:, 0:128], ["pb5"], ["vaugt"])
                        cp_op("act", vgt[:, j, :], pb[5][:, 128:384], ["pb5"], ["vgt"])
                        for d_ in range(2):
                            mm(pb[6 + d_][:, 0:256], kdtok[d_][:, j, :], vgt[:, j, :], True, True, [f"kdtok{d_}", "vgt"], [f"pb{6 + d_}"])
                            for h in range(4):
                                cp_op(("dve", "act")[d_], stt[nm][d_][32 * h:32 * h + 32, blk, :],
                                      pb[6 + d_][32 * h:32 * h + 32, h * 64:(h + 1) * 64], [f"pb{6 + d_}"], [("stt", nm, d_)])
                    P.dma("sp", sc["vaug"][st * NBK:(st + 1) * NBK].rearrange("b p c -> p b c"), vaugt[:], reads=["vaugt"],
                          writes=[("vaugs", nm, st)])
                    P.dma("sp", sc["vg"][st * NBK:(st + 1) * NBK].rearrange("b p c -> p b c"), vgt[:], reads=["vgt"],
                          writes=[("vgs", nm, st)])
                nblk = T // 128
                for d_ in range(2):
                    order = list(range(nblk)) if d_ == 0 else list(range(nblk - 1, -1, -1))
                    cur = 0
                    if nm == "c":
                        memset("dve", Spp[0][:], 0.0, ["Spp0"])
                    else:
                        cp_op("dve", Spp[0][:], Sfin[d_][:], [f"Sfin{d_}"], ["Spp0"])
                    for blk in order:
                        cp_op("pool", Sin[nm][d_][:, blk, :], Spp[cur][:], [f"Spp{cur}"], [("Sin", nm, d_)])
                        stt_op(Spp[1 - cur][:], Spp[cur][:], dec[nm][d_][:, blk:blk + 1], stt[nm][d_][:, blk, :], ALU.mult, ALU.add,
                               [f"Spp{cur}", ("dec", nm, d_), ("stt", nm, d_)], [f"Spp{1 - cur}"])
                        cur = 1 - cur
                    if nm == "c":
                        cp_op("dve", Sfin[d_][:], Spp[cur][:], [f"Spp{cur}"], [f"Sfin{d_}"])

            def pass_b(nm):
                T, xsrc, xdst = seq_info(nm, l)
                sidx = 0 if nm == "l" else 1
                sc = scr[nm]
                rope = (nm == "l")
                nblk = T // 128
                last = (l == L - 1)
                def b_loads(st, bs):
                    t0 = st * N
                    P.dma("sp", hTs2[bs][:], sc["hT"][:, :, t0:t0 + N].rearrange("k p t -> p k t"), reads=[("hTs", nm, st)], writes=[f"hT{bs}"])
                    if rope:
                        P.dma("sp", cs2[bs][:, 0, :], cos_d[:, t0:t0 + N], writes=[f"cosb{bs}"])
                        P.dma("sp", cs2[bs][:, 1, :], sin_d[:, t0:t0 + N], writes=[f"sinb{bs}"])
                    rd = [("uTs", nm, st)]
                    rd.append(("uTs", nm, st - 1) if st > 0 else ("uTs", nm, "padl"))
                    rd.append(("uTs", nm, st + 1) if st < T // N - 1 else ("uTs", nm, "padr"))
                    P.dma("sp", uw2[bs][:], sc["uT"][:, :, t0:t0 + N + 32].rearrange("c p t -> p c t"), reads=rd, writes=[f"uw{bs}"])
                    if nm == "l":
                        b_lo = max(0, st * NBK - 1)
                        b_hi = min(nblk, st * NBK + NBK + 1)
                        w0 = st * NBK - 1
                        rdk = [("kTs", nm, s2) for s2 in range(max(0, st - 1), min(T // N, st + 2))]
                        rdv = [("vaugs", nm, s2) for s2 in range(max(0, st - 1), min(T // N, st + 2))]
                        wlo = b_lo - w0
                        whi = b_hi - w0
                        P.dma("sp", kzw2[bs][0][0:64, wlo:whi, :], sc["kT"][0:64, b_lo * 128:b_hi * 128].rearrange("p (b t) -> p b t", t=128),
                              reads=rdk, writes=[f"kzw{bs}_0"])
                        P.dma("sp", kzw2[bs][1][64:128, wlo:whi, :], sc["kT"][64:128, b_lo * 128:b_hi * 128].rearrange("p (b t) -> p b t", t=128),
                              reads=rdk, writes=[f"kzw{bs}_1"])
                        P.dma("sp", vaugw2[bs][:, wlo:whi, :], sc["vaug"][b_lo:b_hi].rearrange("b p c -> p b c"), reads=rdv, writes=[f"vaugw{bs}"])
                    for i4 in range(4):
                        P.dma("sp", qkl2[bs][i4][:], sc["qk"][i4, :, t0:t0 + N], reads=[("qks", nm, st, i4)], writes=[f"qkl{bs}_{i4}"])
                    P.dma("sp", vgl2[bs][:], sc["vg"][st * NBK:(st + 1) * NBK].rearrange("b p c -> p b c"), reads=[("vgs", nm, st)], writes=[f"vgl{bs}"])

                b_loads(0, 0)
                for st in range(T // N):
                    t0 = st * N
                    bs = st % 2
                    if st + 1 < T // N:
                        b_loads(st + 1, 1 - bs)
                    for j in range(NBK):
                        tok = t0 + j * 128
                        P.dma("sp", xts[j][:], xsrc[tok:tok + 128, :], reads=([("xo", nm, tok)] if (l > 0 and 0 in layers) else []), writes=[f"xt{j}"])
                    hT = hTs2[bs]; KhT = f"hT{bs}"
                    cosb = cs2[bs][:, 0, :]; sinb = cs2[bs][:, 1, :]; Kcos = f"cosb{bs}"; Ksin = f"sinb{bs}"
                    uw = uw2[bs]; Kuw = f"uw{bs}"
                    kzw = kzw2[bs]; vaugw = vaugw2[bs]; Kvw = f"vaugw{bs}"
                    qkl = qkl2[bs]; vgl = vgl2[bs]; Kvgl = f"vgl{bs}"

                    def projB(ci, bank):
                        for k in range(8):
                            mm(pb[bank][:, 0:N], wAB[:, k, ci * 128:(ci + 1) * 128], hT[:, k, :], k == 0, k == 7,
                               [("wAB", ci), KhT], [f"pb{bank}"])

                    for c in range(2):
                        projB(c, c)
                        act(sga[:, c, :], pb[c][:, 0:N], AF.Silu, [f"pb{c}"], ["sga"])
                    for g in range(4):
                        projB(10 + g, g % 2)
                        act(sgb[:, g, :], pb[g % 2][:, 0:N], AF.Silu, [f"pb{g % 2}"], ["sgb"])
                    for c in range(2):
                        projB(14 + c, c)
                        act(sgc[:, c, :], pb[c][:, 0:N], AF.Silu, [f"pb{c}"], ["sgc"])
                    for g in range(4):
                        projB(2 + g, 0)
                        if rope:
                            projB(6 + g, 1)
                            tt_op("dve", t1[:], pb[0][:, 0:N], cosb, ALU.mult, ["pb0", Kcos], ["t1"])
                            tt_op("dve", t2[:], pb[1][:, 0:N], sinb, ALU.mult, ["pb1", Ksin], ["t2"])
                            tt_op("pool", qT[:, g, :], t1[:], t2[:], ALU.add, ["t1", "t2"], ["qT"])
                        else:
                            cp_op("act", qT[:, g, :], pb[0][:, 0:N], ["pb0"], ["qT"])
                    for c in range(2):
                        for k in range(31):
                            mm(pb[2][:, 0:N], diag[:, c, k, :], uw[:, c, 1 + k:1 + k + N], k == 0, k == 30, [("diag", c), Kuw], ["pb2"])
                        act(cf[:, c, :], pb[2][:, 0:N], AF.Identity, ["pb2", "cp"], ["cf"], bias=cp[:, c, 0:1])
                        act(sq[:, c, :], pb[2][:, 0:N], AF.Square, ["pb2", "cp"], ["sq"], bias=cp[:, c, 0:1])
                        cp_op("pool", cfb[:, c, :], cf[:, c, :], ["cf"], ["cfb"])
                    for c in range(2):
                        mm(pb[3][:, 0:N], ones256[:], cfb[:, c, :], c == 0, c == 1, ["ones256", "cfb"], ["pb3"])
                    for c in range(2):
                        mm(pb[3][:, N:2 * N], ones256[:], sq[:, c, :], c == 0, c == 1, ["ones256", "sq"], ["pb3"])
                    act(m2[:], pb[3][:, 0:N], AF.Square, ["pb3"], ["m2"])
                    tt_op("dve", crs[:], pb[3][:, N:2 * N], m2[:], ALU.subtract, ["pb3", "m2"], ["crs"])
                    act(crs[:], crs[:], AF.Ln, ["crs"], ["crs"], bias=EPS)
                    act(crs[:], crs[:], AF.Exp, ["crs"], ["crs"], scale=-0.5)
                    for c in range(2):
                        tt_op("dve", tt[c][:], cf[:, c, :], pb[3][:, 0:N], ALU.subtract, ["cf", "pb3"], [f"tt{c}"])
                        tt_op("pool", tt[c][:], tt[c][:], crs[:], ALU.mult, [f"tt{c}", "crs"], [f"tt{c}"])
                        act(tt[c][:], tt[c][:], AF.Silu, [f"tt{c}", "cp"], [f"tt{c}"], bias=cp[:, c, 2:3], scale=cp[:, c, 1:2])
                        tt_op("pool", catT[:, c, :], tt[c][:], sga[:, c, :], ALU.mult, [f"tt{c}", "sga"], ["catT"])
                    w0 = st * NBK - 1
                    ei = 0
                    for qb in range(NBK):
                        n = st * NBK + qb
                        qsl = slice(qb * 128, (qb + 1) * 128)
                        for kv in range(2):
                            keys = [("c", 0, None), ("c", 1, None)]
                            if nm == "l":
                                if n > 0:
                                    keys.append(("w", n - 1 - w0, maskP))
                                keys.append(("w", n - w0, None))
                                if n < nblk - 1:
                                    keys.append(("w", n + 1 - w0, maskN))
                            pvb = 6 + kv
                            for ki, (kind, bi, msk) in enumerate(keys):
                                if kind == "c":
                                    klhs = kzc[kv][:, bi * 128:(bi + 1) * 128]; kkey = f"kzc{kv}"
                                    vl = vaugc[:, bi, kv * 128:(kv + 1) * 128]; vkey = "vaugc"
                                else:
                                    klhs = kzw[kv][:, bi, :]; kkey = f"kzw{bs}_{kv}"
                                    vl = vaugw[:, bi, kv * 128:(kv + 1) * 128]; vkey = Kvw
                                sb_ = 4 + (ei % 2)
                                et = eT[ei % 3]; ek = f"eT{ei % 3}"
                                ei += 1
                                mm(pb[sb_][:, :], klhs, qT[:, :, qsl], True, True, [kkey, "qT"], [f"pb{sb_}"])
                                act(et[:], pb[sb_][:, :], AF.Exp, [f"pb{sb_}"], [ek], scale=0.125)
                                if msk is not None:
                                    mkey = "maskP" if msk is maskP else "maskN"
                                    tt_op("pool", et[:].rearrange("p (g q) -> p g q", g=4), et[:].rearrange("p (g q) -> p g q", g=4),
                                          bc_mid(msk[:], 4), ALU.mult, [ek, mkey], [ek])
                                mm(pb[pvb][:, :], vl, et[:], ki == 0, ki == len(keys) - 1, [vkey, ek], [f"pb{pvb}"])
                            dlo = kv * 64
                            olo = (1 - kv) * 64
                            ds = bigf[0][dlo:dlo + 64, :].rearrange("p (g q) -> p g q", g=4)
                            rr = bigf[1][dlo:dlo + 64, :].rearrange("p (g q) -> p g q", g=4)
                            tt_op("dve", ds, pb[pvb][dlo:dlo + 64, :].rearrange("p (g q) -> p g q", g=4),
                                  bc_last(esink[dlo:dlo + 64, kv * 4:kv * 4 + 4].rearrange("p (g o) -> p g o", o=1), 128), ALU.add,
                                  [f"pb{pvb}", "esink"], [("bigf0", kv)])
                            act(ds, ds, AF.Ln, [("bigf0", kv)], [("bigf0", kv)])
                            act(rr, ds, AF.Exp, [("bigf0", kv)], [("bigf1", kv)], scale=-1.0)
                            tt_op("pool", rr, rr, sgb[dlo:dlo + 64, :, qsl], ALU.mult, [("bigf1", kv), "sgb"], [("bigf1", kv)])
                            tt_op("dve", catT[dlo:dlo + 64, 2:6, qsl], pb[pvb][olo:olo + 64, :].rearrange("p (g q) -> p g q", g=4), rr,
                                  ALU.mult, [f"pb{pvb}", ("bigf1", kv)], ["catT"])
                    for j in range(NBK):
                        blk = st * NBK + j
                        sl = slice(j * 128, (j + 1) * 128)
                        for d_ in range(2):
                            for h in range(4):
                                cp_op("pool", bd[d_][32 * h:32 * h + 32, h, :], qkl[d_][32 * h:32 * h + 32, sl], [f"qkl{bs}_{d_}"], [f"bd{d_}"])
                                cp_op("pool", sbd[d_][32 * h:32 * h + 32, h * 64:(h + 1) * 64], Sin[nm][d_][32 * h:32 * h + 32, blk, :],
                                      [("Sin", nm, d_)], [f"sbd{d_}"])
                            mm(pb[d_][:, :], qkl[2 + d_][:, sl], bd[d_][:].rearrange("p h t -> p (h t)"), True, True,
                               [f"qkl{bs}_{2 + d_}", f"bd{d_}"], [f"pb{d_}"])
                            msk = maskN if d_ == 0 else maskP
                            tt_op("dve", attm[d_][:], pb[d_][:, :].rearrange("p (h t) -> p h t", h=4), bc_mid(msk[:], 4), ALU.mult,
                                  [f"pb{d_}", "maskN", "maskP"], [f"attm{d_}"])
                        for h in range(4):
                            osl = pb[2][0:64, h * 128:(h + 1) * 128]
                            mm(osl, vgl[:, j, h * 64:(h + 1) * 64], attm[0][:, h, :], True, False, [Kvgl, "attm0"], ["pb2"])
                            mm(osl, vgl[:, j, h * 64:(h + 1) * 64], attm[1][:, h, :], False, False, [Kvgl, "attm1"], ["pb2"])
                            mm(osl, sbd[0][:, h * 64:(h + 1) * 64], qkl[0][:, sl], False, False, ["sbd0", f"qkl{bs}_0"], ["pb2"])
                            mm(osl, sbd[1][:, h * 64:(h + 1) * 64], qkl[1][:, sl], False, True, ["sbd1", f"qkl{bs}_1"], ["pb2"])
                        act(osqb[0:64, :], pb[2][0:64, :], AF.Square, ["pb2"], ["osqb"])
                        mm(pb[3][:, :], ones64[0:64, :], osqb[0:64, :], True, True, ["ones64", "osqb"], ["pb3"])
                        rsg = bigf[3]
                        act(rsg[:], pb[3][:, :], AF.Ln, ["pb3"], [("bigf3", 0)], bias=EPS)
                        act(rsg[:], rsg[:], AF.Exp, [("bigf3", 0)], [("bigf3", 0)], scale=-0.5)
                        for h in range(4):
                            e_ = h % 2
                            jc = h // 2
                            tt_op("pool", rgc[e_ * 64:e_ * 64 + 64, :], rsg[e_ * 64:e_ * 64 + 64, h * 128:(h + 1) * 128],
                                  sgc[e_ * 64:e_ * 64 + 64, jc, sl], ALU.mult, [("bigf3", 0), "sgc"], [("rgc", e_)])
                            tt_op("dve", catT[e_ * 64:e_ * 64 + 64, 6 + jc, sl], pb[2][0:64, h * 128:(h + 1) * 128],
                                  rgc[e_ * 64:e_ * 64 + 64, :], ALU.mult, ["pb2", ("rgc", e_)], ["catT"])
                    if debug and l == 0 and nm == "l":
                        P.dma("sp", dbg["cat"][:, :, t0:t0 + N].rearrange("k p t -> p k t"), catT[:], reads=["catT"])
                    for j in range(NBK):
                        tok = t0 + j * 128
                        xt = xts[j]; Kxt = f"xt{j}"
                        for hf in range(2):
                            ob = 4 + hf
                            for kc in range(8):
                                mm(pb[ob][:, :], catT[:, kc, j * 128:(j + 1) * 128], wout[:, kc, hf * 512:(hf + 1) * 512], kc == 0, kc == 7,
                                   ["catT", ("wout", kc)], [f"pb{ob}"])
                            yt = bigf[4 + hf]
                            tt_op("dve", yt[:], pb[ob][:, :], gate_bc[sidx][:, hf * 512:(hf + 1) * 512], ALU.mult,
                                  [f"pb{ob}", f"gate_bc{sidx}"], [(f"bigf{4 + hf}", 0), (f"bigf{4 + hf}", 1)])
                            tt_op("pool", x2[:, hf * 512:(hf + 1) * 512], yt[:], xt[:, hf * 512:(hf + 1) * 512], ALU.add,
                                  [(f"bigf{4 + hf}", 0), (f"bigf{4 + hf}", 1), Kxt], ["x2"])
                        if last and nm == "l":
                            act(xt[:], x2[:], AF.Square, ["x2"], [Kxt])
                            P.op("dve", lambda e, xt=xt: e.tensor_reduce(out=smallf[:, 0:1], in_=xt[:], axis=mybir.AxisListType.X, op=ALU.add), [Kxt], ["ss"])
                            ts_op("dve", smallf[:, 1:2], smallf[:, 0:1], 1.0 / D, EPS, ALU.mult, ALU.add, ["ss"], ["ss1"])
                            act(smallf[:, 2:3], smallf[:, 1:2], AF.Sqrt, ["ss1"], ["ss2"])
                            P.op("dve", lambda e: e.reciprocal(out=smallf[:, 3:4], in_=smallf[:, 2:3]), ["ss2"], ["rstd"])
                            stt_op(xt[:], x2[:], smallf[:, 3:4], gate_bc[1][:], ALU.mult, ALU.mult, ["x2", "rstd", "gate_bc1"], [Kxt])
                            P.dma("sp", xdst[tok:tok + 128, :], xt[:], reads=[Kxt], writes=[("xo", nm, tok)])
                        else:
                            P.dma("sp", xdst[tok:tok + 128, :], x2[:], reads=["x2"], writes=[("xo", nm, tok)])
                            if debug and l == 0:
                                P.dma("sp", dbg["x1" if nm == "l" else "xc1"][tok:tok + 128, :], x2[:], reads=["x2"])

            if stg_lim < 2:
                break
            pass_a("c")
            if stg_lim < 3:
                break
            pass_a("l")
            if stg_lim < 4:
                break
            P.dma("sp", kzc[0][0:64, :], scr["c"]["kT"][0:64, :], reads=[("kTs", "c", 0)], writes=["kzc0"])
            P.dma("sp", kzc[1][64:128, :], scr["c"]["kT"][64:128, :], reads=[("kTs", "c", 0)], writes=["kzc1"])
            P.dma("sp", vaugc[:], scr["c"]["vaug"].rearrange("b p c -> p b c"), reads=[("vaugs", "c", 0)], writes=["vaugc"])
            for ci, pieces in enumerate(Bchunks):
                load_chunk_cols(l, pieces, wAB[:, :, ci * 128:(ci + 1) * 128], ("wAB", ci), 128)
            if upd:
                pass_b("c")
            if stg_lim < 5:
                break
            pass_b("l")
            if stg_lim < 6:
                break

        if RESCHEDULE:
            P.reschedule()
        P.analyze()
        sems_eng = {}
        for e in P.ENGS:
            for ep in range(P.n_epochs[e]):
                sems_eng[(e, ep)] = es.enter_context(nc.semaphore(f"s_{e}_{ep}"))
        sems_dma = [es.enter_context(nc.semaphore(f"d_{k}")) for k in range(N_DMA_SEMS)]
        block = es.enter_context(nc.Block())
        P.emit(block, sems_eng, sems_dma)
    return nc


def _consts():
    ident = np.eye(128, dtype=np.float32)
    j = np.arange(128)[:, None]
    i = np.arange(128)[None, :]
    maskP = (j >= i).astype(np.float32)
    maskN = (j <= i).astype(np.float32)
    rst = np.ones((128, N), np.float32)
    rst[:, 0::128] = 0.0
    t = np.arange(S)
    r = (t // 64).astype(np.float32)
    col = (t % 64).astype(np.float32)
    nf = 16
    inv = (10000.0 ** (-np.arange(nf, dtype=np.float32) / nf)).astype(np.float32)
    ang = np.concatenate([r[:, None] * inv, col[:, None] * inv], axis=-1).astype(np.float32)
    ang = np.concatenate([ang, ang], axis=-1)
    cos = np.cos(ang).astype(np.float32).T
    sin = np.sin(ang).astype(np.float32).T
    sgn = np.where(np.arange(64) < 32, -1.0, 1.0).astype(np.float32)[:, None]
    sinS = sin * sgn
    cosT = np.ascontiguousarray(np.concatenate([cos, cos], axis=0))
    sinT = np.ascontiguousarray(np.concatenate([sinS, sinS], axis=0))
    return dict(ident=ident, maskP=maskP, maskN=maskN, rst=rst, cosT=cosT, sinT=sinT)


_NC_CACHE = {}


def _prep_inputs(x, c, ctx, c_ctx, w_mod, b_mod, w_in, conv_w, conv_b, conv_ln_w, conv_ln_b,
                 attn_sink, gla_w_up, gla_b_up, gla_norm_w, w_out, final_norm_w):
    f = lambda a: np.ascontiguousarray(np.asarray(a, dtype=np.float32))
    cons = _consts()
    bmT = f(np.asarray(b_mod).reshape(L, 24, 128).transpose(0, 2, 1))
    cw = f(np.asarray(conv_w).reshape(L, 31, 2, 128).transpose(0, 3, 2, 1))
    cpar = f(np.stack([np.asarray(conv_b), np.asarray(conv_ln_w), np.asarray(conv_ln_b)], axis=-1)
             .reshape(L, 2, 128, 3).transpose(0, 2, 1, 3))
    wup = np.zeros((L, 32, 2, 128), np.float32)
    wup[:, 0:16, 0, :] = np.asarray(gla_w_up)[:, 0]
    wup[:, 16:32, 1, :] = np.asarray(gla_w_up)[:, 1]
    bup = f(np.asarray(gla_b_up).transpose(0, 2, 1))
    gnw = f(np.asarray(gla_norm_w).reshape(L, 2, 128).transpose(0, 2, 1))
    shared = dict(w_mod=f(w_mod), bmT=bmT, w_in=f(w_in), w_out=f(w_out), cw=cw, cp=cpar, sink=f(attn_sink),
                  wup=wup, bup=bup, gnw=gnw, fnw=f(final_norm_w), **cons)
    in_maps = []
    cctx = np.asarray(c_ctx, np.float32).reshape(8, 128).T
    for b in range(8):
        cc = np.stack([np.asarray(c[b], np.float32).reshape(8, 128).T, cctx], axis=-1)
        m = dict(shared)
        m["x"] = f(x[b])
        m["ctx"] = f(ctx[b])
        m["cc"] = f(cc)
        in_maps.append(m)
    return in_maps


def kernel(**inputs):
    in_maps = _prep_inputs(**inputs)
    if "nc" not in _NC_CACHE:
        _NC_CACHE["nc"] = build(False)
    nc = _NC_CACHE["nc"]
    outs = []
    G = CORES_PER_LAUNCH
    for g0 in range(0, 8, G):
        res = run_bass_kernel_spmd(nc, in_maps[g0:g0 + G], core_ids=list(range(G)))
        outs.extend(np.asarray(r["out"], dtype=np.float32) for r in res.results)
    return np.stack(outs, axis=0)
```
